# Optimizing a Trainium2 kernel written in Bass

```python
import jax, jax.numpy as jnp
from jax import lax
import numpy as np

D_MODEL = 2048
BATCH = 8
SEQ = 2048
DEPTH = 1
DEC_BATCH = 4
DEC_SEQ = 8192
PAST_LEN = 128

SSD_D_INNER = 2 * D_MODEL
SSD_HEADDIM = 64
SSD_HEADS = SSD_D_INNER // SSD_HEADDIM
SSD_STATE = 128
SSD_GROUPS = 8
SSD_CONV = 5
SSD_CHUNK = 128
SSD_BC_WIDTH = SSD_GROUPS * SSD_STATE
SSD_CONV_DIM = SSD_D_INNER + 2 * SSD_BC_WIDTH
RET_HEADS = 8
RET_QK_DIM = D_MODEL // RET_HEADS
RET_V_DIM = 2 * RET_QK_DIM
RET_QK_WIDTH = RET_HEADS * RET_QK_DIM
RET_V_WIDTH = RET_HEADS * RET_V_DIM
RET_CHUNK = 128
ROPE_BASE = 10000.0
D_FF = 5632
FFN_CONV = 3
N_MOD = 6
EPS = 1e-6
IN_SIZES = (SSD_D_INNER, SSD_CONV_DIM, 2 * SSD_HEADS, RET_QK_WIDTH, RET_QK_WIDTH, RET_V_WIDTH, RET_V_WIDTH, 2 * D_MODEL)
IN_WIDTH = SSD_D_INNER + SSD_CONV_DIM + 2 * SSD_HEADS + 2 * RET_QK_WIDTH + 2 * RET_V_WIDTH + 2 * D_MODEL

kernel_name = 'bidir_ssd_retention_convffn_adaln'

F32 = jnp.float32


def rmsnorm(x, w=None):
    xf = x.astype(F32)
    y = xf * lax.rsqrt(jnp.mean(xf * xf, axis=-1, keepdims=True) + EPS)
    if w is not None:
        y = y * w.astype(F32)
    return y.astype(x.dtype)


def dwconv_centred(x, w, b):
    k = w.shape[0]
    y = lax.conv_general_dilated(x.astype(w.dtype), w[:, None, :], window_strides=(1,),
                                 padding=[((k - 1) // 2, (k - 1) // 2)],
                                 dimension_numbers=('NWC', 'WIO', 'NWC'),
                                 feature_group_count=x.shape[-1])
    return y + b


def rope(x, pos):
    half = x.shape[-1] // 2
    inv = ROPE_BASE ** (-jnp.arange(half, dtype=F32) / half)
    ang = pos.astype(F32)[:, None] * inv[None]
    cos = jnp.cos(ang)[None, :, None, :]
    sin = jnp.sin(ang)[None, :, None, :]
    xf = x.astype(F32)
    x1, x2 = xf[..., :half], xf[..., half:]
    return jnp.concatenate([x1 * cos - x2 * sin, x1 * sin + x2 * cos], axis=-1).astype(x.dtype)


def flip(t):
    return jnp.flip(t, axis=1)


def ssd_scan(x, dt, A, B, C):
    b, L, h, p = x.shape
    g, n = B.shape[2], B.shape[3]
    r = h // g
    nc = L // SSD_CHUNK

    def chunks(t):
        return t.astype(F32).reshape((b, nc, SSD_CHUNK) + t.shape[2:]).swapaxes(0, 1)

    xdt = (x.astype(F32) * dt[..., None]).reshape(b, L, g, r, p)
    dA = (dt * A).reshape(b, L, g, r)
    mask = jnp.tril(jnp.ones((SSD_CHUNK, SSD_CHUNK), dtype=bool))[None, :, :, None, None]

    def step(state, inp):
        xc, dAc, Bc, Cc = inp
        cum = jnp.cumsum(dAc, axis=1)
        seg = cum[:, :, None] - cum[:, None, :]
        decay = jnp.exp(jnp.where(mask, seg, -jnp.inf))
        cb = jnp.einsum('bign,bjgn->bijg', Cc, Bc)
        y = jnp.einsum('bijgr,bjgrp->bigrp', decay * cb[..., None], xc)
        y = y + jnp.einsum('bign,bgrpn->bigrp', Cc, state) * jnp.exp(cum)[..., None]
        to_end = jnp.exp(cum[:, -1:] - cum)
        state = state * jnp.exp(cum[:, -1])[..., None, None] + jnp.einsum('bjgn,bjgrp->bgrpn', Bc, xc * to_end[..., None])
        return state, y

    state0 = jnp.zeros((b, g, r, p, n), F32)
    _, ys = lax.scan(step, state0, (chunks(xdt), chunks(dA), chunks(B), chunks(C)))
    return ys.swapaxes(0, 1).reshape(b, L, h, p)


def retention_scan(q, k, v, log_gamma, strict):
    b, L, h, dk = q.shape
    dv = v.shape[-1]
    nc = L // RET_CHUNK

    def chunks(t):
        return t.astype(F32).reshape((b, nc, RET_CHUNK) + t.shape[2:]).swapaxes(0, 1)

    idx = jnp.arange(RET_CHUNK, dtype=F32)
    diff = idx[:, None] - idx[None, :]
    mask = (diff > 0) if strict else (diff >= 0)
    decay = jnp.where(mask[None], jnp.exp(jnp.maximum(diff, 0.0)[None] * log_gamma[:, None, None]), 0.0)
    xi = jnp.exp((idx + 1.0)[:, None] * log_gamma[None])
    zeta = jnp.exp((RET_CHUNK - 1.0 - idx)[:, None] * log_gamma[None])
    g_chunk = jnp.exp(RET_CHUNK * log_gamma)

    def step(state, inp):
        qc, kc, vc = inp
        att = jnp.einsum('bihd,bjhd->bhij', qc, kc) * decay[None]
        y = jnp.einsum('bhij,bjhv->bihv', att, vc)
        y = y + jnp.einsum('bihd,bhdv->bihv', qc, state) * xi[None, :, :, None]
        state = state * g_chunk[None, :, None, None] + jnp.einsum('bjhd,bjhv->bhdv', kc * zeta[None, :, :, None], vc)
        return state, y

    state0 = jnp.zeros((b, h, dk, dv), F32)
    _, ys = lax.scan(step, state0, (chunks(q), chunks(k), chunks(v)))
    return ys.swapaxes(0, 1).reshape(b, L, h, dv)


def trunk(x, c, w_ada, b_ada, norm1, w_in, conv_w, conv_b, dt_bias, a_log, ssd_d, ssd_norm,
          w_pa, w_pb, w_out, norm2, w_up, ffn_conv_w, ffn_conv_b, w_down, norm_f):
    bsz, L, _ = x.shape
    pos = jnp.arange(L)
    split_idx = list(np.cumsum(IN_SIZES)[:-1])
    log_gamma_f = jnp.log1p(-jnp.exp2(-5.0 - jnp.arange(RET_HEADS, dtype=F32)))
    log_gamma_b = log_gamma_f[::-1]
    h = x
    for l in range(DEPTH):
        mod = jnp.einsum('bd,de->be', jax.nn.silu(c), w_ada[l]) + b_ada[l]
        sh1, sc1, g1, sh2, sc2, g2 = [t[:, None, :] for t in jnp.split(mod, N_MOD, axis=-1)]
        n1 = rmsnorm(h, norm1[l]) * (1.0 + sc1) + sh1
        proj = jnp.einsum('bld,de->ble', n1, w_in[l])
        z, xbc, dt_raw, q, k, v, gret, gates = jnp.split(proj, split_idx, axis=-1)
        xbc = jax.nn.silu(dwconv_centred(xbc, conv_w[l], conv_b[l]))
        xs, Bm, Cm = jnp.split(xbc, [SSD_D_INNER, SSD_D_INNER + SSD_BC_WIDTH], axis=-1)
        xs = xs.reshape(bsz, L, SSD_HEADS, SSD_HEADDIM)
        Bm = Bm.reshape(bsz, L, SSD_GROUPS, SSD_STATE)
        Cm = Cm.reshape(bsz, L, SSD_GROUPS, SSD_STATE)
        dt = jax.nn.softplus(dt_raw.astype(F32).reshape(bsz, L, 2, SSD_HEADS) + dt_bias[l].astype(F32))
        A = -jnp.exp(a_log[l].astype(F32))
        y_f = ssd_scan(xs, dt[:, :, 0], A[0], Bm, Cm)
        y_b = flip(ssd_scan(flip(xs), flip(dt[:, :, 1]), A[1], flip(Bm), flip(Cm)))
        ya = (y_f + y_b + ssd_d[l].astype(F32)[:, None] * xs.astype(F32)).astype(h.dtype)
        ya = ya.reshape(bsz, L, SSD_D_INNER) * jax.nn.silu(z)
        ya = rmsnorm(ya.reshape(bsz, L, SSD_GROUPS, SSD_D_INNER // SSD_GROUPS)).reshape(bsz, L, SSD_D_INNER) * ssd_norm[l]
        ya = jnp.einsum('ble,ed->bld', ya, w_pa[l])
        q = rope(q.reshape(bsz, L, RET_HEADS, RET_QK_DIM), pos)
        k = rope(k.reshape(bsz, L, RET_HEADS, RET_QK_DIM), pos) * (RET_QK_DIM ** -0.5)
        v = v.reshape(bsz, L, RET_HEADS, RET_V_DIM)
        o = retention_scan(q, k, v, log_gamma_f, False) + flip(retention_scan(flip(q), flip(k), flip(v), log_gamma_b, True))
        o = rmsnorm(o).astype(h.dtype).reshape(bsz, L, RET_V_WIDTH)
        o = jax.nn.silu(gret) * o
        yb = jnp.einsum('ble,ed->bld', o, w_pb[l])
        ga, gb = jnp.split(gates, 2, axis=-1)
        m = jax.nn.sigmoid(ga) * ya + jax.nn.sigmoid(gb) * yb
        h = h + g1 * jnp.einsum('bld,de->ble', m, w_out[l])
        n2 = rmsnorm(h, norm2[l]) * (1.0 + sc2) + sh2
        a, gt = jnp.split(jnp.einsum('bld,df->blf', n2, w_up[l]), 2, axis=-1)
        a = dwconv_centred(a, ffn_conv_w[l], ffn_conv_b[l])
        h = h + g2 * jnp.einsum('blf,fd->bld', jax.nn.silu(a) * gt, w_down[l])
    return rmsnorm(h, norm_f)


def setup_inputs(seed: int = 0) -> dict:
    key = jax.random.key(seed)
    ks = jax.random.split(key, 32)
    nrm = jax.random.normal
    D = D_MODEL
    dt0 = jnp.exp(jax.random.uniform(ks[9], (DEPTH, 2, SSD_HEADS), minval=float(np.log(1e-3)), maxval=float(np.log(1e-1))))
    return {
        'x_prompt': nrm(ks[0], (BATCH, SEQ, D), F32),
        'x_sample': nrm(ks[1], (DEC_BATCH, DEC_SEQ, D), F32),
        'c_prompt': nrm(ks[2], (BATCH, D), F32),
        'c_sample': nrm(ks[3], (DEC_BATCH, D), F32),
        'w_ada': nrm(ks[4], (DEPTH, D, N_MOD * D), F32) * D ** -0.5,
        'b_ada': nrm(ks[5], (DEPTH, N_MOD * D), F32) * 0.02,
        'norm1': 1.0 + 0.02 * nrm(ks[6], (DEPTH, D), F32),
        'w_in': nrm(ks[7], (DEPTH, D, IN_WIDTH), F32) * D ** -0.5,
        'conv_w': nrm(ks[8], (DEPTH, SSD_CONV, SSD_CONV_DIM), F32) * SSD_CONV ** -0.5,
        'conv_b': nrm(ks[10], (DEPTH, SSD_CONV_DIM), F32) * 0.02,
        'dt_bias': dt0 + jnp.log(-jnp.expm1(-dt0)),
        'a_log': jnp.log(jax.random.uniform(ks[11], (DEPTH, 2, SSD_HEADS), minval=1.0, maxval=16.0)),
        'ssd_d': 1.0 + 0.1 * nrm(ks[12], (DEPTH, SSD_HEADS), F32),
        'ssd_norm': 1.0 + 0.02 * nrm(ks[13], (DEPTH, SSD_D_INNER), F32),
        'w_pa': nrm(ks[14], (DEPTH, SSD_D_INNER, D), F32) * SSD_D_INNER ** -0.5,
        'w_pb': nrm(ks[15], (DEPTH, RET_V_WIDTH, D), F32) * RET_V_WIDTH ** -0.5,
        'w_out': nrm(ks[16], (DEPTH, D, D), F32) * D ** -0.5,
        'norm2': 1.0 + 0.02 * nrm(ks[17], (DEPTH, D), F32),
        'w_up': nrm(ks[18], (DEPTH, D, 2 * D_FF), F32) * D ** -0.5,
        'ffn_conv_w': nrm(ks[19], (DEPTH, FFN_CONV, D_FF), F32) * FFN_CONV ** -0.5,
        'ffn_conv_b': nrm(ks[20], (DEPTH, D_FF), F32) * 0.02,
        'w_down': nrm(ks[21], (DEPTH, D_FF, D), F32) * D_FF ** -0.5,
        'norm_f': 1.0 + 0.02 * nrm(ks[22], (D,), F32),
    }


def reference(x_prompt, x_sample, c_prompt, c_sample, w_ada, b_ada, norm1, w_in, conv_w, conv_b,
              dt_bias, a_log, ssd_d, ssd_norm, w_pa, w_pb, w_out, norm2, w_up, ffn_conv_w,
              ffn_conv_b, w_down, norm_f):
    y_prompt = trunk(x_prompt, c_prompt, w_ada, b_ada, norm1, w_in, conv_w, conv_b, dt_bias, a_log, ssd_d,
                     ssd_norm, w_pa, w_pb, w_out, norm2, w_up, ffn_conv_w, ffn_conv_b, w_down, norm_f)
    y_sample = trunk(x_sample, c_sample, w_ada, b_ada, norm1, w_in, conv_w, conv_b, dt_bias, a_log, ssd_d,
                     ssd_norm, w_pa, w_pb, w_out, norm2, w_up, ffn_conv_w, ffn_conv_b, w_down, norm_f)
    return (y_prompt, y_sample)
```

```python
import numpy as np
import concourse.bass as bass
import concourse.mybir as mybir
from concourse.bass_utils import run_bass_kernel_spmd
from contextlib import ExitStack

F32 = mybir.dt.float32
BF16 = mybir.dt.bfloat16
AF = mybir.ActivationFunctionType
ALU = mybir.AluOpType
AX = mybir.AxisListType

D = 2048
KD = 16
E_IN = 26752
DFF = 5632
NF = 44
EPS = 1e-6
ENGS = ("pe", "act", "dve", "pool", "sp")
DMAQ = {"sp": 12, "pool": 8, "act": 4}
DEBUG_LINES = []
SCAN_STOP = 99
D_STOP = 99


import types as _types


def freeze(fn):
    if fn.__closure__ is None:
        return fn
    cells = []
    for c in fn.__closure__:
        try:
            cells.append(_types.CellType(c.cell_contents))
        except ValueError:
            cells.append(c)
    return _types.FunctionType(fn.__code__, fn.__globals__, fn.__name__, fn.__defaults__, tuple(cells))


class Res:
    __slots__ = ("name", "w", "r", "x")

    def __init__(self, name="", excl=False):
        self.name = name
        self.w = None
        self.r = {}
        self.x = excl


class Prog:
    def __init__(self, nc):
        self.nc = nc
        self.es = ExitStack()
        self.scopes = []
        self.q = {e: [] for e in ENGS}
        self.sems = {}
        self.cnt = {}
        self.known = {e: {} for e in ENGS}
        for e in ENGS:
            self.sems[e] = self.es.enter_context(nc.semaphore("s_" + e))
            self.cnt[e] = 0
        self.dq = {}
        for qn, n in DMAQ.items():
            ks = []
            for i in range(n):
                k = "d_%s%d" % (qn, i)
                self.sems[k] = self.es.enter_context(nc.semaphore("s_" + k))
                self.cnt[k] = 0
                ks.append(k)
            self.dq[qn] = [ks, 0]
        self.ninst = 0
        self.uid = 0

    def push(self):
        self.scopes.append(ExitStack())

    def pop(self):
        self.scopes.pop().close()

    def _stk(self):
        return self.scopes[-1] if self.scopes else self.es

    def sbuf(self, name, shape, dt):
        self.uid += 1
        return self._stk().enter_context(self.nc.sbuf_tensor("%s_%d" % (name, self.uid), list(shape), dt))

    def psum(self, name, shape, dt):
        self.uid += 1
        return self._stk().enter_context(self.nc.psum_tensor("%s_%d" % (name, self.uid), list(shape), dt))

    def _deps(self, reads, writes, eng=None):
        deps = []
        for r in reads:
            if r.w is not None:
                deps.append(r.w)
            if r.x:
                for k, v in r.r.items():
                    if k != eng:
                        deps.append((k, v))
        for w in writes:
            if w.w is not None:
                deps.append(w.w)
            deps.extend(w.r.items())
        return deps

    def _emit_waits(self, eng, deps, skip_self_pe=False):
        need = {}
        kn = self.known[eng]
        for (k, v) in deps:
            if skip_self_pe and k == "pe" and eng == "pe":
                continue
            if kn.get(k, 0) >= v:
                continue
            if need.get(k, 0) < v:
                need[k] = v
        for k, v in need.items():
            kn[k] = v
            sem = self.sems[k]
            self.q[eng].append(lambda e, sem=sem, v=v: e.wait_ge(sem, v))

    def _mark(self, tok, reads, writes):
        k, v = tok
        for r in reads:
            if r.r.get(k, 0) < v:
                r.r[k] = v
        for w in writes:
            w.w = tok
            w.r = {}

    def op(self, eng, fn, reads=(), writes=(), inc=True):
        fn = freeze(fn)
        if DEBUG_LINES:
            import sys as _s
            fr = _s._getframe(1)
            if fr.f_code.co_name in ("mm", "tr", "evac"):
                fr = fr.f_back
            self.q[eng].append(("L", fr.f_lineno))
        deps = self._deps(reads, writes, eng)
        self._emit_waits(eng, deps, skip_self_pe=True)
        self.ninst += 1
        if inc:
            self.cnt[eng] += 1
            v = self.cnt[eng]
            sem = self.sems[eng]
            self.q[eng].append(lambda e, fn=fn, sem=sem: fn(e).then_inc(sem, 1))
            tok = (eng, v)
        else:
            self.q[eng].append(lambda e, fn=fn: fn(e))
            tok = (eng, self.cnt[eng] + 1)
        self._mark(tok, reads, writes)
        return tok

    def dma(self, qn, fn, reads=(), writes=()):
        fn = freeze(fn)
        ks, rr = self.dq[qn]
        k = ks[rr]
        self.dq[qn][1] = (rr + 1) % len(ks)
        deps = self._deps(reads, writes)
        if self.cnt[k] > 0:
            deps.append((k, self.cnt[k]))
        self._emit_waits(qn, deps)
        self.cnt[k] += 16
        v = self.cnt[k]
        sem = self.sems[k]
        self.ninst += 1
        self.q[qn].append(lambda e, fn=fn, sem=sem: fn(e).then_inc(sem, 16))
        tok = (k, v)
        self._mark(tok, reads, writes)
        return tok

    def barrier(self):
        for eng in ENGS:
            deps = [(k, v) for k, v in self.cnt.items() if v > 0 and k != eng]
            self._emit_waits(eng, deps)

    def finish(self):
        self.barrier()
        nc = self.nc
        q = self.q
        with nc.Block() as block:
            def run(e, lst, nm):
                line = None
                for f in lst:
                    if isinstance(f, tuple):
                        line = f[1]
                        continue
                    if DEBUG_LINES:
                        b = nc.next_id()
                        f(e)
                        a = nc.next_id()
                        for t in DEBUG_LINES:
                            if b <= t < a + 1:
                                print("DEBUG_INST", t, nm, "line", line, "ids", b, a)
                    else:
                        f(e)

            @block.tensor
            def _(e):
                run(e, q["pe"], "pe")

            @block.scalar
            def _(e):
                run(e, q["act"], "act")

            @block.vector
            def _(e):
                run(e, q["dve"], "dve")

            @block.gpsimd
            def _(e):
                run(e, q["pool"], "pool")

            @block.sync
            def _(e):
                run(e, q["sp"], "sp")
        while self.scopes:
            self.pop()
        self.es.close()


class Rot:
    def __init__(self, items):
        self.items = items
        self.i = 0

    def next(self):
        it = self.items[self.i]
        self.i = (self.i + 1) % len(self.items)
        return it


def in_blocks():
    blks = []
    for i in range(8):
        blks.append(("z", i * 512, 512, i))
    for i in range(12):
        blks.append(("xbc", 4096 + i * 512, 512, i))
    blks.append(("dt", 10240, 128, 0))
    for i in range(4):
        blks.append(("q", 10368 + i * 512, 512, i))
    for i in range(4):
        blks.append(("k", 12416 + i * 512, 512, i))
    for i in range(8):
        blks.append(("v", 14464 + i * 512, 512, i))
    for i in range(8):
        blks.append(("gret", 18560 + i * 512, 512, i))
    for i in range(8):
        blks.append(("gates", 22656 + i * 512, 512, i))
    return blks


def ret_consts():
    h = np.arange(8, dtype=np.float64)
    lg_f = np.log1p(-np.exp2(-5.0 - h))
    lg_b = lg_f[::-1].copy()
    idx = np.arange(128, dtype=np.float64)
    dij = idx[None, :] - idx[:, None]
    dec = np.zeros((128, 8, 128), np.float64)
    for hh in range(8):
        f = np.where(dij >= 0, np.exp(np.maximum(dij, 0) * lg_f[hh]), 0.0)
        b = np.where(dij < 0, np.exp(np.maximum(-dij, 0) * lg_b[hh]), 0.0)
        dec[:, hh, :] = f + b
    xif = np.exp((idx[None, :] + 1.0) * lg_f[:, None])
    xib = np.exp((128.0 - idx[None, :]) * lg_b[:, None])
    zf = np.exp((127.0 - idx[:, None]) * lg_f[None, :])
    zb = np.exp((idx[:, None]) * lg_b[None, :])
    gf = np.exp(128.0 * lg_f)
    gb = np.exp(128.0 * lg_b)
    xi = np.stack([xif, xib], 0)
    xi_bc = np.broadcast_to(xi[None], (128, 2, 8, 128)).astype(np.float32).copy()
    zeta = np.stack([zf, zb], 1).astype(np.float32)
    return dec.astype(np.float32), xi_bc, zeta, gf, gb


def build(NSEG, SEGLEN, dump=(), upto=99):
    NT = NSEG * SEGLEN
    NTT = NT // 512
    NCH = NT // 128
    TC = 512
    nc = bass.Bass("TRN2", target_bir_lowering=False)

    def din(name, shape, dt=F32):
        return nc.dram_tensor(name, list(shape), dt, kind="ExternalInput").ap()

    def dscr(name, shape, dt=BF16):
        kind = "ExternalOutput" if name in dump else "Internal"
        return nc.dram_tensor(name, list(shape), dt, kind=kind).ap()

    x_in = din("x", [NT, D])
    cT_in = din("cT", [128, KD, NSEG])
    cont_in = din("cont", [128, 1])
    cos_in = din("cosT", [128, NT])
    sin_in = din("sinT", [128, NT])
    dec_in = din("dec", [128, 8, 128])
    xi_in = din("xi", [128, 2, 8, 128])
    zeta_in = din("zeta", [128, 2, 8])
    w_ada = din("w_ada", [D, 6 * D])
    b_adaT = din("b_adaT", [128, 96])
    norm1T = din("norm1T", [128, KD])
    w_in = din("w_in", [D, E_IN])
    conv_wT = din("conv_wT", [128, 48, 5])
    conv_bT = din("conv_bT", [128, 48])
    dt_bias = din("dt_bias", [1, 128])
    a_log = din("a_log", [1, 128])
    ssd_d = din("ssd_d", [1, 64])
    ssd_normT = din("ssd_normT", [128, 32])
    w_pa = din("w_pa", [4096, D])
    w_pb = din("w_pb", [4096, D])
    w_out = din("w_out", [D, D])
    norm2T = din("norm2T", [128, KD])
    w_up = din("w_up", [D, 2 * DFF])
    fconv_wT = din("fconv_wT", [128, NF, 3])
    fconv_bT = din("fconv_bT", [128, NF])
    w_down = din("w_down", [DFF, D])
    norm_fT = din("norm_fT", [128, KD])
    y_out = nc.dram_tensor("y", [NT, D], F32, kind="ExternalOutput").ap()

    WIN = dscr("WIN", [D, E_IN])
    WPA = dscr("WPA", [4096, D])
    WPB = dscr("WPB", [4096, D])
    WOUT = dscr("WOUT", [D, D])
    WUP = dscr("WUP", [D, 2 * DFF])
    WDN = dscr("WDN", [DFF, D])
    XT = dscr("XT", [D, NT], F32)
    Z = dscr("Z", [NT, 4096])
    XBC = dscr("XBC", [6144, NT])
    DTR = dscr("DTR", [NT, 128], F32)
    QT = dscr("QT", [2048, NT])
    KT = dscr("KT", [2048, NT])
    V = dscr("V", [NT, 4096])
    GRET = dscr("GRET", [NT, 4096])
    GATES = dscr("GATES", [4096, NT])
    XS = dscr("XS", [NT, 4096])
    BTM = dscr("BTM", [NT, 1024])
    BT = dscr("BT", [1024, NT])
    CT = dscr("CT", [1024, NT])
    SBS = dscr("SBS", [NCH, 128, 4096])
    SBR = dscr("SBR", [NCH, 128, 8192])
    YAT = dscr("YAT", [4096, NT])
    OT = dscr("OT", [4096, NT])
    H = dscr("H", [D, NT], F32)
    AT = dscr("AT", [DFF, NT])
    GT = dscr("GT", [DFF, NT])

    P = Prog(nc)
    dec_c, xi_c, zeta_c, gch_f, gch_b = ret_consts()

    ident_f = P.sbuf("ident_f", [128, 128], F32)
    ident_b = P.sbuf("ident_b", [128, 128], BF16)
    ones_f = P.sbuf("ones_f", [128, 128], F32)
    Rc = Res("consts")
    P.op("pool", lambda e: e.memset(ident_f[:], 1.0), writes=[Rc])
    P.op("pool", lambda e: e.affine_select(out=ident_f[:], in_=ident_f[:], pattern=[[-1, 128]],
                                           compare_op=ALU.is_equal, fill=0.0, base=0, channel_multiplier=1),
         reads=[Rc], writes=[Rc])
    P.op("pool", lambda e: e.tensor_copy(out=ident_b[:], in_=ident_f[:]), reads=[Rc], writes=[Rc])
    P.op("pool", lambda e: e.memset(ones_f[:], 1.0), writes=[Rc])
    epsc = P.sbuf("epsc", [128, 1], F32)
    P.op("pool", lambda e: e.memset(epsc[:], EPS), writes=[Rc])
    cont = P.sbuf("cont", [128, 1], F32)
    P.dma("sp", lambda e: e.dma_start(out=cont[:], in_=cont_in), writes=[Rc])
    modT = P.sbuf("modT", [128, 96, NSEG], F32)
    scale1 = P.sbuf("scale1", [128, KD, NSEG], F32)
    scale2 = P.sbuf("scale2", [128, KD, NSEG], F32)
    n1T = P.sbuf("n1T_c", [128, KD], F32)
    n2T = P.sbuf("n2T_c", [128, KD], F32)
    nfT = P.sbuf("nfT_c", [128, KD], F32)
    Rmod = Res("mod")
    P.dma("sp", lambda e: e.dma_start(out=n1T[:], in_=norm1T), writes=[Rmod])
    P.dma("sp", lambda e: e.dma_start(out=n2T[:], in_=norm2T), writes=[Rmod])
    P.dma("sp", lambda e: e.dma_start(out=nfT[:], in_=norm_fT), writes=[Rmod])
    P.barrier()

    banks = []
    for i in range(8):
        t = P.psum("bank%d" % i, [128, 512], F32)
        banks.append((t, Res("bank%d" % i, excl=True)))
    PB = Rot(banks[:7])
    ACCB = banks[7]

    evac_flip = [0]

    def evac(out, in_, reads, writes):
        evac_flip[0] ^= 1
        if evac_flip[0]:
            P.op("act", lambda e: e.activation(out=out, in_=in_, func=AF.Copy), reads, writes)
        else:
            P.op("dve", lambda e: e.tensor_copy(out=out, in_=in_), reads, writes)

    def mm(out, lhsT, rhs, start, stop, reads, writes, inc):
        P.op("pe", lambda e: e.matmul(out, lhsT=lhsT, rhs=rhs, start=start, stop=stop), reads, writes, inc)

    def tr(out, in_, idn, reads, writes, inc=True):
        P.op("pe", lambda e: e.transpose(out, in_, idn), reads, writes, inc)

    def phase0():
        P.push()
        def cast(dst, src, rows, cols, cstep):
            for r in range(0, rows, 128):
                for c0 in range(0, cols, cstep):
                    c1 = min(cols, c0 + cstep)
                    P.dma("pool", lambda e, r=r, c0=c0, c1=c1: e.dma_start(out=dst[r:r + 128, c0:c1], in_=src[r:r + 128, c0:c1]))
        cast(WIN, w_in, D, E_IN, 6688)
        cast(WPB, w_pb, 4096, D, 2048)
        cast(WOUT, w_out, D, D, 2048)
        cast(WUP, w_up, D, 2 * DFF, 5632)
        cast(WDN, w_down, DFF, D, 2048)
        snT = P.sbuf("snT", [128, 32], F32)
        Rsn = Res()
        P.dma("sp", lambda e: e.dma_start(out=snT[:], in_=ssd_normT), writes=[Rsn])
        stg = [(P.sbuf("wpa_f", [128, D], F32), Res()) for _ in range(2)]
        stgb = [(P.sbuf("wpa_b", [128, D], BF16), Res()) for _ in range(2)]
        for t in range(32):
            (a, Ra), (b, Rb) = stg[t % 2], stgb[t % 2]
            P.dma("sp", lambda e, a=a, t=t: e.dma_start(out=a[:], in_=w_pa[t * 128:(t + 1) * 128, :]), writes=[Ra])
            P.op("act", lambda e, a=a, b=b, t=t: e.activation(out=b[:], in_=a[:], func=AF.Copy, scale=snT[:, t:t + 1]),
                 reads=[Ra, Rsn], writes=[Rb])
            P.dma("sp", lambda e, b=b, t=t: e.dma_start(out=WPA[t * 128:(t + 1) * 128, :], in_=b[:]), reads=[Rb])
        cT = P.sbuf("cT", [128, KD, NSEG], F32)
        sc = P.sbuf("sc", [128, KD, NSEG], F32)
        baT = P.sbuf("baT", [128, 96], F32)
        Rct, Rsc, Rba = Res(), Res(), Res()
        P.dma("sp", lambda e: e.dma_start(out=cT[:], in_=cT_in), writes=[Rct])
        P.dma("sp", lambda e: e.dma_start(out=baT[:], in_=b_adaT), writes=[Rba])
        P.op("act", lambda e: e.activation(out=sc[:], in_=cT[:], func=AF.Silu), reads=[Rct], writes=[Rsc])
        wa = [(P.sbuf("wada", [128, KD, 512], F32), Res()) for _ in range(2)]
        for blk in range(24):
            w, Rw = wa[blk % 2]
            P.dma("sp", lambda e, w=w, blk=blk: e.dma_start(
                out=w[:], in_=w_ada[:, blk * 512:(blk + 1) * 512].rearrange("(k p) n -> p k n", p=128)), writes=[Rw])
            bk, Rbk = PB.next()
            for j in range(4):
                for k in range(KD):
                    mm(bk[:, j * NSEG:(j + 1) * NSEG], w[:, k, j * 128:(j + 1) * 128], sc[:, k, :],
                       k == 0, k == KD - 1, [Rw, Rsc], [Rbk], k == KD - 1)
            for j in range(4):
                jj = blk * 4 + j
                P.op("dve", lambda e, bk=bk, j=j, jj=jj: e.tensor_scalar(
                    out=modT[:, jj, :], in0=bk[:, j * NSEG:(j + 1) * NSEG], scalar1=baT[:, jj:jj + 1], scalar2=None,
                    op0=ALU.add), reads=[Rbk, Rba], writes=[Rmod])
        for (dst, nT, off) in ((scale1, n1T, 16), (scale2, n2T, 64)):
            for s in range(NSEG):
                P.op("dve", lambda e, dst=dst, nT=nT, off=off, s=s: e.scalar_tensor_tensor(
                    out=dst[:, :, s], in0=modT[:, off:off + 16, s], scalar=1.0, in1=nT[:, :],
                    op0=ALU.add, op1=ALU.mult), reads=[Rmod], writes=[Rmod])
        P.barrier()
        P.pop()

    def phaseA():
        P.push()
        xin = [(P.sbuf("xin", [128, D], F32), Res()) for _ in range(2)]
        xT = P.sbuf("xT", [128, KD, 512], F32)
        RxT = Res()
        nT = P.sbuf("nT", [128, KD, 512], BF16)
        RnT = Res()
        sq = [(P.sbuf("sq", [128, 512], F32), Res()) for _ in range(2)]
        rstd = P.sbuf("rstd", [128, 512], F32)
        Rrstd = Res()
        tmp = [(P.sbuf("tmpA", [128, 512], F32), Res()) for _ in range(2)]
        cs = P.sbuf("cosA", [128, 512], F32)
        sn = P.sbuf("sinA", [128, 512], F32)
        Rcs = Res()
        wb = [(P.sbuf("wblk", [128, KD, 512], BF16), Res()) for _ in range(3)]
        WB = Rot(wb)
        stg = Rot([(P.sbuf("stgA", [128, 512], BF16), Res()) for _ in range(6)])
        stgf = Rot([(P.sbuf("stgAf", [128, 128], F32), Res()) for _ in range(2)])
        rt = [(P.sbuf("ropeT", [128, 512], F32), Res()) for _ in range(4)]
        blks = in_blocks()
        for tt in range(NTT):
            t0 = tt * 512
            s = t0 // SEGLEN
            P.dma("sp", lambda e, t0=t0: e.dma_start(out=cs[:], in_=cos_in[:, t0:t0 + 512]), writes=[Rcs])
            P.dma("sp", lambda e, t0=t0: e.dma_start(out=sn[:], in_=sin_in[:, t0:t0 + 512]), writes=[Rcs])
            for sb in range(4):
                xi_, Rxi = xin[sb % 2]
                P.dma("sp", lambda e, xi_=xi_, r0=t0 + sb * 128: e.dma_start(out=xi_[:], in_=x_in[r0:r0 + 128, :]), writes=[Rxi])
                for k4 in range(4):
                    bk, Rbk = PB.next()
                    for kk in range(4):
                        k = k4 * 4 + kk
                        tr(bk[:, kk * 128:(kk + 1) * 128], xi_[:, k * 128:(k + 1) * 128], ident_f[:], [Rxi, Rc], [Rbk], kk == 3)
                    evac(xT[:, k4 * 4:(k4 + 1) * 4, sb * 128:(sb + 1) * 128],
                         bk[:].rearrange("p (k n) -> p k n", k=4), [Rbk], [RxT])
            for k in range(KD):
                P.dma("pool", lambda e, k=k, t0=t0: e.dma_start(out=XT[k * 128:(k + 1) * 128, t0:t0 + 512], in_=xT[:, k, :]), reads=[RxT])
            bss, Rbss = ACCB
            for k in range(KD):
                q_, Rq = sq[k % 2]
                P.op("act", lambda e, q_=q_, k=k: e.activation(out=q_[:], in_=xT[:, k, :], func=AF.Square), reads=[RxT], writes=[Rq])
                mm(bss[:], ones_f[:], q_[:], k == 0, k == KD - 1, [Rq, Rc], [Rbss], True)
            P.op("act", lambda e, bss=bss: e.activation(out=rstd[:], in_=bss[:], func=AF.Ln, bias=epsc[:, 0:1], scale=1.0 / D), reads=[Rbss], writes=[Rrstd])
            P.op("act", lambda e, bss=bss: e.activation(out=rstd[:], in_=rstd[:], func=AF.Exp, scale=-0.5), reads=[Rrstd], writes=[Rrstd])
            for k in range(KD):
                tp, Rtp = tmp[k % 2]
                P.op("dve", lambda e, tp=tp, k=k, s=s: e.scalar_tensor_tensor(
                    out=tp[:], in0=xT[:, k, :], scalar=scale1[:, k, s:s + 1], in1=rstd[:], op0=ALU.mult, op1=ALU.mult),
                    reads=[RxT, Rrstd, Rmod], writes=[Rtp])
                P.op("act", lambda e, tp=tp, k=k, s=s: e.activation(
                    out=nT[:, k, :], in_=tp[:], func=AF.Identity, bias=modT[:, k, s:s + 1], scale=1.0),
                    reads=[Rtp, Rmod], writes=[RnT])
            for (kind, e0, ncol, bi) in blks:
                w, Rw = WB.next()
                P.dma("sp", lambda e, w=w, e0=e0, ncol=ncol: e.dma_start(
                    out=w[:, :, 0:ncol], in_=WIN[:, e0:e0 + ncol].rearrange("(k p) n -> p k n", p=128)), writes=[Rw])
                if kind in ("z", "v", "gret", "dt"):
                    dst = {"z": Z, "v": V, "gret": GRET, "dt": DTR}[kind]
                    for sb in range(4):
                        bk, Rbk = PB.next()
                        for k in range(KD):
                            mm(bk[:, 0:ncol], nT[:, k, sb * 128:(sb + 1) * 128], w[:, k, 0:ncol], k == 0, k == KD - 1,
                               [RnT, Rw], [Rbk], k == KD - 1)
                        r0 = t0 + sb * 128
                        if kind == "dt":
                            st, Rst = stgf.next()
                            evac(st[:], bk[:, 0:128], [Rbk], [Rst])
                            P.dma("pool", lambda e, st=st, r0=r0: e.dma_start(out=DTR[r0:r0 + 128, :], in_=st[:]), reads=[Rst])
                        else:
                            st, Rst = stg.next()
                            evac(st[:], bk[:], [Rbk], [Rst])
                            P.dma("pool", lambda e, st=st, r0=r0, dst=dst, c0=bi * 512: e.dma_start(
                                out=dst[r0:r0 + 128, c0:c0 + 512], in_=st[:]), reads=[Rst])
                elif kind in ("xbc", "gates"):
                    dst = XBC if kind == "xbc" else GATES
                    for j in range(4):
                        bk, Rbk = PB.next()
                        for k in range(KD):
                            mm(bk[:], w[:, k, j * 128:(j + 1) * 128], nT[:, k, :], k == 0, k == KD - 1,
                               [RnT, Rw], [Rbk], k == KD - 1)
                        st, Rst = stg.next()
                        evac(st[:], bk[:], [Rbk], [Rst])
                        r0 = bi * 512 + j * 128
                        P.dma("pool", lambda e, st=st, r0=r0, dst=dst, t0=t0: e.dma_start(
                            out=dst[r0:r0 + 128, t0:t0 + 512], in_=st[:]), reads=[Rst])
                else:
                    dst = QT if kind == "q" else KT
                    sc_ = 1.0 if kind == "q" else 0.0625
                    for hh in range(2):
                        pr = []
                        for j in range(2):
                            bk, Rbk = PB.next()
                            jj = hh * 2 + j
                            for k in range(KD):
                                mm(bk[:], w[:, k, jj * 128:(jj + 1) * 128], nT[:, k, :], k == 0, k == KD - 1,
                                   [RnT, Rw], [Rbk], k == KD - 1)
                            pr.append((bk, Rbk))
                        (b1, R1), (b2, R2) = pr
                        (ta, Ra), (tb, Rb), (tc_, Rcc), (td, Rd) = rt
                        P.op("dve", lambda e, b1=b1, ta=ta: e.scalar_tensor_tensor(out=ta[:], in0=b1[:], scalar=sc_, in1=cs[:], op0=ALU.mult, op1=ALU.mult), reads=[R1, Rcs], writes=[Ra])
                        P.op("dve", lambda e, b2=b2, tb=tb: e.scalar_tensor_tensor(out=tb[:], in0=b2[:], scalar=sc_, in1=sn[:], op0=ALU.mult, op1=ALU.mult), reads=[R2, Rcs], writes=[Rb])
                        P.op("dve", lambda e, b1=b1, tc_=tc_: e.scalar_tensor_tensor(out=tc_[:], in0=b1[:], scalar=sc_, in1=sn[:], op0=ALU.mult, op1=ALU.mult), reads=[R1, Rcs], writes=[Rcc])
                        P.op("dve", lambda e, b2=b2, td=td: e.scalar_tensor_tensor(out=td[:], in0=b2[:], scalar=sc_, in1=cs[:], op0=ALU.mult, op1=ALU.mult), reads=[R2, Rcs], writes=[Rd])
                        s1, Rs1 = stg.next()
                        P.op("pool", lambda e, s1=s1, ta=ta, tb=tb: e.tensor_tensor(out=s1[:], in0=ta[:], in1=tb[:], op=ALU.subtract), reads=[Ra, Rb], writes=[Rs1])
                        s2, Rs2 = stg.next()
                        P.op("pool", lambda e, s2=s2, tc_=tc_, td=td: e.tensor_tensor(out=s2[:], in0=tc_[:], in1=td[:], op=ALU.add), reads=[Rcc, Rd], writes=[Rs2])
                        r0 = bi * 512 + hh * 256
                        P.dma("pool", lambda e, s1=s1, r0=r0, dst=dst, t0=t0: e.dma_start(out=dst[r0:r0 + 128, t0:t0 + 512], in_=s1[:]), reads=[Rs1])
                        P.dma("pool", lambda e, s2=s2, r0=r0, dst=dst, t0=t0: e.dma_start(out=dst[r0 + 128:r0 + 256, t0:t0 + 512], in_=s2[:]), reads=[Rs2])
        P.barrier()
        P.pop()

    def phaseA2():
        P.push()
        cw = P.sbuf("cw", [128, 48, 5], F32)
        cb = P.sbuf("cb", [128, 48], F32)
        Rcw = Res()
        P.dma("sp", lambda e: e.dma_start(out=cw[:], in_=conv_wT), writes=[Rcw])
        P.dma("sp", lambda e: e.dma_start(out=cb[:], in_=conv_bT), writes=[Rcw])
        xb = Rot([(P.sbuf("xbA2", [128, 516], BF16), Res()) for _ in range(6)])
        acc = Rot([(P.sbuf("accA2", [128, 512], F32), Res()) for _ in range(4)])
        grp = Rot([(P.sbuf("grpA2", [128, 8, 512], BF16), Res()) for _ in range(2)])
        stg = Rot([(P.sbuf("stgA2", [128, 1024], BF16), Res()) for _ in range(3)])
        def a2_s1(g, f8, t0, lo, hi):
            f = g * 8 + f8
            x_, Rx = xb.next()
            if lo > t0 - 2:
                P.op("pool", lambda e, x_=x_: e.memset(x_[:, 0:2], 0.0), writes=[Rx])
            if hi < t0 + 514:
                P.op("pool", lambda e, x_=x_: e.memset(x_[:, 514:516], 0.0), writes=[Rx])
            P.dma("sp", lambda e, x_=x_, f=f, lo=lo, hi=hi, t0=t0: e.dma_start(
                out=x_[:, lo - (t0 - 2):hi - (t0 - 2)], in_=XBC[f * 128:(f + 1) * 128, lo:hi]), writes=[Rx])
            if t0 % SEGLEN == 0 and t0 > 0:
                P.op("act", lambda e, x_=x_: e.activation(out=x_[:, 0:2], in_=x_[:, 0:2], func=AF.Copy, scale=cont[:, 0:1]), reads=[Rx, Rc], writes=[Rx])
            if (t0 + 512) % SEGLEN == 0 and t0 + 512 < NT:
                P.op("act", lambda e, x_=x_: e.activation(out=x_[:, 514:516], in_=x_[:, 514:516], func=AF.Copy, scale=cont[:, 0:1]), reads=[Rx, Rc], writes=[Rx])
            a_, Ra = acc.next()
            P.op("act", lambda e, a_=a_, x_=x_, f=f: e.activation(out=a_[:], in_=x_[:, 0:512], func=AF.Identity,
                                                               bias=cb[:, f:f + 1], scale=cw[:, f, 0:1]), reads=[Rx, Rcw], writes=[Ra])
            for k in range(1, 5):
                P.op("dve", lambda e, a_=a_, x_=x_, f=f, k=k: e.scalar_tensor_tensor(
                    out=a_[:], in0=x_[:, k:k + 512], scalar=cw[:, f, k:k + 1], in1=a_[:], op0=ALU.mult, op1=ALU.add),
                    reads=[Rx, Rcw, Ra], writes=[Ra])
            return (f8, a_, Ra)

        for tt in range(NTT):
            t0 = tt * 512
            lo = max(t0 - 2, 0)
            hi = min(t0 + 514, NT)
            for g in range(6):
                gt, Rgt = grp.next()
                prev = None
                for f8 in range(9):
                    if f8 < 8:
                        cur = a2_s1(g, f8, t0, lo, hi)
                    if prev is not None:
                        pf8, pa_, pRa = prev
                        P.op("act", lambda e, pa_=pa_, gt=gt, pf8=pf8: e.activation(out=gt[:, pf8, :], in_=pa_[:], func=AF.Silu), reads=[pRa], writes=[Rgt])
                    prev = cur if f8 < 8 else None
                if g >= 4:
                    dstT = BT if g == 4 else CT
                    for f8 in range(8):
                        P.dma("pool", lambda e, gt=gt, f8=f8, dstT=dstT, t0=t0: e.dma_start(
                            out=dstT[f8 * 128:(f8 + 1) * 128, t0:t0 + 512], in_=gt[:, f8, :]), reads=[Rgt])
                if g <= 4:
                    for sb in range(4):
                        bk, Rbk = PB.next()
                        bkb = bk[:].bitcast(BF16)
                        for f8 in range(8):
                            tr(bkb[:, f8 * 128:(f8 + 1) * 128], gt[:, f8, sb * 128:(sb + 1) * 128], ident_b[:], [Rgt, Rc], [Rbk], f8 == 7)
                        st, Rst = stg.next()
                        evac(st[:], bkb, [Rbk], [Rst])
                        r0 = t0 + sb * 128
                        if g < 4:
                            P.dma("pool", lambda e, st=st, r0=r0, g=g: e.dma_start(out=XS[r0:r0 + 128, g * 1024:(g + 1) * 1024], in_=st[:]), reads=[Rst])
                        else:
                            P.dma("pool", lambda e, st=st, r0=r0: e.dma_start(out=BTM[r0:r0 + 128, :], in_=st[:]), reads=[Rst])
        P.barrier()
        P.pop()

    def scans():
        P.push()
        Rk = Res("scanconst")
        dtb = P.sbuf("dtb", [128, 128], F32)
        Abc = P.sbuf("Abc", [128, 128], F32)
        Dbc = P.sbuf("Dbc", [128, 64], F32)
        P.dma("sp", lambda e: e.dma_start(out=dtb[:], in_=dt_bias.partition_broadcast(128)), writes=[Rk])
        P.dma("sp", lambda e: e.dma_start(out=Abc[:], in_=a_log.partition_broadcast(128)), writes=[Rk])
        P.dma("sp", lambda e: e.dma_start(out=Dbc[:], in_=ssd_d.partition_broadcast(128)), writes=[Rk])
        P.op("act", lambda e: e.activation(out=Abc[:], in_=Abc[:], func=AF.Exp), reads=[Rk], writes=[Rk])
        P.op("dve", lambda e: e.tensor_scalar(out=Abc[:], in0=Abc[:], scalar1=-1.0, scalar2=None, op0=ALU.mult), reads=[Rk], writes=[Rk])
        dec = P.sbuf("dec", [128, 8, 128], F32)
        xi = P.sbuf("xi", [128, 2, 8, 128], F32)
        zeta = P.sbuf("zeta", [128, 2, 8], F32)
        P.dma("sp", lambda e: e.dma_start(out=dec[:], in_=dec_in), writes=[Rk])
        P.dma("sp", lambda e: e.dma_start(out=xi[:], in_=xi_in), writes=[Rk])
        P.dma("sp", lambda e: e.dma_start(out=zeta[:], in_=zeta_in), writes=[Rk])
        UT = P.sbuf("UT", [128, 128], F32)
        LT = P.sbuf("LT", [128, 128], F32)
        NMf = P.sbuf("NMf", [128, 128], BF16)
        NMb = P.sbuf("NMb", [128, 128], BF16)
        P.op("pool", lambda e: e.memset(UT[:], 1.0), writes=[Rk])
        P.op("pool", lambda e: e.affine_select(out=UT[:], in_=UT[:], pattern=[[1, 128]], compare_op=ALU.is_ge, fill=0.0, base=0, channel_multiplier=-1), reads=[Rk], writes=[Rk])
        P.op("pool", lambda e: e.memset(LT[:], 1.0), writes=[Rk])
        P.op("pool", lambda e: e.affine_select(out=LT[:], in_=LT[:], pattern=[[-1, 128]], compare_op=ALU.is_ge, fill=0.0, base=0, channel_multiplier=1), reads=[Rk], writes=[Rk])
        P.op("pool", lambda e: e.memset(NMf[:], 0.0), writes=[Rk])
        P.op("pool", lambda e: e.affine_select(out=NMf[:], in_=NMf[:], pattern=[[1, 128]], compare_op=ALU.is_ge, fill=-30000.0, base=0, channel_multiplier=-1), reads=[Rk], writes=[Rk])
        P.op("pool", lambda e: e.memset(NMb[:], 0.0), writes=[Rk])
        P.op("pool", lambda e: e.affine_select(out=NMb[:], in_=NMb[:], pattern=[[-1, 128]], compare_op=ALU.is_ge, fill=-30000.0, base=0, channel_multiplier=1), reads=[Rk], writes=[Rk])

        S32 = P.sbuf("S32", [128, 4096], F32)
        R32 = P.sbuf("R32", [128, 8, 2, 512], F32)
        RS, RR = Res("S"), Res("R")

        def dbl(name, shape, dt, n=2):
            return Rot([(P.sbuf(name, shape, dt), Res()) for _ in range(n)])
        sbf = dbl("sbf", [128, 512], BF16, 3)
        rbf = dbl("rbf", [128, 2, 512], BF16, 3)
        xsT = dbl("xs", [128, 4096], BF16)
        bTM = dbl("btm", [128, 1024], BF16)
        dtr = dbl("dtr", [128, 128], F32)
        vT = dbl("v", [128, 4096], BF16, 1)
        kTt = dbl("kT", [128, 16, 128], BF16)
        dtt = P.sbuf("dtt", [128, 128], F32); Rdt = Res()
        lndt = P.sbuf("lndt", [128, 128], F32)
        dA = P.sbuf("dA", [128, 128], F32); RdA = Res()
        cum = P.sbuf("cum", [128, 128], F32); Rcum = Res()
        biasE = P.sbuf("biasE", [128, 128], F32); RbE = Res()
        etot = P.sbuf("etot", [128, 128], F32); Ret = Res()
        wgt = P.sbuf("wgt", [128, 128], F32); Rwg = Res()
        ecum = P.sbuf("ecum", [128, 128], F32); Rec = Res()
        tsm = P.sbuf("tsm", [128, 128], F32); Rts = Res()
        xw = dbl("xw", [128, 512], BF16)
        kz = P.sbuf("kz", [128, 8, 256], BF16); Rkz = Res()

        def small_dt(c, dr, Rdr):
            P.op("dve", lambda e: e.tensor_tensor(out=tsm[:], in0=dr[:], in1=dtb[:], op=ALU.add), reads=[Rdr, Rk], writes=[Rts])
            P.op("act", lambda e: e.activation(out=tsm[:], in_=tsm[:], func=AF.Exp), reads=[Rts], writes=[Rts])
            P.op("act", lambda e: e.activation(out=dtt[:], in_=tsm[:], func=AF.Ln, bias=1.0, scale=1.0), reads=[Rts], writes=[Rdt])
            P.op("act", lambda e: e.activation(out=lndt[:], in_=dtt[:], func=AF.Ln), reads=[Rdt], writes=[Rdt])
            P.op("dve", lambda e: e.tensor_tensor(out=dA[:], in0=dtt[:], in1=Abc[:], op=ALU.mult), reads=[Rdt, Rk], writes=[RdA])
            if SCAN_STOP == 31:
                return
            bk, Rbk = PB.next()
            mm(bk[:, 0:64], UT[:], dA[:, 0:64], True, True, [RdA, Rk], [Rbk], False)
            mm(bk[:, 64:128], LT[:], dA[:, 64:128], True, True, [RdA, Rk], [Rbk], False)
            mm(bk[:, 128:256], ones_f[:], dA[:], True, True, [RdA, Rk, Rc], [Rbk], True)
            if SCAN_STOP == 32:
                return
            P.op("dve", lambda e: e.tensor_copy(out=cum[:], in_=bk[:, 0:128]), reads=[Rbk], writes=[Rcum])
            P.op("dve", lambda e: e.tensor_tensor(out=biasE[:], in0=lndt[:], in1=cum[:], op=ALU.subtract), reads=[Rdt, Rcum], writes=[RbE])
            if SCAN_STOP == 33:
                return
            P.op("act", lambda e: e.activation(out=etot[:], in_=bk[:, 128:256], func=AF.Exp), reads=[Rbk], writes=[Ret])
            P.op("dve", lambda e: e.tensor_tensor(out=tsm[:], in0=bk[:, 128:256], in1=biasE[:], op=ALU.add), reads=[Rbk, RbE, Rts], writes=[Rts])
            P.op("act", lambda e: e.activation(out=wgt[:], in_=tsm[:], func=AF.Exp), reads=[Rts], writes=[Rwg])

        def bc(ap2d, n_outer, n_inner):
            return ap2d.unsqueeze(2).to_broadcast([128, n_outer, n_inner])

        def load_chunk(c):
            t0 = c * 128
            xs_, Rxs = xsT.next()
            b_, Rb = bTM.next()
            dr, Rdr = dtr.next()
            v_, Rv = vT.next()
            k_, Rkt = kTt.next()
            P.dma("sp", lambda e: e.dma_start(out=xs_[:], in_=XS[t0:t0 + 128, :]), writes=[Rxs])
            P.dma("sp", lambda e: e.dma_start(out=b_[:], in_=BTM[t0:t0 + 128, :]), writes=[Rb])
            P.dma("sp", lambda e: e.dma_start(out=dr[:], in_=DTR[t0:t0 + 128, :]), writes=[Rdr])
            P.dma("sp", lambda e: e.dma_start(out=v_[:], in_=V[t0:t0 + 128, :]), writes=[Rv])
            P.dma("sp", lambda e: e.dma_start(out=k_[:], in_=KT[:, t0:t0 + 128].rearrange("(t p) n -> p t n", p=128)), writes=[Rkt])
            return (xs_, Rxs), (b_, Rb), (dr, Rdr), (v_, Rv), (k_, Rkt)

        def state_update(d, xs_, Rxs, b_, Rb, v_, Rv, k_, Rkt):
            ho = d * 64
            for g in range(8):
                sl = slice(g * 512, (g + 1) * 512)
                xw_, Rxw = xw.next()
                P.op("dve", lambda e, xw_=xw_, sl=sl, g=g: e.tensor_tensor(
                    out=xw_[:].rearrange("p (h q) -> p h q", q=64), in0=xs_[:, sl].rearrange("p (h q) -> p h q", q=64),
                    in1=bc(wgt[:, ho + g * 8:ho + g * 8 + 8], 8, 64), op=ALU.mult), reads=[Rxs, Rwg], writes=[Rxw])
                bk, Rbk = PB.next()
                mm(bk[:], b_[:, g * 128:(g + 1) * 128], xw_[:], True, True, [Rb, Rxw], [Rbk], True)
                P.op("dve", lambda e, sl=sl, g=g: e.tensor_tensor(
                    out=S32[:, sl].rearrange("p (h q) -> p h q", q=64), in0=S32[:, sl].rearrange("p (h q) -> p h q", q=64),
                    in1=bc(etot[:, ho + g * 8:ho + g * 8 + 8], 8, 64), op=ALU.mult), reads=[Ret, RS], writes=[RS])
                P.op("dve", lambda e, sl=sl, bk=bk: e.tensor_tensor(out=S32[:, sl], in0=S32[:, sl], in1=bk[:], op=ALU.add), reads=[Rbk, RS], writes=[RS])
            for half in range(2):
                bk, Rbk = PB.next()
                bkb = bk[:].bitcast(BF16)
                for t in range(8):
                    tt_ = half * 8 + t
                    tr(bkb[:, t * 128:(t + 1) * 128], k_[:, tt_, :], ident_b[:], [Rkt, Rc], [Rbk], t == 7)
                P.op("dve", lambda e, bkb=bkb, half=half: e.tensor_tensor(
                    out=kz[:, half * 4:(half + 1) * 4, :], in0=bkb.rearrange("p (h q) -> p h q", q=256),
                    in1=bc(zeta[:, d, half * 4:(half + 1) * 4], 4, 256), op=ALU.mult), reads=[Rbk, Rk], writes=[Rkz])
            gch = gch_f if d == 0 else gch_b
            for h in range(8):
                for dt_ in range(2):
                    bk, Rbk = PB.next()
                    mm(bk[:], kz[:, h, dt_ * 128:(dt_ + 1) * 128], v_[:, h * 512:(h + 1) * 512], True, True, [Rkz, Rv], [Rbk], True)
                    P.op("dve", lambda e, h=h, dt_=dt_, bk=bk: e.scalar_tensor_tensor(
                        out=R32[:, h, dt_, :], in0=R32[:, h, dt_, :], scalar=float(gch[h]), in1=bk[:], op0=ALU.mult, op1=ALU.add),
                        reads=[Rbk, RR], writes=[RR])

        def reset_states():
            P.op("dve", lambda e: e.memset(S32[:], 0.0), writes=[RS])
            P.op("dve", lambda e: e.memset(R32[:].rearrange("p a b c -> p (a b c)"), 0.0), writes=[RR])

        def apply_cont():
            P.op("pool", lambda e: e.tensor_scalar(out=S32[:], in0=S32[:], scalar1=cont[:, 0:1], scalar2=None, op0=ALU.mult), reads=[RS, Rc], writes=[RS])
            v2 = R32[:].rearrange("p a b c -> p (a b c)")
            P.op("pool", lambda e: e.tensor_scalar(out=v2, in0=v2, scalar1=cont[:, 0:1], scalar2=None, op0=ALU.mult), reads=[RR, Rc], writes=[RR])

        shadow_eng = ["act"]

        def s_shadow(g):
            t, Rt = sbf.next()
            if shadow_eng[0] == "act":
                P.op("act", lambda e: e.activation(out=t[:], in_=S32[:, g * 512:(g + 1) * 512], func=AF.Copy), reads=[RS], writes=[Rt])
            else:
                P.op("pool", lambda e: e.tensor_copy(out=t[:], in_=S32[:, g * 512:(g + 1) * 512]), reads=[RS], writes=[Rt])
            return t, Rt

        def r_shadow(h):
            t, Rt = rbf.next()
            if shadow_eng[0] == "act":
                P.op("act", lambda e: e.activation(out=t[:], in_=R32[:, h], func=AF.Copy), reads=[RR], writes=[Rt])
            else:
                P.op("pool", lambda e: e.tensor_copy(out=t[:], in_=R32[:, h]), reads=[RR], writes=[Rt])
            return t, Rt

        if SCAN_STOP == 1:
            P.barrier(); P.pop(); return
        reset_states()
        for c in range(NCH - 1, -1, -1):
            (xs_, Rxs), (b_, Rb), (dr, Rdr), (v_, Rv), (k_, Rkt) = load_chunk(c)
            if (c + 1) * 128 % SEGLEN == 0 and c != NCH - 1:
                apply_cont()
            for g in range(8):
                t, Rt = s_shadow(g)
                P.dma("pool", lambda e, t=t, g=g: e.dma_start(out=SBS[c, :, g * 512:(g + 1) * 512], in_=t[:]), reads=[Rt])
            for h in range(8):
                t, Rt = r_shadow(h)
                P.dma("pool", lambda e, t=t, h=h: e.dma_start(out=SBR[c, :, h * 1024:(h + 1) * 1024], in_=t[:].rearrange("p a b -> p (a b)")), reads=[Rt])
            if SCAN_STOP == 2:
                continue
            small_dt(c, dr, Rdr)
            if SCAN_STOP in (3, 31, 32, 33):
                continue
            state_update(1, xs_, Rxs, b_, Rb, v_, Rv, k_, Rkt)
        P.barrier()
        if SCAN_STOP <= 4:
            P.pop(); return

        reset_states()
        shadow_eng[0] = "pool"
        btT = dbl("bt", [128, 8, 128], BF16)
        ctT = dbl("ct", [128, 8, 128], BF16)
        qTt = dbl("qT", [128, 16, 128], BF16, 1)
        zT = dbl("z", [128, 4096], BF16, 1)
        grT = dbl("gr", [128, 4096], BF16, 1)
        sbin = dbl("sbin", [128, 512], BF16, 3)
        rbin = dbl("rbin", [128, 2, 512], BF16, 3)
        cbT = P.sbuf("cbT", [128, 8, 128], BF16); Rcb = Res()
        hi = P.sbuf("hi", [128, 128], BF16)
        lo = P.sbuf("lo", [128, 128], BF16); Rhl = Res()
        Eb = dbl("E", [128, 2, 4, 128], BF16)
        Es = dbl("Es", [128, 4, 128], BF16)
        MT = dbl("MT", [128, 4, 128], BF16)
        xsD = dbl("xsD", [128, 512], BF16)
        ysb = dbl("ysb", [128, 512], F32)
        t512 = dbl("t512", [128, 512], F32)
        szr = dbl("sz", [128, 512], BF16)
        junk = P.sbuf("junk", [128, 512], F32); Rjk = Res()
        ss = P.sbuf("ss", [128, 16], F32); Rss = Res()
        yab = P.sbuf("yab", [128, 4096], BF16); Ryab = Res()
        MTr = P.sbuf("MTr", [128, 8, 128], BF16); RMr = Res()
        qxr = dbl("qx", [128, 2, 2, 128], BF16)
        stg = dbl("stgF", [128, 8, 128], BF16, 3)
        for c in range(NCH):
            t0 = c * 128
            (xs_, Rxs), (b_, Rb), (dr, Rdr), (v_, Rv), (k_, Rkt) = load_chunk(c)
            bt_, Rbt = btT.next(); ct_, Rct = ctT.next(); q_, Rq = qTt.next(); z_, Rz = zT.next(); gr_, Rgr = grT.next()
            P.dma("sp", lambda e: e.dma_start(out=bt_[:], in_=BT[:, t0:t0 + 128].rearrange("(t p) n -> p t n", p=128)), writes=[Rbt])
            P.dma("sp", lambda e: e.dma_start(out=ct_[:], in_=CT[:, t0:t0 + 128].rearrange("(t p) n -> p t n", p=128)), writes=[Rct])
            P.dma("sp", lambda e: e.dma_start(out=q_[:], in_=QT[:, t0:t0 + 128].rearrange("(t p) n -> p t n", p=128)), writes=[Rq])
            P.dma("sp", lambda e: e.dma_start(out=z_[:], in_=Z[t0:t0 + 128, :]), writes=[Rz])
            P.dma("sp", lambda e: e.dma_start(out=gr_[:], in_=GRET[t0:t0 + 128, :]), writes=[Rgr])
            if t0 % SEGLEN == 0 and c > 0:
                apply_cont()
            small_dt(c, dr, Rdr)
            P.op("act", lambda e: e.activation(out=ecum[:], in_=cum[:], func=AF.Exp), reads=[Rcum], writes=[Rec])
            bk, Rbk = PB.next()
            mm(bk[0:64, 0:128], dA[:, 0:64], UT[:], True, True, [RdA, Rk], [Rbk], False)
            mm(bk[64:128, 0:128], dA[:, 64:128], LT[:], True, True, [RdA, Rk], [Rbk], True)
            P.op("act", lambda e, bk=bk: e.activation(out=hi[:], in_=bk[:, 0:128], func=AF.Copy), reads=[Rbk], writes=[Rhl])
            P.op("dve", lambda e, bk=bk: e.tensor_tensor(out=lo[:], in0=bk[:, 0:128], in1=hi[:], op=ALU.subtract), reads=[Rbk, Rhl], writes=[Rhl])
            for half in range(2):
                bk, Rbk = PB.next()
                for g4 in range(4):
                    g = half * 4 + g4
                    mm(bk[:, g4 * 128:(g4 + 1) * 128], bt_[:, g, :], ct_[:, g, :], True, True, [Rbt, Rct], [Rbk], g4 == 3)
                evac(cbT[:, half * 4:(half + 1) * 4, :], bk[:].rearrange("p (g n) -> p g n", g=4), [Rbk], [Rcb])
            YB = Rot(banks[0:2]); GB = Rot(banks[2:6]); ST = Rot(banks[6:8])

            def ssd_pro(g):
                sl = slice(g * 512, (g + 1) * 512)
                xd, Rxd = xsD.next()
                P.op("pool", lambda e, xd=xd, sl=sl, g=g: e.tensor_tensor(
                    out=xd[:].rearrange("p (h q) -> p h q", q=64), in0=xs_[:, sl].rearrange("p (h q) -> p h q", q=64),
                    in1=bc(Dbc[:, g * 8:g * 8 + 8], 8, 64), op=ALU.mult), reads=[Rxs, Rk], writes=[Rxd])
                sz_, Rsz = szr.next()
                P.op("act", lambda e, sz_=sz_, sl=sl: e.activation(out=sz_[:], in_=z_[:, sl], func=AF.Silu), reads=[Rz], writes=[Rsz])
                sfb, Rsfb = s_shadow(g)
                sbi, Rsbi = sbin.next()
                P.dma("sp", lambda e, sbi=sbi, sl=sl: e.dma_start(out=sbi[:], in_=SBS[c, :, sl]), writes=[Rsbi])
                yb, Ryb = YB.next()
                mm(yb[:], ident_b[:], xd[:], True, False, [Rxd, Rc], [Ryb], False)
                return dict(sl=sl, sz_=sz_, Rsz=Rsz, sfb=sfb, Rsfb=Rsfb, sbi=sbi, Rsbi=Rsbi, yb=yb, Ryb=Ryb, mt={})

            def ssd_A(cx, g, h4):
                if True:
                    E_, RE = Eb.next()
                    for d in range(2):
                        gb_, Rgb = GB.next()
                        NM = NMf if d == 0 else NMb
                        for r in range(4):
                            hidx = d * 64 + g * 8 + h4 * 4 + r
                            sel = bass.AP(ident_b, hidx, [[128, 128], [0, 128]])
                            o_ = gb_[:, r * 128:(r + 1) * 128]
                            mm(o_, sel, hi[:], True, False, [Rhl, Rc], [Rgb], False)
                            mm(o_, sel, lo[:], False, False, [Rhl, Rc], [Rgb], False)
                            mm(o_, ident_b[:], NM[:], False, True, [Rk, Rc], [Rgb], r == 3)
                        for r in range(4):
                            hidx = d * 64 + g * 8 + h4 * 4 + r
                            P.op("act", lambda e, E_=E_, d=d, r=r, gb_=gb_, hidx=hidx: e.activation(
                                out=E_[:, d, r, :], in_=gb_[:, r * 128:(r + 1) * 128], func=AF.Exp, bias=biasE[:, hidx:hidx + 1], scale=1.0),
                                reads=[Rgb, RbE], writes=[RE])
                    es_, Res_ = Es.next()
                    mt_, Rmt = MT.next()
                    P.op("dve", lambda e, es_=es_, E_=E_: e.tensor_tensor(out=es_[:], in0=E_[:, 0], in1=E_[:, 1], op=ALU.add), reads=[RE], writes=[Res_])
                    P.op("dve", lambda e, es_=es_, mt_=mt_, g=g: e.tensor_tensor(out=mt_[:], in0=es_[:], in1=cbT[:, g:g + 1, :].to_broadcast([128, 4, 128]), op=ALU.mult),
                         reads=[Res_, Rcb], writes=[Rmt])
                    cx['mt'][h4] = (mt_, Rmt)

            def ssd_B(cx, g, h4):
                if True:
                    mt_, Rmt = cx['mt'][h4]
                    yb, Ryb = cx['yb'], cx['Ryb']
                    for r in range(4):
                        hl = g * 8 + h4 * 4 + r
                        last = (h4 == 1 and r == 3)
                        mm(yb[:, (h4 * 4 + r) * 64:(h4 * 4 + r + 1) * 64], mt_[:, r, :], xs_[:, hl * 64:(hl + 1) * 64],
                           False, last, [Rmt, Rxs], [Ryb], last)

            def ssd_tail(g, cx):
                sl = cx['sl']; sz_ = cx['sz_']; Rsz = cx['Rsz']; sfb = cx['sfb']; Rsfb = cx['Rsfb']; sbi = cx['sbi']; Rsbi = cx['Rsbi']; yb = cx['yb']; Ryb = cx['Ryb']
                sf, Rsf = ST.next()
                mm(sf[:], ct_[:, g, :], sfb[:], True, True, [Rct, Rsfb], [Rsf], True)
                sbk, Rsbk = ST.next()
                mm(sbk[:], ct_[:, g, :], sbi[:], True, True, [Rct, Rsbi], [Rsbk], True)
                ta, Rta = t512.next()
                P.op("dve", lambda e, ta=ta, sf=sf, g=g: e.tensor_tensor(out=ta[:].rearrange("p (h q) -> p h q", q=64), in0=sf[:].rearrange("p (h q) -> p h q", q=64),
                                                                         in1=bc(ecum[:, g * 8:g * 8 + 8], 8, 64), op=ALU.mult), reads=[Rsf, Rec], writes=[Rta])
                tb, Rtb = t512.next()
                P.op("dve", lambda e, tb=tb, sbk=sbk, g=g: e.tensor_tensor(out=tb[:].rearrange("p (h q) -> p h q", q=64), in0=sbk[:].rearrange("p (h q) -> p h q", q=64),
                                                                           in1=bc(ecum[:, 64 + g * 8:64 + g * 8 + 8], 8, 64), op=ALU.mult), reads=[Rsbk, Rec], writes=[Rtb])
                P.op("pool", lambda e, ta=ta, tb=tb: e.tensor_tensor(out=ta[:], in0=ta[:], in1=tb[:], op=ALU.add), reads=[Rta, Rtb], writes=[Rta])
                ys_, Rys = ysb.next()
                P.op("dve", lambda e, ta=ta, yb=yb, ys_=ys_: e.tensor_tensor(out=ys_[:], in0=yb[:], in1=ta[:], op=ALU.add), reads=[Ryb, Rta], writes=[Rys])
                P.op("pool", lambda e, ys_=ys_, sz_=sz_: e.tensor_tensor(out=ys_[:], in0=ys_[:], in1=sz_[:], op=ALU.mult), reads=[Rys, Rsz], writes=[Rys])
                P.op("act", lambda e, ys_=ys_, g=g: e.activation(out=junk[:], in_=ys_[:], func=AF.Square, accum_out=ss[:, g:g + 1]), reads=[Rys], writes=[Rjk, Rss])
                P.op("act", lambda e, g=g: e.activation(out=ss[:, g:g + 1], in_=ss[:, g:g + 1], func=AF.Ln, bias=epsc[:, 0:1], scale=1.0 / 512), reads=[Rss], writes=[Rss])
                P.op("act", lambda e, g=g: e.activation(out=ss[:, g:g + 1], in_=ss[:, g:g + 1], func=AF.Exp, scale=-0.5), reads=[Rss], writes=[Rss])
                P.op("dve", lambda e, ys_=ys_, sl=sl, g=g: e.tensor_scalar(out=yab[:, sl], in0=ys_[:], scalar1=ss[:, g:g + 1], scalar2=None, op0=ALU.mult), reads=[Rys, Rss], writes=[Ryab])

            cxs = {}
            for n in range(18):
                if n < 16:
                    g, h4 = divmod(n, 2)
                    if h4 == 0:
                        cxs[g] = ssd_pro(g)
                    ssd_A(cxs[g], g, h4)
                if 1 <= n <= 16:
                    g, h4 = divmod(n - 1, 2)
                    ssd_B(cxs[g], g, h4)
                if n >= 3 and (n - 3) % 2 == 0:
                    g = (n - 3) // 2
                    ssd_tail(g, cxs.pop(g))
            for t8 in range(4):
                bk, Rbk = PB.next()
                bkb = bk[:].bitcast(BF16)
                for t in range(8):
                    f = t8 * 8 + t
                    tr(bkb[:, t * 128:(t + 1) * 128], yab[:, f * 128:(f + 1) * 128], ident_b[:], [Ryab, Rc], [Rbk], t == 7)
                st, Rst = stg.next()
                evac(st[:], bkb.rearrange("p (t n) -> p t n", t=8), [Rbk], [Rst])
                P.dma("pool", lambda e, st=st, t8=t8: e.dma_start(
                    out=YAT[t8 * 1024:(t8 + 1) * 1024, t0:t0 + 128].rearrange("(t p) n -> p t n", p=128), in_=st[:]), reads=[Rst])
            for half in range(2):
                bk, Rbk = PB.next()
                for h4 in range(4):
                    h = half * 4 + h4
                    for dt_ in range(2):
                        mm(bk[:, h4 * 128:(h4 + 1) * 128], k_[:, h * 2 + dt_, :], q_[:, h * 2 + dt_, :], dt_ == 0, dt_ == 1, [Rkt, Rq], [Rbk], (h4 == 3 and dt_ == 1))
                P.op("dve", lambda e, bk=bk, half=half: e.tensor_tensor(out=MTr[:, half * 4:(half + 1) * 4, :], in0=bk[:].rearrange("p (h n) -> p h n", h=4),
                                                                        in1=dec[:, half * 4:(half + 1) * 4, :], op=ALU.mult), reads=[Rbk, Rk], writes=[RMr])
            OBK = Rot(banks[0:2])

            def ret_head(h):
                hs = slice(h * 512, (h + 1) * 512)
                qx, Rqx = qxr.next()
                P.op("pool", lambda e, qx=qx, h=h: e.tensor_tensor(out=qx[:], in0=q_[:, h * 2:h * 2 + 2, :].unsqueeze(1).to_broadcast([128, 2, 2, 128]),
                                                                   in1=xi[:, :, h, :].unsqueeze(2).to_broadcast([128, 2, 2, 128]), op=ALU.mult), reads=[Rq, Rk], writes=[Rqx])
                sz_, Rsz = szr.next()
                P.op("act", lambda e, sz_=sz_, hs=hs: e.activation(out=sz_[:], in_=gr_[:, hs], func=AF.Silu), reads=[Rgr], writes=[Rsz])
                rfb, Rrfb = r_shadow(h)
                rbi, Rrbi = rbin.next()
                P.dma("sp", lambda e, rbi=rbi, h=h: e.dma_start(out=rbi[:].rearrange("p a b -> p (a b)"), in_=SBR[c, :, h * 1024:(h + 1) * 1024]), writes=[Rrbi])
                obk, Robk = OBK.next()
                mm(obk[:], MTr[:, h, :], v_[:, hs], True, False, [RMr, Rv], [Robk], False)
                for dt_ in range(2):
                    mm(obk[:], qx[:, 0, dt_, :], rfb[:, dt_, :], False, False, [Rqx, Rrfb], [Robk], False)
                for dt_ in range(2):
                    mm(obk[:], qx[:, 1, dt_, :], rbi[:, dt_, :], False, dt_ == 1, [Rqx, Rrbi], [Robk], dt_ == 1)
                return dict(hs=hs, sz_=sz_, Rsz=Rsz, obk=obk, Robk=Robk)

            def ret_tail(h, cx):
                hs = cx['hs']; sz_ = cx['sz_']; Rsz = cx['Rsz']; obk = cx['obk']; Robk = cx['Robk']
                P.op("act", lambda e, obk=obk, h=h: e.activation(out=junk[:], in_=obk[:], func=AF.Square, accum_out=ss[:, 8 + h:9 + h]), reads=[Robk], writes=[Rjk, Rss])
                P.op("act", lambda e, h=h: e.activation(out=ss[:, 8 + h:9 + h], in_=ss[:, 8 + h:9 + h], func=AF.Ln, bias=epsc[:, 0:1], scale=1.0 / 512), reads=[Rss], writes=[Rss])
                P.op("act", lambda e, h=h: e.activation(out=ss[:, 8 + h:9 + h], in_=ss[:, 8 + h:9 + h], func=AF.Exp, scale=-0.5), reads=[Rss], writes=[Rss])
                P.op("dve", lambda e, obk=obk, h=h, hs=hs, sz_=sz_: e.scalar_tensor_tensor(out=yab[:, hs], in0=obk[:], scalar=ss[:, 8 + h:9 + h], in1=sz_[:],
                                                                                  op0=ALU.mult, op1=ALU.mult), reads=[Robk, Rss, Rsz], writes=[Ryab])

            cxs = {}
            for h in range(9):
                if h < 8:
                    cxs[h] = ret_head(h)
                if h >= 1:
                    ret_tail(h - 1, cxs.pop(h - 1))
            for t8 in range(4):
                bk, Rbk = PB.next()
                bkb = bk[:].bitcast(BF16)
                for t in range(8):
                    f = t8 * 8 + t
                    tr(bkb[:, t * 128:(t + 1) * 128], yab[:, f * 128:(f + 1) * 128], ident_b[:], [Ryab, Rc], [Rbk], t == 7)
                st, Rst = stg.next()
                evac(st[:], bkb.rearrange("p (t n) -> p t n", t=8), [Rbk], [Rst])
                P.dma("pool", lambda e, st=st, t8=t8: e.dma_start(
                    out=OT[t8 * 1024:(t8 + 1) * 1024, t0:t0 + 128].rearrange("(t p) n -> p t n", p=128), in_=st[:]), reads=[Rst])
            state_update(0, xs_, Rxs, b_, Rb, v_, Rv, k_, Rkt)
        P.barrier()
        P.pop()

    def phaseC():
        P.push()
        NTC = NT // TC
        ain = Rot([(P.sbuf("ain", [128, 32, TC], BF16), Res()) for _ in range(1)])
        wbig = Rot([(P.sbuf("wbig", [128, 32, 512], BF16), Res()) for _ in range(2)])
        gte = Rot([(P.sbuf("gte", [128, TC], BF16), Res()) for _ in range(3)])
        sg = Rot([(P.sbuf("sgC", [128, TC], F32), Res()) for _ in range(2)])
        m32 = P.sbuf("m32", [128, KD, TC], F32); Rm32 = Res()
        mT = P.sbuf("mT", [128, KD, TC], BF16); RmT = Res()
        hT = P.sbuf("hT", [128, KD, TC], F32); RhT = Res()
        n2, Rn2 = mT, RmT
        xr = Rot([(P.sbuf("xr", [128, TC], F32), Res()) for _ in range(2)])
        sq = Rot([(P.sbuf("sqC", [128, TC], F32), Res()) for _ in range(2)])
        rstd = P.sbuf("rstdC", [128, TC], F32); Rrs = Res()
        tmp = Rot([(P.sbuf("tmpC", [128, TC], F32), Res()) for _ in range(2)])
        stg = Rot([(P.sbuf("stgC", [128, TC], BF16), Res()) for _ in range(4)])
        for tt in range(NTC):
            t0 = tt * TC
            s = t0 // SEGLEN
            for br in range(2):
                src = YAT if br == 0 else OT
                W = WPA if br == 0 else WPB
                a_, Ra = ain.next()
                for hlf in range(2):
                    P.dma("sp", lambda e, a_=a_, src=src, hlf=hlf: e.dma_start(
                        out=a_[:, hlf * 16:(hlf + 1) * 16, :], in_=src[hlf * 2048:(hlf + 1) * 2048, t0:t0 + TC].rearrange("(k p) n -> p k n", p=128)), writes=[Ra])
                for blk in range(4):
                    w, Rw = wbig.next()
                    for hlf in range(2):
                        P.dma("sp", lambda e, w=w, W=W, blk=blk, hlf=hlf: e.dma_start(
                            out=w[:, hlf * 16:(hlf + 1) * 16, :], in_=W[hlf * 2048:(hlf + 1) * 2048, blk * 512:(blk + 1) * 512].rearrange("(k p) n -> p k n", p=128)), writes=[Rw])
                    for j in range(4):
                        dtile = blk * 4 + j
                        g_, Rg = gte.next()
                        P.dma("sp", lambda e, g_=g_, r0=br * 2048 + dtile * 128: e.dma_start(out=g_[:], in_=GATES[r0:r0 + 128, t0:t0 + TC]), writes=[Rg])
                        s_, Rs = sg.next()
                        P.op("act", lambda e, s_=s_, g_=g_: e.activation(out=s_[:], in_=g_[:], func=AF.Sigmoid), reads=[Rg], writes=[Rs])
                        bk, Rbk = PB.next()
                        for k in range(32):
                            mm(bk[:, 0:TC], w[:, k, j * 128:(j + 1) * 128], a_[:, k, :], k == 0, k == 31, [Rw, Ra], [Rbk], k == 31)
                        if br == 0:
                            P.op("dve", lambda e, bk=bk, s_=s_, dtile=dtile: e.tensor_tensor(out=m32[:, dtile, :], in0=bk[:, 0:TC], in1=s_[:], op=ALU.mult), reads=[Rbk, Rs], writes=[Rm32])
                        else:
                            tp, Rtp = tmp.next()
                            P.op("dve", lambda e, bk=bk, s_=s_, tp=tp: e.tensor_tensor(out=tp[:], in0=bk[:, 0:TC], in1=s_[:], op=ALU.mult), reads=[Rbk, Rs], writes=[Rtp])
                            P.op("pool", lambda e, tp=tp, dtile=dtile: e.tensor_tensor(out=mT[:, dtile, :], in0=tp[:], in1=m32[:, dtile, :], op=ALU.add), reads=[Rtp, Rm32], writes=[RmT])
            bss, Rbss = ACCB
            for blk in range(4):
                w, Rw = wbig.next()
                P.dma("sp", lambda e, w=w, blk=blk: e.dma_start(out=w[:, 0:16, :], in_=WOUT[:, blk * 512:(blk + 1) * 512].rearrange("(k p) n -> p k n", p=128)), writes=[Rw])
                for j in range(4):
                    dtile = blk * 4 + j
                    x_, Rx = xr.next()
                    P.dma("sp", lambda e, x_=x_, dtile=dtile: e.dma_start(out=x_[:], in_=XT[dtile * 128:(dtile + 1) * 128, t0:t0 + TC]), writes=[Rx])
                    bk, Rbk = PB.next()
                    for k in range(KD):
                        mm(bk[:, 0:TC], w[:, k, j * 128:(j + 1) * 128], mT[:, k, :], k == 0, k == KD - 1, [Rw, RmT], [Rbk], k == KD - 1)
                    P.op("dve", lambda e, bk=bk, x_=x_, dtile=dtile: e.scalar_tensor_tensor(
                        out=hT[:, dtile, :], in0=bk[:, 0:TC], scalar=modT[:, 32 + dtile, s:s + 1], in1=x_[:], op0=ALU.mult, op1=ALU.add),
                        reads=[Rbk, Rx, Rmod], writes=[RhT])
                    P.dma("pool", lambda e, dtile=dtile: e.dma_start(out=H[dtile * 128:(dtile + 1) * 128, t0:t0 + TC], in_=hT[:, dtile, :]), reads=[RhT])
                    q_, Rq = sq.next()
                    P.op("act", lambda e, q_=q_, dtile=dtile: e.activation(out=q_[:], in_=hT[:, dtile, :], func=AF.Square), reads=[RhT], writes=[Rq])
                    mm(bss[:, 0:TC], ones_f[:], q_[:], dtile == 0, dtile == KD - 1, [Rq, Rc], [Rbss], True)
            P.op("act", lambda e, bss=bss: e.activation(out=rstd[:], in_=bss[:, 0:TC], func=AF.Ln, bias=epsc[:, 0:1], scale=1.0 / D), reads=[Rbss], writes=[Rrs])
            P.op("act", lambda e, bss=bss: e.activation(out=rstd[:], in_=rstd[:], func=AF.Exp, scale=-0.5), reads=[Rrs], writes=[Rrs])
            for k in range(KD):
                tp, Rtp = tmp.next()
                P.op("dve", lambda e, tp=tp, k=k: e.scalar_tensor_tensor(out=tp[:], in0=hT[:, k, :], scalar=scale2[:, k, s:s + 1], in1=rstd[:], op0=ALU.mult, op1=ALU.mult),
                     reads=[RhT, Rrs, Rmod], writes=[Rtp])
                P.op("act", lambda e, tp=tp, k=k: e.activation(out=n2[:, k, :], in_=tp[:], func=AF.Identity, bias=modT[:, 48 + k, s:s + 1], scale=1.0),
                     reads=[Rtp, Rmod], writes=[Rn2])
            for blk in range(22):
                w, Rw = wbig.next()
                P.dma("sp", lambda e, w=w, blk=blk: e.dma_start(out=w[:, 0:16, :], in_=WUP[:, blk * 512:(blk + 1) * 512].rearrange("(k p) n -> p k n", p=128)), writes=[Rw])
                for j in range(4):
                    ft = blk * 4 + j
                    bk, Rbk = PB.next()
                    for k in range(KD):
                        mm(bk[:, 0:TC], w[:, k, j * 128:(j + 1) * 128], n2[:, k, :], k == 0, k == KD - 1, [Rw, Rn2], [Rbk], k == KD - 1)
                    st, Rst = stg.next()
                    evac(st[:], bk[:, 0:TC], [Rbk], [Rst])
                    dst = AT if ft < NF else GT
                    r0 = (ft % NF) * 128
                    P.dma("pool", lambda e, st=st, dst=dst, r0=r0: e.dma_start(out=dst[r0:r0 + 128, t0:t0 + TC], in_=st[:]), reads=[Rst])
        P.barrier()
        P.pop()

    def phaseD():
        P.push()
        NTC = NT // TC
        fw = P.sbuf("fw", [128, NF, 3], F32)
        fb = P.sbuf("fb", [128, NF], F32)
        Rfw = Res()
        P.dma("sp", lambda e: e.dma_start(out=fw[:], in_=fconv_wT), writes=[Rfw])
        P.dma("sp", lambda e: e.dma_start(out=fb[:], in_=fconv_bT), writes=[Rfw])
        ab = Rot([(P.sbuf("abD", [128, TC + 4], BF16), Res()) for _ in range(4)])
        gb = Rot([(P.sbuf("gbD", [128, TC], BF16), Res()) for _ in range(5)])
        acc = Rot([(P.sbuf("accD", [128, TC], F32), Res()) for _ in range(3)])
        sl_ = Rot([(P.sbuf("slD", [128, TC], BF16), Res()) for _ in range(4)])
        uTr = Rot([(P.sbuf("uT", [128, NF, TC], BF16), Res()) for _ in range(2)])
        wd = Rot([(P.sbuf("wd", [128, NF, 256], BF16), Res()) for _ in range(2)])
        hr = Rot([(P.sbuf("hr", [128, TC], F32), Res()) for _ in range(2)])
        h2 = P.sbuf("h2", [128, KD, TC], F32); Rh2 = Res()
        sq = Rot([(P.sbuf("sqD", [128, TC], F32), Res()) for _ in range(2)])
        rstd = P.sbuf("rstdD", [128, TC], F32); Rrs = Res()
        yo = Rot([(P.sbuf("yo", [128, D], F32), Res()) for _ in range(1)])
        def conv(tt, uT, RuT):
            t0 = tt * TC
            lo = max(t0 - 1, 0)
            hi = min(t0 + TC + 1, NT)
            prev = None
            for f in range(NF + 1):
                if f < NF:
                    cur = conv_s1(t0, lo, hi, f)
                if prev is not None:
                    conv_s2(prev, uT, RuT)
                prev = cur if f < NF else None
                yield

        def conv_s1(t0, lo, hi, f):
            if True:
                a_, Ra = ab.next()
                g_, Rg = gb.next()
                if lo > t0 - 1:
                    P.op("pool", lambda e, a_=a_: e.memset(a_[:, 0:2], 0.0), writes=[Ra])
                if hi < t0 + TC + 1:
                    P.op("pool", lambda e, a_=a_: e.memset(a_[:, TC + 2:TC + 4], 0.0), writes=[Ra])
                P.dma("sp", lambda e, a_=a_, f=f: e.dma_start(out=a_[:, lo - t0 + 2:hi - t0 + 2], in_=AT[f * 128:(f + 1) * 128, lo:hi]), writes=[Ra])
                P.dma("sp", lambda e, g_=g_, f=f: e.dma_start(out=g_[:], in_=GT[f * 128:(f + 1) * 128, t0:t0 + TC]), writes=[Rg])
                if t0 % SEGLEN == 0 and t0 > 0:
                    P.op("act", lambda e, a_=a_: e.activation(out=a_[:, 0:2], in_=a_[:, 0:2], func=AF.Copy, scale=cont[:, 0:1]), reads=[Ra, Rc], writes=[Ra])
                if (t0 + TC) % SEGLEN == 0 and t0 + TC < NT:
                    P.op("act", lambda e, a_=a_: e.activation(out=a_[:, TC + 2:TC + 4], in_=a_[:, TC + 2:TC + 4], func=AF.Copy, scale=cont[:, 0:1]), reads=[Ra, Rc], writes=[Ra])
                c_, Rcc = acc.next()
                P.op("act", lambda e, c_=c_, a_=a_, f=f: e.activation(out=c_[:], in_=a_[:, 1:TC + 1], func=AF.Identity, bias=fb[:, f:f + 1], scale=fw[:, f, 0:1]), reads=[Ra, Rfw], writes=[Rcc])
                for k in range(1, 3):
                    P.op("dve", lambda e, c_=c_, a_=a_, f=f, k=k: e.scalar_tensor_tensor(out=c_[:], in0=a_[:, k + 1:k + 1 + TC], scalar=fw[:, f, k:k + 1], in1=c_[:], op0=ALU.mult, op1=ALU.add),
                         reads=[Ra, Rfw, Rcc], writes=[Rcc])
            return (f, c_, Rcc, g_, Rg)

        def conv_s2(st, uT, RuT):
            f, c_, Rcc, g_, Rg = st
            s_, Rs = sl_.next()
            P.op("act", lambda e, s_=s_, c_=c_: e.activation(out=s_[:], in_=c_[:], func=AF.Silu), reads=[Rcc], writes=[Rs])
            P.op("pool", lambda e, s_=s_, g_=g_, f=f: e.tensor_tensor(out=uT[:, f, :], in0=s_[:], in1=g_[:], op=ALU.mult), reads=[Rs, Rg], writes=[RuT])

        def rest(tt, uT, RuT, gen):
            t0 = tt * TC
            s = t0 // SEGLEN
            bss, Rbss = ACCB
            for blk in range(8):
                if gen is not None:
                    for _ in range(6):
                        next(gen, None)
                w, Rw = wd.next()
                for (f0, f1) in ((0, 22), (22, 44)):
                    P.dma("sp", lambda e, w=w, blk=blk, f0=f0, f1=f1: e.dma_start(
                        out=w[:, f0:f1, :], in_=WDN[f0 * 128:f1 * 128, blk * 256:(blk + 1) * 256].rearrange("(k p) n -> p k n", p=128)), writes=[Rw])
                for j in range(2):
                    dtile = blk * 2 + j
                    h_, Rh = hr.next()
                    P.dma("sp", lambda e, h_=h_, dtile=dtile: e.dma_start(out=h_[:], in_=H[dtile * 128:(dtile + 1) * 128, t0:t0 + TC]), writes=[Rh])
                    bk, Rbk = PB.next()
                    for k in range(NF):
                        mm(bk[:, 0:TC], w[:, k, j * 128:(j + 1) * 128], uT[:, k, :], k == 0, k == NF - 1, [Rw, RuT], [Rbk], k == NF - 1)
                    P.op("dve", lambda e, bk=bk, h_=h_, dtile=dtile: e.scalar_tensor_tensor(
                        out=h2[:, dtile, :], in0=bk[:, 0:TC], scalar=modT[:, 80 + dtile, s:s + 1], in1=h_[:], op0=ALU.mult, op1=ALU.add),
                        reads=[Rbk, Rh, Rmod], writes=[Rh2])
                    q_, Rq = sq.next()
                    P.op("act", lambda e, q_=q_, dtile=dtile: e.activation(out=q_[:], in_=h2[:, dtile, :], func=AF.Square), reads=[Rh2], writes=[Rq])
                    mm(bss[:, 0:TC], ones_f[:], q_[:], dtile == 0, dtile == KD - 1, [Rq, Rc], [Rbss], True)
            P.op("act", lambda e, bss=bss: e.activation(out=rstd[:], in_=bss[:, 0:TC], func=AF.Ln, bias=epsc[:, 0:1], scale=1.0 / D), reads=[Rbss], writes=[Rrs])
            P.op("act", lambda e, bss=bss: e.activation(out=rstd[:], in_=rstd[:], func=AF.Exp, scale=-0.5), reads=[Rrs], writes=[Rrs])
            for k in range(KD):
                P.op("dve", lambda e, k=k: e.scalar_tensor_tensor(out=h2[:, k, :], in0=h2[:, k, :], scalar=nfT[:, k:k + 1], in1=rstd[:], op0=ALU.mult, op1=ALU.mult),
                     reads=[Rh2, Rrs, Rmod], writes=[Rh2])
            for sb in range(TC // 128):
                y_, Ry = yo.next()
                for k4 in range(4):
                    bk, Rbk = PB.next()
                    for kk in range(4):
                        k = k4 * 4 + kk
                        tr(bk[:, kk * 128:(kk + 1) * 128], h2[:, k, sb * 128:(sb + 1) * 128], ident_f[:], [Rh2, Rc], [Rbk], kk == 3)
                    evac(y_[:, k4 * 512:(k4 + 1) * 512], bk[:], [Rbk], [Ry])
                r0 = t0 + sb * 128
                P.dma("pool", lambda e, y_=y_, r0=r0: e.dma_start(out=y_out[r0:r0 + 128, :], in_=y_[:]), reads=[Ry])

        cur_u = uTr.next()
        for _ in conv(0, *cur_u):
            pass
        for tt in range(NTC):
            nxt_u = uTr.next() if tt + 1 < NTC else None
            gen = conv(tt + 1, *nxt_u) if nxt_u is not None else None
            rest(tt, cur_u[0], cur_u[1], gen)
            cur_u = nxt_u
        P.barrier()
        P.pop()

    for i, ph in enumerate((phase0, phaseA, phaseA2, scans, phaseC, phaseD)):
        if i <= upto:
            ph()
    P.finish()
    return nc


NSEG_FULL = 4
SEGLEN_FULL = 2048


def rope_tables(pos):
    half = 128
    inv = 10000.0 ** (-np.arange(half, dtype=np.float32) / half)
    ang = pos.astype(np.float32)[None, :] * inv[:, None].astype(np.float32)
    return np.cos(ang).astype(np.float32), np.sin(ang).astype(np.float32)


def tileT(v, ntile):
    return np.ascontiguousarray(np.asarray(v, np.float32).reshape(ntile, 128).T)


def shared_inputs(w):
    dec_c, xi_c, zeta_c, _, _ = ret_consts()
    sq = lambda a: np.ascontiguousarray(np.asarray(a, np.float32)[0])
    conv_w = sq(w["conv_w"])
    fcw = sq(w["ffn_conv_w"])
    return {
        "dec": dec_c, "xi": xi_c, "zeta": zeta_c,
        "w_ada": sq(w["w_ada"]), "b_adaT": tileT(sq(w["b_ada"]), 96), "norm1T": tileT(sq(w["norm1"]), 16),
        "w_in": sq(w["w_in"]),
        "conv_wT": np.ascontiguousarray(conv_w.reshape(5, 48, 128).transpose(2, 1, 0)),
        "conv_bT": tileT(sq(w["conv_b"]), 48),
        "dt_bias": sq(w["dt_bias"]).reshape(1, 128), "a_log": sq(w["a_log"]).reshape(1, 128),
        "ssd_d": sq(w["ssd_d"]).reshape(1, 64), "ssd_normT": tileT(sq(w["ssd_norm"]), 32),
        "w_pa": sq(w["w_pa"]), "w_pb": sq(w["w_pb"]), "w_out": sq(w["w_out"]), "norm2T": tileT(sq(w["norm2"]), 16),
        "w_up": sq(w["w_up"]),
        "fconv_wT": np.ascontiguousarray(fcw.reshape(3, NF, 128).transpose(2, 1, 0)),
        "fconv_bT": tileT(sq(w["ffn_conv_b"]), NF),
        "w_down": sq(w["w_down"]), "norm_fT": tileT(np.asarray(w["norm_f"], np.float32), 16),
    }


def core_inputs(xseg, cseg, cont, seglen, shared):
    nseg = len(xseg)
    x = np.ascontiguousarray(np.concatenate(xseg, 0), dtype=np.float32)
    c = np.stack(cseg, 0).astype(np.float32)
    cT = np.ascontiguousarray(c.reshape(nseg, KD, 128).transpose(2, 1, 0))
    if cont:
        pos = np.arange(nseg * seglen)
    else:
        pos = np.tile(np.arange(seglen), nseg)
    cosT, sinT = rope_tables(pos)
    m = dict(shared)
    m.update({"x": x, "cT": cT, "cont": np.full((128, 1), float(cont), np.float32), "cosT": cosT, "sinT": sinT})
    return m


_NC_CACHE = {}


def kernel(x_prompt, x_sample, c_prompt, c_sample, **w):
    x_prompt = np.asarray(x_prompt, np.float32)
    x_sample = np.asarray(x_sample, np.float32)
    c_prompt = np.asarray(c_prompt, np.float32)
    c_sample = np.asarray(c_sample, np.float32)
    shared = shared_inputs(w)
    NS, SL = NSEG_FULL, SEGLEN_FULL
    in_maps = []
    for b in range(4):
        xs = [x_sample[b, i * SL:(i + 1) * SL] for i in range(NS)]
        in_maps.append(core_inputs(xs, [c_sample[b]] * NS, 1.0, SL, shared))
    zx = np.zeros((SL, D), np.float32)
    zc = np.zeros((D,), np.float32)
    for i in range(4):
        xs = [x_prompt[2 * i], x_prompt[2 * i + 1], zx, zx]
        cs = [c_prompt[2 * i], c_prompt[2 * i + 1], zc, zc]
        in_maps.append(core_inputs(xs, cs, 0.0, SL, shared))
    key = (NS, SL)
    if key not in _NC_CACHE:
        _NC_CACHE[key] = build(NS, SL)
    nc = _NC_CACHE[key]
    res = run_bass_kernel_spmd(nc, in_maps, core_ids=list(range(8)))
    y_sample = np.stack([np.asarray(res.results[b]["y"], np.float32).reshape(NS * SL, D) for b in range(4)], 0)
    yp = []
    for i in range(4):
        y = np.asarray(res.results[4 + i]["y"], np.float32).reshape(NS * SL, D)
        yp.append(y[0:SL])
        yp.append(y[SL:2 * SL])
    y_prompt = np.stack(yp, 0)
    return (y_prompt, y_sample)
```

```python
import numpy as np
import concourse.bass as bass
import concourse.mybir as mybir
from concourse.bass_utils import run_bass_kernel_spmd
from contextlib import ExitStack

F32 = mybir.dt.float32
BF16 = mybir.dt.bfloat16
AF = mybir.ActivationFunctionType
ALU = mybir.AluOpType
AX = mybir.AxisListType

D = 2048
KD = 16
E_IN = 26752
DFF = 5632
NF = 44
EPS = 1e-6
ENGS = ("pe", "act", "dve", "pool", "sp")
DMAQ = {"sp": 12, "pool": 8, "act": 4}
DEBUG_LINES = []
SCAN_STOP = 99
D_STOP = 99


import types as _types


def freeze(fn):
    if fn.__closure__ is None:
        return fn
    cells = []
    for c in fn.__closure__:
        try:
            cells.append(_types.CellType(c.cell_contents))
        except ValueError:
            cells.append(c)
    return _types.FunctionType(fn.__code__, fn.__globals__, fn.__name__, fn.__defaults__, tuple(cells))


class Res:
    __slots__ = ("name", "w", "r", "x")

    def __init__(self, name="", excl=False):
        self.name = name
        self.w = None
        self.r = {}
        self.x = excl


class Prog:
    def __init__(self, nc):
        self.nc = nc
        self.es = ExitStack()
        self.scopes = []
        self.q = {e: [] for e in ENGS}
        self.sems = {}
        self.cnt = {}
        self.known = {e: {} for e in ENGS}
        for e in ENGS:
            self.sems[e] = self.es.enter_context(nc.semaphore("s_" + e))
            self.cnt[e] = 0
        self.dq = {}
        for qn, n in DMAQ.items():
            ks = []
            for i in range(n):
                k = "d_%s%d" % (qn, i)
                self.sems[k] = self.es.enter_context(nc.semaphore("s_" + k))
                self.cnt[k] = 0
                ks.append(k)
            self.dq[qn] = [ks, 0]
        self.ninst = 0
        self.uid = 0

    def push(self):
        self.scopes.append(ExitStack())

    def pop(self):
        self.scopes.pop().close()

    def _stk(self):
        return self.scopes[-1] if self.scopes else self.es

    def sbuf(self, name, shape, dt):
        self.uid += 1
        return self._stk().enter_context(self.nc.sbuf_tensor("%s_%d" % (name, self.uid), list(shape), dt))

    def psum(self, name, shape, dt):
        self.uid += 1
        return self._stk().enter_context(self.nc.psum_tensor("%s_%d" % (name, self.uid), list(shape), dt))

    def _deps(self, reads, writes, eng=None):
        deps = []
        for r in reads:
            if r.w is not None:
                deps.append(r.w)
            if r.x:
                for k, v in r.r.items():
                    if k != eng:
                        deps.append((k, v))
        for w in writes:
            if w.w is not None:
                deps.append(w.w)
            deps.extend(w.r.items())
        return deps

    def _emit_waits(self, eng, deps, skip_self_pe=False):
        need = {}
        kn = self.known[eng]
        for (k, v) in deps:
            if skip_self_pe and k == "pe" and eng == "pe":
                continue
            if kn.get(k, 0) >= v:
                continue
            if need.get(k, 0) < v:
                need[k] = v
        for k, v in need.items():
            kn[k] = v
            sem = self.sems[k]
            self.q[eng].append(lambda e, sem=sem, v=v: e.wait_ge(sem, v))

    def _mark(self, tok, reads, writes):
        k, v = tok
        for r in reads:
            if r.r.get(k, 0) < v:
                r.r[k] = v
        for w in writes:
            w.w = tok
            w.r = {}

    def op(self, eng, fn, reads=(), writes=(), inc=True):
        fn = freeze(fn)
        if DEBUG_LINES:
            import sys as _s
            fr = _s._getframe(1)
            if fr.f_code.co_name in ("mm", "tr", "evac"):
                fr = fr.f_back
            self.q[eng].append(("L", fr.f_lineno))
        deps = self._deps(reads, writes, eng)
        self._emit_waits(eng, deps, skip_self_pe=True)
        self.ninst += 1
        if inc:
            self.cnt[eng] += 1
            v = self.cnt[eng]
            sem = self.sems[eng]
            self.q[eng].append(lambda e, fn=fn, sem=sem: fn(e).then_inc(sem, 1))
            tok = (eng, v)
        else:
            self.q[eng].append(lambda e, fn=fn: fn(e))
            tok = (eng, self.cnt[eng] + 1)
        self._mark(tok, reads, writes)
        return tok

    def dma(self, qn, fn, reads=(), writes=()):
        fn = freeze(fn)
        ks, rr = self.dq[qn]
        k = ks[rr]
        self.dq[qn][1] = (rr + 1) % len(ks)
        deps = self._deps(reads, writes)
        if self.cnt[k] > 0:
            deps.append((k, self.cnt[k]))
        self._emit_waits(qn, deps)
        self.cnt[k] += 16
        v = self.cnt[k]
        sem = self.sems[k]
        self.ninst += 1
        self.q[qn].append(lambda e, fn=fn, sem=sem: fn(e).then_inc(sem, 16))
        tok = (k, v)
        self._mark(tok, reads, writes)
        return tok

    def barrier(self):
        for eng in ENGS:
            deps = [(k, v) for k, v in self.cnt.items() if v > 0 and k != eng]
            self._emit_waits(eng, deps)

    def finish(self):
        self.barrier()
        nc = self.nc
        q = self.q
        with nc.Block() as block:
            def run(e, lst, nm):
                line = None
                for f in lst:
                    if isinstance(f, tuple):
                        line = f[1]
                        continue
                    if DEBUG_LINES:
                        b = nc.next_id()
                        f(e)
                        a = nc.next_id()
                        for t in DEBUG_LINES:
                            if b <= t < a + 1:
                                print("DEBUG_INST", t, nm, "line", line, "ids", b, a)
                    else:
                        f(e)

            @block.tensor
            def _(e):
                run(e, q["pe"], "pe")

            @block.scalar
            def _(e):
                run(e, q["act"], "act")

            @block.vector
            def _(e):
                run(e, q["dve"], "dve")

            @block.gpsimd
            def _(e):
                run(e, q["pool"], "pool")

            @block.sync
            def _(e):
                run(e, q["sp"], "sp")
        while self.scopes:
            self.pop()
        self.es.close()


class Rot:
    def __init__(self, items):
        self.items = items
        self.i = 0

    def next(self):
        it = self.items[self.i]
        self.i = (self.i + 1) % len(self.items)
        return it


def in_blocks():
    blks = []
    for i in range(8):
        blks.append(("z", i * 512, 512, i))
    for i in range(12):
        blks.append(("xbc", 4096 + i * 512, 512, i))
    blks.append(("dt", 10240, 128, 0))
    for i in range(4):
        blks.append(("q", 10368 + i * 512, 512, i))
    for i in range(4):
        blks.append(("k", 12416 + i * 512, 512, i))
    for i in range(8):
        blks.append(("v", 14464 + i * 512, 512, i))
    for i in range(8):
        blks.append(("gret", 18560 + i * 512, 512, i))
    for i in range(8):
        blks.append(("gates", 22656 + i * 512, 512, i))
    return blks


def ret_consts():
    h = np.arange(8, dtype=np.float64)
    lg_f = np.log1p(-np.exp2(-5.0 - h))
    lg_b = lg_f[::-1].copy()
    idx = np.arange(128, dtype=np.float64)
    dij = idx[None, :] - idx[:, None]
    dec = np.zeros((128, 8, 128), np.float64)
    for hh in range(8):
        f = np.where(dij >= 0, np.exp(np.maximum(dij, 0) * lg_f[hh]), 0.0)
        b = np.where(dij < 0, np.exp(np.maximum(-dij, 0) * lg_b[hh]), 0.0)
        dec[:, hh, :] = f + b
    xif = np.exp((idx[None, :] + 1.0) * lg_f[:, None])
    xib = np.exp((128.0 - idx[None, :]) * lg_b[:, None])
    zf = np.exp((127.0 - idx[:, None]) * lg_f[None, :])
    zb = np.exp((idx[:, None]) * lg_b[None, :])
    gf = np.exp(128.0 * lg_f)
    gb = np.exp(128.0 * lg_b)
    xi = np.stack([xif, xib], 0)
    xi_bc = np.broadcast_to(xi[None], (128, 2, 8, 128)).astype(np.float32).copy()
    zeta = np.stack([zf, zb], 1).astype(np.float32)
    return dec.astype(np.float32), xi_bc, zeta, gf, gb


def build(NSEG, SEGLEN, dump=(), upto=99):
    NT = NSEG * SEGLEN
    NTT = NT // 512
    NCH = NT // 128
    TC = 512
    nc = bass.Bass("TRN2", target_bir_lowering=False)

    def din(name, shape, dt=F32):
        return nc.dram_tensor(name, list(shape), dt, kind="ExternalInput").ap()

    def dscr(name, shape, dt=BF16):
        kind = "ExternalOutput" if name in dump else "Internal"
        return nc.dram_tensor(name, list(shape), dt, kind=kind).ap()

    x_in = din("x", [NT, D])
    cT_in = din("cT", [128, KD, NSEG])
    cont_in = din("cont", [128, 1])
    cos_in = din("cosT", [128, NT])
    sin_in = din("sinT", [128, NT])
    dec_in = din("dec", [128, 8, 128])
    xi_in = din("xi", [128, 2, 8, 128])
    zeta_in = din("zeta", [128, 2, 8])
    w_ada = din("w_ada", [D, 6 * D])
    b_adaT = din("b_adaT", [128, 96])
    norm1T = din("norm1T", [128, KD])
    w_in = din("w_in", [D, E_IN])
    conv_wT = din("conv_wT", [128, 48, 5])
    conv_bT = din("conv_bT", [128, 48])
    dt_bias = din("dt_bias", [1, 128])
    a_log = din("a_log", [1, 128])
    ssd_d = din("ssd_d", [1, 64])
    ssd_normT = din("ssd_normT", [128, 32])
    w_pa = din("w_pa", [4096, D])
    w_pb = din("w_pb", [4096, D])
    w_out = din("w_out", [D, D])
    norm2T = din("norm2T", [128, KD])
    w_up = din("w_up", [D, 2 * DFF])
    fconv_wT = din("fconv_wT", [128, NF, 3])
    fconv_bT = din("fconv_bT", [128, NF])
    w_down = din("w_down", [DFF, D])
    norm_fT = din("norm_fT", [128, KD])
    y_out = nc.dram_tensor("y", [NT, D], F32, kind="ExternalOutput").ap()

    WIN = dscr("WIN", [D, E_IN])
    WPA = dscr("WPA", [4096, D])
    WPB = dscr("WPB", [4096, D])
    WOUT = dscr("WOUT", [D, D])
    WUP = dscr("WUP", [D, 2 * DFF])
    WDN = dscr("WDN", [DFF, D])
    XT = dscr("XT", [D, NT], F32)
    Z = dscr("Z", [NT, 4096])
    XBC = dscr("XBC", [6144, NT])
    DTR = dscr("DTR", [NT, 128], F32)
    QT = dscr("QT", [2048, NT])
    KT = dscr("KT", [2048, NT])
    V = dscr("V", [NT, 4096])
    GRET = dscr("GRET", [NT, 4096])
    GATES = dscr("GATES", [4096, NT])
    XS = dscr("XS", [NT, 4096])
    BTM = dscr("BTM", [NT, 1024])
    BT = dscr("BT", [1024, NT])
    CT = dscr("CT", [1024, NT])
    SBS = dscr("SBS", [NCH, 128, 4096])
    SBR = dscr("SBR", [NCH, 128, 8192])
    YAT = dscr("YAT", [4096, NT])
    OT = dscr("OT", [4096, NT])
    H = dscr("H", [D, NT], F32)
    AT = dscr("AT", [DFF, NT])
    GT = dscr("GT", [DFF, NT])

    P = Prog(nc)
    dec_c, xi_c, zeta_c, gch_f, gch_b = ret_consts()

    ident_f = P.sbuf("ident_f", [128, 128], F32)
    ident_b = P.sbuf("ident_b", [128, 128], BF16)
    ones_f = P.sbuf("ones_f", [128, 128], F32)
    Rc = Res("consts")
    P.op("pool", lambda e: e.memset(ident_f[:], 1.0), writes=[Rc])
    P.op("pool", lambda e: e.affine_select(out=ident_f[:], in_=ident_f[:], pattern=[[-1, 128]],
                                           compare_op=ALU.is_equal, fill=0.0, base=0, channel_multiplier=1),
         reads=[Rc], writes=[Rc])
    P.op("pool", lambda e: e.tensor_copy(out=ident_b[:], in_=ident_f[:]), reads=[Rc], writes=[Rc])
    P.op("pool", lambda e: e.memset(ones_f[:], 1.0), writes=[Rc])
    epsc = P.sbuf("epsc", [128, 1], F32)
    P.op("pool", lambda e: e.memset(epsc[:], EPS), writes=[Rc])
    cont = P.sbuf("cont", [128, 1], F32)
    P.dma("sp", lambda e: e.dma_start(out=cont[:], in_=cont_in), writes=[Rc])
    modT = P.sbuf("modT", [128, 96, NSEG], F32)
    scale1 = P.sbuf("scale1", [128, KD, NSEG], F32)
    scale2 = P.sbuf("scale2", [128, KD, NSEG], F32)
    n1T = P.sbuf("n1T_c", [128, KD], F32)
    n2T = P.sbuf("n2T_c", [128, KD], F32)
    nfT = P.sbuf("nfT_c", [128, KD], F32)
    Rmod = Res("mod")
    P.dma("sp", lambda e: e.dma_start(out=n1T[:], in_=norm1T), writes=[Rmod])
    P.dma("sp", lambda e: e.dma_start(out=n2T[:], in_=norm2T), writes=[Rmod])
    P.dma("sp", lambda e: e.dma_start(out=nfT[:], in_=norm_fT), writes=[Rmod])
    P.barrier()

    banks = []
    for i in range(8):
        t = P.psum("bank%d" % i, [128, 512], F32)
        banks.append((t, Res("bank%d" % i, excl=True)))
    PB = Rot(banks[:7])
    ACCB = banks[7]

    evac_flip = [0]

    def evac(out, in_, reads, writes):
        evac_flip[0] ^= 1
        if evac_flip[0]:
            P.op("act", lambda e: e.activation(out=out, in_=in_, func=AF.Copy), reads, writes)
        else:
            P.op("dve", lambda e: e.tensor_copy(out=out, in_=in_), reads, writes)

    def mm(out, lhsT, rhs, start, stop, reads, writes, inc):
        P.op("pe", lambda e: e.matmul(out, lhsT=lhsT, rhs=rhs, start=start, stop=stop), reads, writes, inc)

    def tr(out, in_, idn, reads, writes, inc=True):
        P.op("pe", lambda e: e.transpose(out, in_, idn), reads, writes, inc)

    def phase0():
        P.push()
        def cast(dst, src, rows, cols, cstep):
            for r in range(0, rows, 128):
                for c0 in range(0, cols, cstep):
                    c1 = min(cols, c0 + cstep)
                    P.dma("pool", lambda e, r=r, c0=c0, c1=c1: e.dma_start(out=dst[r:r + 128, c0:c1], in_=src[r:r + 128, c0:c1]))
        cast(WIN, w_in, D, E_IN, 6688)
        cast(WPB, w_pb, 4096, D, 2048)
        cast(WOUT, w_out, D, D, 2048)
        cast(WUP, w_up, D, 2 * DFF, 5632)
        cast(WDN, w_down, DFF, D, 2048)
        snT = P.sbuf("snT", [128, 32], F32)
        Rsn = Res()
        P.dma("sp", lambda e: e.dma_start(out=snT[:], in_=ssd_normT), writes=[Rsn])
        stg = [(P.sbuf("wpa_f", [128, D], F32), Res()) for _ in range(2)]
        stgb = [(P.sbuf("wpa_b", [128, D], BF16), Res()) for _ in range(2)]
        for t in range(32):
            (a, Ra), (b, Rb) = stg[t % 2], stgb[t % 2]
            P.dma("sp", lambda e, a=a, t=t: e.dma_start(out=a[:], in_=w_pa[t * 128:(t + 1) * 128, :]), writes=[Ra])
            P.op("act", lambda e, a=a, b=b, t=t: e.activation(out=b[:], in_=a[:], func=AF.Copy, scale=snT[:, t:t + 1]),
                 reads=[Ra, Rsn], writes=[Rb])
            P.dma("sp", lambda e, b=b, t=t: e.dma_start(out=WPA[t * 128:(t + 1) * 128, :], in_=b[:]), reads=[Rb])
        cT = P.sbuf("cT", [128, KD, NSEG], F32)
        sc = P.sbuf("sc", [128, KD, NSEG], F32)
        baT = P.sbuf("baT", [128, 96], F32)
        Rct, Rsc, Rba = Res(), Res(), Res()
        P.dma("sp", lambda e: e.dma_start(out=cT[:], in_=cT_in), writes=[Rct])
        P.dma("sp", lambda e: e.dma_start(out=baT[:], in_=b_adaT), writes=[Rba])
        P.op("act", lambda e: e.activation(out=sc[:], in_=cT[:], func=AF.Silu), reads=[Rct], writes=[Rsc])
        wa = [(P.sbuf("wada", [128, KD, 512], F32), Res()) for _ in range(2)]
        for blk in range(24):
            w, Rw = wa[blk % 2]
            P.dma("sp", lambda e, w=w, blk=blk: e.dma_start(
                out=w[:], in_=w_ada[:, blk * 512:(blk + 1) * 512].rearrange("(k p) n -> p k n", p=128)), writes=[Rw])
            bk, Rbk = PB.next()
            for j in range(4):
                for k in range(KD):
                    mm(bk[:, j * NSEG:(j + 1) * NSEG], w[:, k, j * 128:(j + 1) * 128], sc[:, k, :],
                       k == 0, k == KD - 1, [Rw, Rsc], [Rbk], k == KD - 1)
            for j in range(4):
                jj = blk * 4 + j
                P.op("dve", lambda e, bk=bk, j=j, jj=jj: e.tensor_scalar(
                    out=modT[:, jj, :], in0=bk[:, j * NSEG:(j + 1) * NSEG], scalar1=baT[:, jj:jj + 1], scalar2=None,
                    op0=ALU.add), reads=[Rbk, Rba], writes=[Rmod])
        for (dst, nT, off) in ((scale1, n1T, 16), (scale2, n2T, 64)):
            for s in range(NSEG):
                P.op("dve", lambda e, dst=dst, nT=nT, off=off, s=s: e.scalar_tensor_tensor(
                    out=dst[:, :, s], in0=modT[:, off:off + 16, s], scalar=1.0, in1=nT[:, :],
                    op0=ALU.add, op1=ALU.mult), reads=[Rmod], writes=[Rmod])
        P.barrier()
        P.pop()

    def phaseA():
        P.push()
        xin = [(P.sbuf("xin", [128, D], F32), Res()) for _ in range(2)]
        xT = P.sbuf("xT", [128, KD, 512], F32)
        RxT = Res()
        nT = P.sbuf("nT", [128, KD, 512], BF16)
        RnT = Res()
        sq = [(P.sbuf("sq", [128, 512], F32), Res()) for _ in range(2)]
        rstd = P.sbuf("rstd", [128, 512], F32)
        Rrstd = Res()
        tmp = [(P.sbuf("tmpA", [128, 512], F32), Res()) for _ in range(2)]
        cs = P.sbuf("cosA", [128, 512], F32)
        sn = P.sbuf("sinA", [128, 512], F32)
        Rcs = Res()
        wb = [(P.sbuf("wblk", [128, KD, 512], BF16), Res()) for _ in range(3)]
        WB = Rot(wb)
        stg = Rot([(P.sbuf("stgA", [128, 512], BF16), Res()) for _ in range(6)])
        stgf = Rot([(P.sbuf("stgAf", [128, 128], F32), Res()) for _ in range(2)])
        rt = [(P.sbuf("ropeT", [128, 512], F32), Res()) for _ in range(4)]
        blks = in_blocks()
        for tt in range(NTT):
            t0 = tt * 512
            s = t0 // SEGLEN
            P.dma("sp", lambda e, t0=t0: e.dma_start(out=cs[:], in_=cos_in[:, t0:t0 + 512]), writes=[Rcs])
            P.dma("sp", lambda e, t0=t0: e.dma_start(out=sn[:], in_=sin_in[:, t0:t0 + 512]), writes=[Rcs])
            for sb in range(4):
                xi_, Rxi = xin[sb % 2]
                P.dma("sp", lambda e, xi_=xi_, r0=t0 + sb * 128: e.dma_start(out=xi_[:], in_=x_in[r0:r0 + 128, :]), writes=[Rxi])
                for k4 in range(4):
                    bk, Rbk = PB.next()
                    for kk in range(4):
                        k = k4 * 4 + kk
                        tr(bk[:, kk * 128:(kk + 1) * 128], xi_[:, k * 128:(k + 1) * 128], ident_f[:], [Rxi, Rc], [Rbk], kk == 3)
                    evac(xT[:, k4 * 4:(k4 + 1) * 4, sb * 128:(sb + 1) * 128],
                         bk[:].rearrange("p (k n) -> p k n", k=4), [Rbk], [RxT])
            for k in range(KD):
                P.dma("pool", lambda e, k=k, t0=t0: e.dma_start(out=XT[k * 128:(k + 1) * 128, t0:t0 + 512], in_=xT[:, k, :]), reads=[RxT])
            bss, Rbss = ACCB
            for k in range(KD):
                q_, Rq = sq[k % 2]
                P.op("act", lambda e, q_=q_, k=k: e.activation(out=q_[:], in_=xT[:, k, :], func=AF.Square), reads=[RxT], writes=[Rq])
                mm(bss[:], ones_f[:], q_[:], k == 0, k == KD - 1, [Rq, Rc], [Rbss], True)
            P.op("act", lambda e, bss=bss: e.activation(out=rstd[:], in_=bss[:], func=AF.Ln, bias=epsc[:, 0:1], scale=1.0 / D), reads=[Rbss], writes=[Rrstd])
            P.op("act", lambda e, bss=bss: e.activation(out=rstd[:], in_=rstd[:], func=AF.Exp, scale=-0.5), reads=[Rrstd], writes=[Rrstd])
            for k in range(KD):
                tp, Rtp = tmp[k % 2]
                P.op("dve", lambda e, tp=tp, k=k, s=s: e.scalar_tensor_tensor(
                    out=tp[:], in0=xT[:, k, :], scalar=scale1[:, k, s:s + 1], in1=rstd[:], op0=ALU.mult, op1=ALU.mult),
                    reads=[RxT, Rrstd, Rmod], writes=[Rtp])
                P.op("act", lambda e, tp=tp, k=k, s=s: e.activation(
                    out=nT[:, k, :], in_=tp[:], func=AF.Identity, bias=modT[:, k, s:s + 1], scale=1.0),
                    reads=[Rtp, Rmod], writes=[RnT])
            for (kind, e0, ncol, bi) in blks:
                w, Rw = WB.next()
                P.dma("sp", lambda e, w=w, e0=e0, ncol=ncol: e.dma_start(
                    out=w[:, :, 0:ncol], in_=WIN[:, e0:e0 + ncol].rearrange("(k p) n -> p k n", p=128)), writes=[Rw])
                if kind in ("z", "v", "gret", "dt"):
                    dst = {"z": Z, "v": V, "gret": GRET, "dt": DTR}[kind]
                    for sb in range(4):
                        bk, Rbk = PB.next()
                        for k in range(KD):
                            mm(bk[:, 0:ncol], nT[:, k, sb * 128:(sb + 1) * 128], w[:, k, 0:ncol], k == 0, k == KD - 1,
                               [RnT, Rw], [Rbk], k == KD - 1)
                        r0 = t0 + sb * 128
                        if kind == "dt":
                            st, Rst = stgf.next()
                            evac(st[:], bk[:, 0:128], [Rbk], [Rst])
                            P.dma("pool", lambda e, st=st, r0=r0: e.dma_start(out=DTR[r0:r0 + 128, :], in_=st[:]), reads=[Rst])
                        else:
                            st, Rst = stg.next()
                            evac(st[:], bk[:], [Rbk], [Rst])
                            P.dma("pool", lambda e, st=st, r0=r0, dst=dst, c0=bi * 512: e.dma_start(
                                out=dst[r0:r0 + 128, c0:c0 + 512], in_=st[:]), reads=[Rst])
                elif kind in ("xbc", "gates"):
                    dst = XBC if kind == "xbc" else GATES
                    for j in range(4):
                        bk, Rbk = PB.next()
                        for k in range(KD):
                            mm(bk[:], w[:, k, j * 128:(j + 1) * 128], nT[:, k, :], k == 0, k == KD - 1,
                               [RnT, Rw], [Rbk], k == KD - 1)
                        st, Rst = stg.next()
                        evac(st[:], bk[:], [Rbk], [Rst])
                        r0 = bi * 512 + j * 128
                        P.dma("pool", lambda e, st=st, r0=r0, dst=dst, t0=t0: e.dma_start(
                            out=dst[r0:r0 + 128, t0:t0 + 512], in_=st[:]), reads=[Rst])
                else:
                    dst = QT if kind == "q" else KT
                    sc_ = 1.0 if kind == "q" else 0.0625
                    for hh in range(2):
                        pr = []
                        for j in range(2):
                            bk, Rbk = PB.next()
                            jj = hh * 2 + j
                            for k in range(KD):
                                mm(bk[:], w[:, k, jj * 128:(jj + 1) * 128], nT[:, k, :], k == 0, k == KD - 1,
                                   [RnT, Rw], [Rbk], k == KD - 1)
                            pr.append((bk, Rbk))
                        (b1, R1), (b2, R2) = pr
                        (ta, Ra), (tb, Rb), (tc_, Rcc), (td, Rd) = rt
                        P.op("dve", lambda e, b1=b1, ta=ta: e.scalar_tensor_tensor(out=ta[:], in0=b1[:], scalar=sc_, in1=cs[:], op0=ALU.mult, op1=ALU.mult), reads=[R1, Rcs], writes=[Ra])
                        P.op("dve", lambda e, b2=b2, tb=tb: e.scalar_tensor_tensor(out=tb[:], in0=b2[:], scalar=sc_, in1=sn[:], op0=ALU.mult, op1=ALU.mult), reads=[R2, Rcs], writes=[Rb])
                        P.op("dve", lambda e, b1=b1, tc_=tc_: e.scalar_tensor_tensor(out=tc_[:], in0=b1[:], scalar=sc_, in1=sn[:], op0=ALU.mult, op1=ALU.mult), reads=[R1, Rcs], writes=[Rcc])
                        P.op("dve", lambda e, b2=b2, td=td: e.scalar_tensor_tensor(out=td[:], in0=b2[:], scalar=sc_, in1=cs[:], op0=ALU.mult, op1=ALU.mult), reads=[R2, Rcs], writes=[Rd])
                        s1, Rs1 = stg.next()
                        P.op("pool", lambda e, s1=s1, ta=ta, tb=tb: e.tensor_tensor(out=s1[:], in0=ta[:], in1=tb[:], op=ALU.subtract), reads=[Ra, Rb], writes=[Rs1])
                        s2, Rs2 = stg.next()
                        P.op("pool", lambda e, s2=s2, tc_=tc_, td=td: e.tensor_tensor(out=s2[:], in0=tc_[:], in1=td[:], op=ALU.add), reads=[Rcc, Rd], writes=[Rs2])
                        r0 = bi * 512 + hh * 256
                        P.dma("pool", lambda e, s1=s1, r0=r0, dst=dst, t0=t0: e.dma_start(out=dst[r0:r0 + 128, t0:t0 + 512], in_=s1[:]), reads=[Rs1])
                        P.dma("pool", lambda e, s2=s2, r0=r0, dst=dst, t0=t0: e.dma_start(out=dst[r0 + 128:r0 + 256, t0:t0 + 512], in_=s2[:]), reads=[Rs2])
        P.barrier()
        P.pop()

    def phaseA2():
        P.push()
        cw = P.sbuf("cw", [128, 48, 5], F32)
        cb = P.sbuf("cb", [128, 48], F32)
        Rcw = Res()
        P.dma("sp", lambda e: e.dma_start(out=cw[:], in_=conv_wT), writes=[Rcw])
        P.dma("sp", lambda e: e.dma_start(out=cb[:], in_=conv_bT), writes=[Rcw])
        xb = Rot([(P.sbuf("xbA2", [128, 516], BF16), Res()) for _ in range(6)])
        acc = Rot([(P.sbuf("accA2", [128, 512], F32), Res()) for _ in range(4)])
        grp = Rot([(P.sbuf("grpA2", [128, 8, 512], BF16), Res()) for _ in range(2)])
        stg = Rot([(P.sbuf("stgA2", [128, 1024], BF16), Res()) for _ in range(3)])
        def a2_s1(g, f8, t0, lo, hi):
            f = g * 8 + f8
            x_, Rx = xb.next()
            if lo > t0 - 2:
                P.op("pool", lambda e, x_=x_: e.memset(x_[:, 0:2], 0.0), writes=[Rx])
            if hi < t0 + 514:
                P.op("pool", lambda e, x_=x_: e.memset(x_[:, 514:516], 0.0), writes=[Rx])
            P.dma("sp", lambda e, x_=x_, f=f, lo=lo, hi=hi, t0=t0: e.dma_start(
                out=x_[:, lo - (t0 - 2):hi - (t0 - 2)], in_=XBC[f * 128:(f + 1) * 128, lo:hi]), writes=[Rx])
            if t0 % SEGLEN == 0 and t0 > 0:
                P.op("act", lambda e, x_=x_: e.activation(out=x_[:, 0:2], in_=x_[:, 0:2], func=AF.Copy, scale=cont[:, 0:1]), reads=[Rx, Rc], writes=[Rx])
            if (t0 + 512) % SEGLEN == 0 and t0 + 512 < NT:
                P.op("act", lambda e, x_=x_: e.activation(out=x_[:, 514:516], in_=x_[:, 514:516], func=AF.Copy, scale=cont[:, 0:1]), reads=[Rx, Rc], writes=[Rx])
            a_, Ra = acc.next()
            P.op("act", lambda e, a_=a_, x_=x_, f=f: e.activation(out=a_[:], in_=x_[:, 0:512], func=AF.Identity,
                                                               bias=cb[:, f:f + 1], scale=cw[:, f, 0:1]), reads=[Rx, Rcw], writes=[Ra])
            for k in range(1, 5):
                P.op("dve", lambda e, a_=a_, x_=x_, f=f, k=k: e.scalar_tensor_tensor(
                    out=a_[:], in0=x_[:, k:k + 512], scalar=cw[:, f, k:k + 1], in1=a_[:], op0=ALU.mult, op1=ALU.add),
                    reads=[Rx, Rcw, Ra], writes=[Ra])
            return (f8, a_, Ra)

        for tt in range(NTT):
            t0 = tt * 512
            lo = max(t0 - 2, 0)
            hi = min(t0 + 514, NT)
            for g in range(6):
                gt, Rgt = grp.next()
                prev = None
                for f8 in range(9):
                    if f8 < 8:
                        cur = a2_s1(g, f8, t0, lo, hi)
                    if prev is not None:
                        pf8, pa_, pRa = prev
                        P.op("act", lambda e, pa_=pa_, gt=gt, pf8=pf8: e.activation(out=gt[:, pf8, :], in_=pa_[:], func=AF.Silu), reads=[pRa], writes=[Rgt])
                    prev = cur if f8 < 8 else None
                if g >= 4:
                    dstT = BT if g == 4 else CT
                    for f8 in range(8):
                        P.dma("pool", lambda e, gt=gt, f8=f8, dstT=dstT, t0=t0: e.dma_start(
                            out=dstT[f8 * 128:(f8 + 1) * 128, t0:t0 + 512], in_=gt[:, f8, :]), reads=[Rgt])
                if g <= 4:
                    for sb in range(4):
                        bk, Rbk = PB.next()
                        bkb = bk[:].bitcast(BF16)
                        for f8 in range(8):
                            tr(bkb[:, f8 * 128:(f8 + 1) * 128], gt[:, f8, sb * 128:(sb + 1) * 128], ident_b[:], [Rgt, Rc], [Rbk], f8 == 7)
                        st, Rst = stg.next()
                        evac(st[:], bkb, [Rbk], [Rst])
                        r0 = t0 + sb * 128
                        if g < 4:
                            P.dma("pool", lambda e, st=st, r0=r0, g=g: e.dma_start(out=XS[r0:r0 + 128, g * 1024:(g + 1) * 1024], in_=st[:]), reads=[Rst])
                        else:
                            P.dma("pool", lambda e, st=st, r0=r0: e.dma_start(out=BTM[r0:r0 + 128, :], in_=st[:]), reads=[Rst])
        P.barrier()
        P.pop()

    def scans():
        P.push()
        Rk = Res("scanconst")
        dtb = P.sbuf("dtb", [128, 128], F32)
        Abc = P.sbuf("Abc", [128, 128], F32)
        Dbc = P.sbuf("Dbc", [128, 64], F32)
        P.dma("sp", lambda e: e.dma_start(out=dtb[:], in_=dt_bias.partition_broadcast(128)), writes=[Rk])
        P.dma("sp", lambda e: e.dma_start(out=Abc[:], in_=a_log.partition_broadcast(128)), writes=[Rk])
        P.dma("sp", lambda e: e.dma_start(out=Dbc[:], in_=ssd_d.partition_broadcast(128)), writes=[Rk])
        P.op("act", lambda e: e.activation(out=Abc[:], in_=Abc[:], func=AF.Exp), reads=[Rk], writes=[Rk])
        P.op("dve", lambda e: e.tensor_scalar(out=Abc[:], in0=Abc[:], scalar1=-1.0, scalar2=None, op0=ALU.mult), reads=[Rk], writes=[Rk])
        dec = P.sbuf("dec", [128, 8, 128], F32)
        xi = P.sbuf("xi", [128, 2, 8, 128], F32)
        zeta = P.sbuf("zeta", [128, 2, 8], F32)
        P.dma("sp", lambda e: e.dma_start(out=dec[:], in_=dec_in), writes=[Rk])
        P.dma("sp", lambda e: e.dma_start(out=xi[:], in_=xi_in), writes=[Rk])
        P.dma("sp", lambda e: e.dma_start(out=zeta[:], in_=zeta_in), writes=[Rk])
        UT = P.sbuf("UT", [128, 128], F32)
        LT = P.sbuf("LT", [128, 128], F32)
        NMf = P.sbuf("NMf", [128, 128], BF16)
        NMb = P.sbuf("NMb", [128, 128], BF16)
        P.op("pool", lambda e: e.memset(UT[:], 1.0), writes=[Rk])
        P.op("pool", lambda e: e.affine_select(out=UT[:], in_=UT[:], pattern=[[1, 128]], compare_op=ALU.is_ge, fill=0.0, base=0, channel_multiplier=-1), reads=[Rk], writes=[Rk])
        P.op("pool", lambda e: e.memset(LT[:], 1.0), writes=[Rk])
        P.op("pool", lambda e: e.affine_select(out=LT[:], in_=LT[:], pattern=[[-1, 128]], compare_op=ALU.is_ge, fill=0.0, base=0, channel_multiplier=1), reads=[Rk], writes=[Rk])
        P.op("pool", lambda e: e.memset(NMf[:], 0.0), writes=[Rk])
        P.op("pool", lambda e: e.affine_select(out=NMf[:], in_=NMf[:], pattern=[[1, 128]], compare_op=ALU.is_ge, fill=-30000.0, base=0, channel_multiplier=-1), reads=[Rk], writes=[Rk])
        P.op("pool", lambda e: e.memset(NMb[:], 0.0), writes=[Rk])
        P.op("pool", lambda e: e.affine_select(out=NMb[:], in_=NMb[:], pattern=[[-1, 128]], compare_op=ALU.is_ge, fill=-30000.0, base=0, channel_multiplier=1), reads=[Rk], writes=[Rk])

        S32 = P.sbuf("S32", [128, 4096], F32)
        R32 = P.sbuf("R32", [128, 8, 2, 512], F32)
        RS, RR = Res("S"), Res("R")

        def dbl(name, shape, dt, n=2):
            return Rot([(P.sbuf(name, shape, dt), Res()) for _ in range(n)])
        sbf = dbl("sbf", [128, 512], BF16, 3)
        rbf = dbl("rbf", [128, 2, 512], BF16, 3)
        xsT = dbl("xs", [128, 4096], BF16)
        bTM = dbl("btm", [128, 1024], BF16)
        dtr = dbl("dtr", [128, 128], F32)
        vT = dbl("v", [128, 4096], BF16, 1)
        kTt = dbl("kT", [128, 16, 128], BF16)
        dtt = P.sbuf("dtt", [128, 128], F32); Rdt = Res()
        lndt = P.sbuf("lndt", [128, 128], F32)
        dA = P.sbuf("dA", [128, 128], F32); RdA = Res()
        cum = P.sbuf("cum", [128, 128], F32); Rcum = Res()
        biasE = P.sbuf("biasE", [128, 128], F32); RbE = Res()
        etotP = [(P.sbuf("etot", [128, 128], F32), Res()) for _ in range(2)]
        wgtP = [(P.sbuf("wgt", [128, 128], F32), Res()) for _ in range(2)]
        ecum = P.sbuf("ecum", [128, 128], F32); Rec = Res()
        tsm = P.sbuf("tsm", [128, 128], F32); Rts = Res()
        xw = dbl("xw", [128, 512], BF16)
        kz = P.sbuf("kz", [128, 8, 256], BF16); Rkz = Res()

        def small_dt(c, dr, Rdr):
            etot, Ret = etotP[c % 2]
            wgt, Rwg = wgtP[c % 2]
            P.op("dve", lambda e: e.tensor_tensor(out=tsm[:], in0=dr[:], in1=dtb[:], op=ALU.add), reads=[Rdr, Rk], writes=[Rts])
            P.op("act", lambda e: e.activation(out=tsm[:], in_=tsm[:], func=AF.Exp), reads=[Rts], writes=[Rts])
            P.op("act", lambda e: e.activation(out=dtt[:], in_=tsm[:], func=AF.Ln, bias=1.0, scale=1.0), reads=[Rts], writes=[Rdt])
            P.op("act", lambda e: e.activation(out=lndt[:], in_=dtt[:], func=AF.Ln), reads=[Rdt], writes=[Rdt])
            P.op("dve", lambda e: e.tensor_tensor(out=dA[:], in0=dtt[:], in1=Abc[:], op=ALU.mult), reads=[Rdt, Rk], writes=[RdA])
            if SCAN_STOP == 31:
                return
            bk, Rbk = PB.next()
            mm(bk[:, 0:64], UT[:], dA[:, 0:64], True, True, [RdA, Rk], [Rbk], False)
            mm(bk[:, 64:128], LT[:], dA[:, 64:128], True, True, [RdA, Rk], [Rbk], False)
            mm(bk[:, 128:256], ones_f[:], dA[:], True, True, [RdA, Rk, Rc], [Rbk], True)
            if SCAN_STOP == 32:
                return
            P.op("dve", lambda e: e.tensor_copy(out=cum[:], in_=bk[:, 0:128]), reads=[Rbk], writes=[Rcum])
            P.op("dve", lambda e: e.tensor_tensor(out=biasE[:], in0=lndt[:], in1=cum[:], op=ALU.subtract), reads=[Rdt, Rcum], writes=[RbE])
            if SCAN_STOP == 33:
                return
            P.op("act", lambda e: e.activation(out=etot[:], in_=bk[:, 128:256], func=AF.Exp), reads=[Rbk], writes=[Ret])
            P.op("dve", lambda e: e.tensor_tensor(out=tsm[:], in0=bk[:, 128:256], in1=biasE[:], op=ALU.add), reads=[Rbk, RbE, Rts], writes=[Rts])
            P.op("act", lambda e: e.activation(out=wgt[:], in_=tsm[:], func=AF.Exp), reads=[Rts], writes=[Rwg])

        def bc(ap2d, n_outer, n_inner):
            return ap2d.unsqueeze(2).to_broadcast([128, n_outer, n_inner])

        def load_early(c):
            t0 = c * 128
            xs_, Rxs = xsT.next()
            b_, Rb = bTM.next()
            dr, Rdr = dtr.next()
            P.dma("sp", lambda e: e.dma_start(out=xs_[:], in_=XS[t0:t0 + 128, :]), writes=[Rxs])
            P.dma("sp", lambda e: e.dma_start(out=b_[:], in_=BTM[t0:t0 + 128, :]), writes=[Rb])
            P.dma("sp", lambda e: e.dma_start(out=dr[:], in_=DTR[t0:t0 + 128, :]), writes=[Rdr])
            return (xs_, Rxs), (b_, Rb), (dr, Rdr)

        def load_v(c):
            t0 = c * 128
            v_, Rv = vT.next()
            P.dma("sp", lambda e: e.dma_start(out=v_[:], in_=V[t0:t0 + 128, :]), writes=[Rv])
            return (v_, Rv)

        def load_k(c):
            t0 = c * 128
            k_, Rkt = kTt.next()
            P.dma("sp", lambda e: e.dma_start(out=k_[:], in_=KT[:, t0:t0 + 128].rearrange("(t p) n -> p t n", p=128)), writes=[Rkt])
            return (k_, Rkt)

        def state_update(c, d, xs_, Rxs, b_, Rb, v_, Rv, k_, Rkt):
            etot, Ret = etotP[c % 2]
            wgt, Rwg = wgtP[c % 2]
            ho = d * 64
            for g in range(8):
                sl = slice(g * 512, (g + 1) * 512)
                xw_, Rxw = xw.next()
                P.op("dve", lambda e, xw_=xw_, sl=sl, g=g: e.tensor_tensor(
                    out=xw_[:].rearrange("p (h q) -> p h q", q=64), in0=xs_[:, sl].rearrange("p (h q) -> p h q", q=64),
                    in1=bc(wgt[:, ho + g * 8:ho + g * 8 + 8], 8, 64), op=ALU.mult), reads=[Rxs, Rwg], writes=[Rxw])
                bk, Rbk = PB.next()
                mm(bk[:], b_[:, g * 128:(g + 1) * 128], xw_[:], True, True, [Rb, Rxw], [Rbk], True)
                P.op("dve", lambda e, sl=sl, g=g: e.tensor_tensor(
                    out=S32[:, sl].rearrange("p (h q) -> p h q", q=64), in0=S32[:, sl].rearrange("p (h q) -> p h q", q=64),
                    in1=bc(etot[:, ho + g * 8:ho + g * 8 + 8], 8, 64), op=ALU.mult), reads=[Ret, RS], writes=[RS])
                P.op("dve", lambda e, sl=sl, bk=bk: e.tensor_tensor(out=S32[:, sl], in0=S32[:, sl], in1=bk[:], op=ALU.add), reads=[Rbk, RS], writes=[RS])
            for half in range(2):
                bk, Rbk = PB.next()
                bkb = bk[:].bitcast(BF16)
                for t in range(8):
                    tt_ = half * 8 + t
                    tr(bkb[:, t * 128:(t + 1) * 128], k_[:, tt_, :], ident_b[:], [Rkt, Rc], [Rbk], t == 7)
                P.op("dve", lambda e, bkb=bkb, half=half: e.tensor_tensor(
                    out=kz[:, half * 4:(half + 1) * 4, :], in0=bkb.rearrange("p (h q) -> p h q", q=256),
                    in1=bc(zeta[:, d, half * 4:(half + 1) * 4], 4, 256), op=ALU.mult), reads=[Rbk, Rk], writes=[Rkz])
            gch = gch_f if d == 0 else gch_b
            for h in range(8):
                for dt_ in range(2):
                    bk, Rbk = PB.next()
                    mm(bk[:], kz[:, h, dt_ * 128:(dt_ + 1) * 128], v_[:, h * 512:(h + 1) * 512], True, True, [Rkz, Rv], [Rbk], True)
                    P.op("dve", lambda e, h=h, dt_=dt_, bk=bk: e.scalar_tensor_tensor(
                        out=R32[:, h, dt_, :], in0=R32[:, h, dt_, :], scalar=float(gch[h]), in1=bk[:], op0=ALU.mult, op1=ALU.add),
                        reads=[Rbk, RR], writes=[RR])

        def reset_states():
            P.op("dve", lambda e: e.memset(S32[:], 0.0), writes=[RS])
            P.op("dve", lambda e: e.memset(R32[:].rearrange("p a b c -> p (a b c)"), 0.0), writes=[RR])

        def apply_cont():
            P.op("pool", lambda e: e.tensor_scalar(out=S32[:], in0=S32[:], scalar1=cont[:, 0:1], scalar2=None, op0=ALU.mult), reads=[RS, Rc], writes=[RS])
            v2 = R32[:].rearrange("p a b c -> p (a b c)")
            P.op("pool", lambda e: e.tensor_scalar(out=v2, in0=v2, scalar1=cont[:, 0:1], scalar2=None, op0=ALU.mult), reads=[RR, Rc], writes=[RR])

        shadow_eng = ["act"]

        def s_shadow(g):
            t, Rt = sbf.next()
            if shadow_eng[0] == "act":
                P.op("act", lambda e: e.activation(out=t[:], in_=S32[:, g * 512:(g + 1) * 512], func=AF.Copy), reads=[RS], writes=[Rt])
            else:
                P.op("pool", lambda e: e.tensor_copy(out=t[:], in_=S32[:, g * 512:(g + 1) * 512]), reads=[RS], writes=[Rt])
            return t, Rt

        def r_shadow(h):
            t, Rt = rbf.next()
            if shadow_eng[0] == "act":
                P.op("act", lambda e: e.activation(out=t[:], in_=R32[:, h], func=AF.Copy), reads=[RR], writes=[Rt])
            else:
                P.op("pool", lambda e: e.tensor_copy(out=t[:], in_=R32[:, h]), reads=[RR], writes=[Rt])
            return t, Rt

        if SCAN_STOP == 1:
            P.barrier(); P.pop(); return
        reset_states()
        def r0_load(c):
            ops = dict(c=c)
            ops['xs'], ops['b'], ops['dr'] = load_early(c)
            ops['k'] = load_k(c)
            small_dt(c, *ops['dr'])
            return ops

        nxt = r0_load(NCH - 1)
        for c in range(NCH - 1, -1, -1):
            cur = nxt
            nxt = r0_load(c - 1) if c > 0 else None
            if (c + 1) * 128 % SEGLEN == 0 and c != NCH - 1:
                apply_cont()
            for g in range(8):
                t, Rt = s_shadow(g)
                P.dma("pool", lambda e, t=t, g=g: e.dma_start(out=SBS[c, :, g * 512:(g + 1) * 512], in_=t[:]), reads=[Rt])
            for h in range(8):
                t, Rt = r_shadow(h)
                P.dma("pool", lambda e, t=t, h=h: e.dma_start(out=SBR[c, :, h * 1024:(h + 1) * 1024], in_=t[:].rearrange("p a b -> p (a b)")), reads=[Rt])
            if c > 0:
                v_, Rv = load_v(c)
                (xs_, Rxs), (b_, Rb), (k_, Rkt) = cur['xs'], cur['b'], cur['k']
                state_update(c, 1, xs_, Rxs, b_, Rb, v_, Rv, k_, Rkt)
        P.barrier()
        if SCAN_STOP <= 4:
            P.pop(); return

        reset_states()
        shadow_eng[0] = "pool"
        btT = dbl("bt", [128, 8, 128], BF16)
        ctT = dbl("ct", [128, 8, 128], BF16)
        qTt = dbl("qT", [128, 16, 128], BF16, 1)
        zT = dbl("z", [128, 4096], BF16, 1)
        grT = dbl("gr", [128, 4096], BF16, 1)
        sbin = dbl("sbin", [128, 512], BF16, 3)
        rbin = dbl("rbin", [128, 2, 512], BF16, 3)
        cbT = P.sbuf("cbT", [128, 8, 128], BF16); Rcb = Res()
        hi = P.sbuf("hi", [128, 128], BF16)
        lo = P.sbuf("lo", [128, 128], BF16); Rhl = Res()
        Eb = dbl("E", [128, 2, 4, 128], BF16)
        Es = dbl("Es", [128, 4, 128], BF16)
        MT = dbl("MT", [128, 4, 128], BF16)
        xsD = dbl("xsD", [128, 512], BF16)
        ysb = dbl("ysb", [128, 512], F32)
        t512 = dbl("t512", [128, 512], F32)
        szr = dbl("sz", [128, 512], BF16)
        junk = P.sbuf("junk", [128, 512], F32); Rjk = Res()
        ss = P.sbuf("ss", [128, 16], F32); Rss = Res()
        yab = P.sbuf("yab", [128, 4096], BF16); Ryab = Res()
        MTr = P.sbuf("MTr", [128, 8, 128], BF16); RMr = Res()
        qxr = dbl("qx", [128, 2, 2, 128], BF16)
        stg = dbl("stgF", [128, 8, 128], BF16, 3)
        def chunk_gen(c):
            t0 = c * 128
            (xs_, Rxs), (b_, Rb), (dr, Rdr) = load_early(c)
            bt_, Rbt = btT.next(); ct_, Rct = ctT.next(); z_, Rz = zT.next()
            P.dma("sp", lambda e: e.dma_start(out=bt_[:], in_=BT[:, t0:t0 + 128].rearrange("(t p) n -> p t n", p=128)), writes=[Rbt])
            P.dma("sp", lambda e: e.dma_start(out=ct_[:], in_=CT[:, t0:t0 + 128].rearrange("(t p) n -> p t n", p=128)), writes=[Rct])
            P.dma("sp", lambda e: e.dma_start(out=z_[:], in_=Z[t0:t0 + 128, :]), writes=[Rz])
            yield "E"
            small_dt(c, dr, Rdr)
            P.op("act", lambda e: e.activation(out=ecum[:], in_=cum[:], func=AF.Exp), reads=[Rcum], writes=[Rec])
            bk, Rbk = PB.next()
            mm(bk[0:64, 0:128], dA[:, 0:64], UT[:], True, True, [RdA, Rk], [Rbk], False)
            mm(bk[64:128, 0:128], dA[:, 64:128], LT[:], True, True, [RdA, Rk], [Rbk], True)
            P.op("act", lambda e, bk=bk: e.activation(out=hi[:], in_=bk[:, 0:128], func=AF.Copy), reads=[Rbk], writes=[Rhl])
            P.op("dve", lambda e, bk=bk: e.tensor_tensor(out=lo[:], in0=bk[:, 0:128], in1=hi[:], op=ALU.subtract), reads=[Rbk, Rhl], writes=[Rhl])
            for half in range(2):
                bk, Rbk = PB.next()
                for g4 in range(4):
                    g = half * 4 + g4
                    mm(bk[:, g4 * 128:(g4 + 1) * 128], bt_[:, g, :], ct_[:, g, :], True, True, [Rbt, Rct], [Rbk], g4 == 3)
                evac(cbT[:, half * 4:(half + 1) * 4, :], bk[:].rearrange("p (g n) -> p g n", g=4), [Rbk], [Rcb])
            yield "P"
            (v_, Rv) = load_v(c)
            (k_, Rkt) = load_k(c)
            q_, Rq = qTt.next(); gr_, Rgr = grT.next()
            P.dma("sp", lambda e: e.dma_start(out=q_[:], in_=QT[:, t0:t0 + 128].rearrange("(t p) n -> p t n", p=128)), writes=[Rq])
            P.dma("sp", lambda e: e.dma_start(out=gr_[:], in_=GRET[t0:t0 + 128, :]), writes=[Rgr])
            yield "L"
            if t0 % SEGLEN == 0 and c > 0:
                apply_cont()
            YB = Rot(banks[0:2]); GB = Rot(banks[2:6]); ST = Rot(banks[6:8])

            def ssd_pro(g):
                sl = slice(g * 512, (g + 1) * 512)
                xd, Rxd = xsD.next()
                P.op("pool", lambda e, xd=xd, sl=sl, g=g: e.tensor_tensor(
                    out=xd[:].rearrange("p (h q) -> p h q", q=64), in0=xs_[:, sl].rearrange("p (h q) -> p h q", q=64),
                    in1=bc(Dbc[:, g * 8:g * 8 + 8], 8, 64), op=ALU.mult), reads=[Rxs, Rk], writes=[Rxd])
                sz_, Rsz = szr.next()
                P.op("act", lambda e, sz_=sz_, sl=sl: e.activation(out=sz_[:], in_=z_[:, sl], func=AF.Silu), reads=[Rz], writes=[Rsz])
                sfb, Rsfb = s_shadow(g)
                sbi, Rsbi = sbin.next()
                P.dma("sp", lambda e, sbi=sbi, sl=sl: e.dma_start(out=sbi[:], in_=SBS[c, :, sl]), writes=[Rsbi])
                yb, Ryb = YB.next()
                mm(yb[:], ident_b[:], xd[:], True, False, [Rxd, Rc], [Ryb], False)
                return dict(sl=sl, sz_=sz_, Rsz=Rsz, sfb=sfb, Rsfb=Rsfb, sbi=sbi, Rsbi=Rsbi, yb=yb, Ryb=Ryb, mt={})

            def ssd_A(cx, g, h4):
                if True:
                    E_, RE = Eb.next()
                    for d in range(2):
                        gb_, Rgb = GB.next()
                        NM = NMf if d == 0 else NMb
                        for r in range(4):
                            hidx = d * 64 + g * 8 + h4 * 4 + r
                            sel = bass.AP(ident_b, hidx, [[128, 128], [0, 128]])
                            o_ = gb_[:, r * 128:(r + 1) * 128]
                            mm(o_, sel, hi[:], True, False, [Rhl, Rc], [Rgb], False)
                            mm(o_, sel, lo[:], False, False, [Rhl, Rc], [Rgb], False)
                            mm(o_, ident_b[:], NM[:], False, True, [Rk, Rc], [Rgb], r == 3)
                        for r in range(4):
                            hidx = d * 64 + g * 8 + h4 * 4 + r
                            P.op("act", lambda e, E_=E_, d=d, r=r, gb_=gb_, hidx=hidx: e.activation(
                                out=E_[:, d, r, :], in_=gb_[:, r * 128:(r + 1) * 128], func=AF.Exp, bias=biasE[:, hidx:hidx + 1], scale=1.0),
                                reads=[Rgb, RbE], writes=[RE])
                    es_, Res_ = Es.next()
                    mt_, Rmt = MT.next()
                    P.op("dve", lambda e, es_=es_, E_=E_: e.tensor_tensor(out=es_[:], in0=E_[:, 0], in1=E_[:, 1], op=ALU.add), reads=[RE], writes=[Res_])
                    P.op("dve", lambda e, es_=es_, mt_=mt_, g=g: e.tensor_tensor(out=mt_[:], in0=es_[:], in1=cbT[:, g:g + 1, :].to_broadcast([128, 4, 128]), op=ALU.mult),
                         reads=[Res_, Rcb], writes=[Rmt])
                    cx['mt'][h4] = (mt_, Rmt)

            def ssd_B(cx, g, h4):
                if True:
                    mt_, Rmt = cx['mt'][h4]
                    yb, Ryb = cx['yb'], cx['Ryb']
                    for r in range(4):
                        hl = g * 8 + h4 * 4 + r
                        last = (h4 == 1 and r == 3)
                        mm(yb[:, (h4 * 4 + r) * 64:(h4 * 4 + r + 1) * 64], mt_[:, r, :], xs_[:, hl * 64:(hl + 1) * 64],
                           False, last, [Rmt, Rxs], [Ryb], last)

            def ssd_tail(g, cx):
                sl = cx['sl']; sz_ = cx['sz_']; Rsz = cx['Rsz']; sfb = cx['sfb']; Rsfb = cx['Rsfb']; sbi = cx['sbi']; Rsbi = cx['Rsbi']; yb = cx['yb']; Ryb = cx['Ryb']
                sf, Rsf = ST.next()
                mm(sf[:], ct_[:, g, :], sfb[:], True, True, [Rct, Rsfb], [Rsf], True)
                sbk, Rsbk = ST.next()
                mm(sbk[:], ct_[:, g, :], sbi[:], True, True, [Rct, Rsbi], [Rsbk], True)
                ta, Rta = t512.next()
                P.op("dve", lambda e, ta=ta, sf=sf, g=g: e.tensor_tensor(out=ta[:].rearrange("p (h q) -> p h q", q=64), in0=sf[:].rearrange("p (h q) -> p h q", q=64),
                                                                         in1=bc(ecum[:, g * 8:g * 8 + 8], 8, 64), op=ALU.mult), reads=[Rsf, Rec], writes=[Rta])
                tb, Rtb = t512.next()
                P.op("dve", lambda e, tb=tb, sbk=sbk, g=g: e.tensor_tensor(out=tb[:].rearrange("p (h q) -> p h q", q=64), in0=sbk[:].rearrange("p (h q) -> p h q", q=64),
                                                                           in1=bc(ecum[:, 64 + g * 8:64 + g * 8 + 8], 8, 64), op=ALU.mult), reads=[Rsbk, Rec], writes=[Rtb])
                P.op("pool", lambda e, ta=ta, tb=tb: e.tensor_tensor(out=ta[:], in0=ta[:], in1=tb[:], op=ALU.add), reads=[Rta, Rtb], writes=[Rta])
                ys_, Rys = ysb.next()
                P.op("dve", lambda e, ta=ta, yb=yb, ys_=ys_: e.tensor_tensor(out=ys_[:], in0=yb[:], in1=ta[:], op=ALU.add), reads=[Ryb, Rta], writes=[Rys])
                P.op("pool", lambda e, ys_=ys_, sz_=sz_: e.tensor_tensor(out=ys_[:], in0=ys_[:], in1=sz_[:], op=ALU.mult), reads=[Rys, Rsz], writes=[Rys])
                P.op("act", lambda e, ys_=ys_, g=g: e.activation(out=junk[:], in_=ys_[:], func=AF.Square, accum_out=ss[:, g:g + 1]), reads=[Rys], writes=[Rjk, Rss])
                P.op("act", lambda e, g=g: e.activation(out=ss[:, g:g + 1], in_=ss[:, g:g + 1], func=AF.Ln, bias=epsc[:, 0:1], scale=1.0 / 512), reads=[Rss], writes=[Rss])
                P.op("act", lambda e, g=g: e.activation(out=ss[:, g:g + 1], in_=ss[:, g:g + 1], func=AF.Exp, scale=-0.5), reads=[Rss], writes=[Rss])
                P.op("dve", lambda e, ys_=ys_, sl=sl, g=g: e.tensor_scalar(out=yab[:, sl], in0=ys_[:], scalar1=ss[:, g:g + 1], scalar2=None, op0=ALU.mult), reads=[Rys, Rss], writes=[Ryab])

            cxs = {}
            for n in range(18):
                if n < 16:
                    g, h4 = divmod(n, 2)
                    if h4 == 0:
                        cxs[g] = ssd_pro(g)
                    ssd_A(cxs[g], g, h4)
                if 1 <= n <= 16:
                    g, h4 = divmod(n - 1, 2)
                    ssd_B(cxs[g], g, h4)
                if n >= 3 and (n - 3) % 2 == 0:
                    g = (n - 3) // 2
                    ssd_tail(g, cxs.pop(g))
            for t8 in range(4):
                bk, Rbk = PB.next()
                bkb = bk[:].bitcast(BF16)
                for t in range(8):
                    f = t8 * 8 + t
                    tr(bkb[:, t * 128:(t + 1) * 128], yab[:, f * 128:(f + 1) * 128], ident_b[:], [Ryab, Rc], [Rbk], t == 7)
                st, Rst = stg.next()
                evac(st[:], bkb.rearrange("p (t n) -> p t n", t=8), [Rbk], [Rst])
                P.dma("pool", lambda e, st=st, t8=t8: e.dma_start(
                    out=YAT[t8 * 1024:(t8 + 1) * 1024, t0:t0 + 128].rearrange("(t p) n -> p t n", p=128), in_=st[:]), reads=[Rst])
            yield "S"
            for half in range(2):
                bk, Rbk = PB.next()
                for h4 in range(4):
                    h = half * 4 + h4
                    for dt_ in range(2):
                        mm(bk[:, h4 * 128:(h4 + 1) * 128], k_[:, h * 2 + dt_, :], q_[:, h * 2 + dt_, :], dt_ == 0, dt_ == 1, [Rkt, Rq], [Rbk], (h4 == 3 and dt_ == 1))
                P.op("dve", lambda e, bk=bk, half=half: e.tensor_tensor(out=MTr[:, half * 4:(half + 1) * 4, :], in0=bk[:].rearrange("p (h n) -> p h n", h=4),
                                                                        in1=dec[:, half * 4:(half + 1) * 4, :], op=ALU.mult), reads=[Rbk, Rk], writes=[RMr])
            OBK = Rot(banks[0:2])

            def ret_head(h):
                hs = slice(h * 512, (h + 1) * 512)
                qx, Rqx = qxr.next()
                P.op("pool", lambda e, qx=qx, h=h: e.tensor_tensor(out=qx[:], in0=q_[:, h * 2:h * 2 + 2, :].unsqueeze(1).to_broadcast([128, 2, 2, 128]),
                                                                   in1=xi[:, :, h, :].unsqueeze(2).to_broadcast([128, 2, 2, 128]), op=ALU.mult), reads=[Rq, Rk], writes=[Rqx])
                sz_, Rsz = szr.next()
                P.op("act", lambda e, sz_=sz_, hs=hs: e.activation(out=sz_[:], in_=gr_[:, hs], func=AF.Silu), reads=[Rgr], writes=[Rsz])
                rfb, Rrfb = r_shadow(h)
                rbi, Rrbi = rbin.next()
                P.dma("sp", lambda e, rbi=rbi, h=h: e.dma_start(out=rbi[:].rearrange("p a b -> p (a b)"), in_=SBR[c, :, h * 1024:(h + 1) * 1024]), writes=[Rrbi])
                obk, Robk = OBK.next()
                mm(obk[:], MTr[:, h, :], v_[:, hs], True, False, [RMr, Rv], [Robk], False)
                for dt_ in range(2):
                    mm(obk[:], qx[:, 0, dt_, :], rfb[:, dt_, :], False, False, [Rqx, Rrfb], [Robk], False)
                for dt_ in range(2):
                    mm(obk[:], qx[:, 1, dt_, :], rbi[:, dt_, :], False, dt_ == 1, [Rqx, Rrbi], [Robk], dt_ == 1)
                return dict(hs=hs, sz_=sz_, Rsz=Rsz, obk=obk, Robk=Robk)

            def ret_tail(h, cx):
                hs = cx['hs']; sz_ = cx['sz_']; Rsz = cx['Rsz']; obk = cx['obk']; Robk = cx['Robk']
                P.op("act", lambda e, obk=obk, h=h: e.activation(out=junk[:], in_=obk[:], func=AF.Square, accum_out=ss[:, 8 + h:9 + h]), reads=[Robk], writes=[Rjk, Rss])
                P.op("act", lambda e, h=h: e.activation(out=ss[:, 8 + h:9 + h], in_=ss[:, 8 + h:9 + h], func=AF.Ln, bias=epsc[:, 0:1], scale=1.0 / 512), reads=[Rss], writes=[Rss])
                P.op("act", lambda e, h=h: e.activation(out=ss[:, 8 + h:9 + h], in_=ss[:, 8 + h:9 + h], func=AF.Exp, scale=-0.5), reads=[Rss], writes=[Rss])
                P.op("dve", lambda e, obk=obk, h=h, hs=hs, sz_=sz_: e.scalar_tensor_tensor(out=yab[:, hs], in0=obk[:], scalar=ss[:, 8 + h:9 + h], in1=sz_[:],
                                                                                  op0=ALU.mult, op1=ALU.mult), reads=[Robk, Rss, Rsz], writes=[Ryab])

            cxs = {}
            for h in range(9):
                if h < 8:
                    cxs[h] = ret_head(h)
                if h >= 1:
                    ret_tail(h - 1, cxs.pop(h - 1))
            for t8 in range(4):
                bk, Rbk = PB.next()
                bkb = bk[:].bitcast(BF16)
                for t in range(8):
                    f = t8 * 8 + t
                    tr(bkb[:, t * 128:(t + 1) * 128], yab[:, f * 128:(f + 1) * 128], ident_b[:], [Ryab, Rc], [Rbk], t == 7)
                st, Rst = stg.next()
                evac(st[:], bkb.rearrange("p (t n) -> p t n", t=8), [Rbk], [Rst])
                P.dma("pool", lambda e, st=st, t8=t8: e.dma_start(
                    out=OT[t8 * 1024:(t8 + 1) * 1024, t0:t0 + 128].rearrange("(t p) n -> p t n", p=128), in_=st[:]), reads=[Rst])
            yield "R"
            if c < NCH - 1:
                state_update(c, 0, xs_, Rxs, b_, Rb, v_, Rv, k_, Rkt)
            yield "U"

        gens = [chunk_gen(c) for c in range(NCH)]

        def adv(c, n):
            for _ in range(n):
                next(gens[c])

        adv(0, 4)
        for c in range(NCH):
            if c + 1 < NCH:
                adv(c + 1, 2)
            adv(c, 2)
            if c + 1 < NCH:
                adv(c + 1, 2)
        P.barrier()
        P.pop()

    def phaseC():
        P.push()
        NTC = NT // TC
        ain = Rot([(P.sbuf("ain", [128, 32, TC], BF16), Res()) for _ in range(1)])
        wbig = Rot([(P.sbuf("wbig", [128, 32, 512], BF16), Res()) for _ in range(2)])
        gte = Rot([(P.sbuf("gte", [128, TC], BF16), Res()) for _ in range(3)])
        sg = Rot([(P.sbuf("sgC", [128, TC], F32), Res()) for _ in range(2)])
        m32 = P.sbuf("m32", [128, KD, TC], F32); Rm32 = Res()
        mT = P.sbuf("mT", [128, KD, TC], BF16); RmT = Res()
        hT = P.sbuf("hT", [128, KD, TC], F32); RhT = Res()
        n2, Rn2 = mT, RmT
        xr = Rot([(P.sbuf("xr", [128, TC], F32), Res()) for _ in range(2)])
        sq = Rot([(P.sbuf("sqC", [128, TC], F32), Res()) for _ in range(2)])
        rstd = P.sbuf("rstdC", [128, TC], F32); Rrs = Res()
        tmp = Rot([(P.sbuf("tmpC", [128, TC], F32), Res()) for _ in range(2)])
        stg = Rot([(P.sbuf("stgC", [128, TC], BF16), Res()) for _ in range(4)])
        for tt in range(NTC):
            t0 = tt * TC
            s = t0 // SEGLEN
            for br in range(2):
                src = YAT if br == 0 else OT
                W = WPA if br == 0 else WPB
                a_, Ra = ain.next()
                for hlf in range(2):
                    P.dma("sp", lambda e, a_=a_, src=src, hlf=hlf: e.dma_start(
                        out=a_[:, hlf * 16:(hlf + 1) * 16, :], in_=src[hlf * 2048:(hlf + 1) * 2048, t0:t0 + TC].rearrange("(k p) n -> p k n", p=128)), writes=[Ra])
                for blk in range(4):
                    w, Rw = wbig.next()
                    for hlf in range(2):
                        P.dma("sp", lambda e, w=w, W=W, blk=blk, hlf=hlf: e.dma_start(
                            out=w[:, hlf * 16:(hlf + 1) * 16, :], in_=W[hlf * 2048:(hlf + 1) * 2048, blk * 512:(blk + 1) * 512].rearrange("(k p) n -> p k n", p=128)), writes=[Rw])
                    for j in range(4):
                        dtile = blk * 4 + j
                        g_, Rg = gte.next()
                        P.dma("sp", lambda e, g_=g_, r0=br * 2048 + dtile * 128: e.dma_start(out=g_[:], in_=GATES[r0:r0 + 128, t0:t0 + TC]), writes=[Rg])
                        s_, Rs = sg.next()
                        P.op("act", lambda e, s_=s_, g_=g_: e.activation(out=s_[:], in_=g_[:], func=AF.Sigmoid), reads=[Rg], writes=[Rs])
                        bk, Rbk = PB.next()
                        for k in range(32):
                            mm(bk[:, 0:TC], w[:, k, j * 128:(j + 1) * 128], a_[:, k, :], k == 0, k == 31, [Rw, Ra], [Rbk], k == 31)
                        if br == 0:
                            P.op("dve", lambda e, bk=bk, s_=s_, dtile=dtile: e.tensor_tensor(out=m32[:, dtile, :], in0=bk[:, 0:TC], in1=s_[:], op=ALU.mult), reads=[Rbk, Rs], writes=[Rm32])
                        else:
                            tp, Rtp = tmp.next()
                            P.op("dve", lambda e, bk=bk, s_=s_, tp=tp: e.tensor_tensor(out=tp[:], in0=bk[:, 0:TC], in1=s_[:], op=ALU.mult), reads=[Rbk, Rs], writes=[Rtp])
                            P.op("pool", lambda e, tp=tp, dtile=dtile: e.tensor_tensor(out=mT[:, dtile, :], in0=tp[:], in1=m32[:, dtile, :], op=ALU.add), reads=[Rtp, Rm32], writes=[RmT])
            bss, Rbss = ACCB
            for blk in range(4):
                w, Rw = wbig.next()
                P.dma("sp", lambda e, w=w, blk=blk: e.dma_start(out=w[:, 0:16, :], in_=WOUT[:, blk * 512:(blk + 1) * 512].rearrange("(k p) n -> p k n", p=128)), writes=[Rw])
                for j in range(4):
                    dtile = blk * 4 + j
                    x_, Rx = xr.next()
                    P.dma("sp", lambda e, x_=x_, dtile=dtile: e.dma_start(out=x_[:], in_=XT[dtile * 128:(dtile + 1) * 128, t0:t0 + TC]), writes=[Rx])
                    bk, Rbk = PB.next()
                    for k in range(KD):
                        mm(bk[:, 0:TC], w[:, k, j * 128:(j + 1) * 128], mT[:, k, :], k == 0, k == KD - 1, [Rw, RmT], [Rbk], k == KD - 1)
                    P.op("dve", lambda e, bk=bk, x_=x_, dtile=dtile: e.scalar_tensor_tensor(
                        out=hT[:, dtile, :], in0=bk[:, 0:TC], scalar=modT[:, 32 + dtile, s:s + 1], in1=x_[:], op0=ALU.mult, op1=ALU.add),
                        reads=[Rbk, Rx, Rmod], writes=[RhT])
                    P.dma("pool", lambda e, dtile=dtile: e.dma_start(out=H[dtile * 128:(dtile + 1) * 128, t0:t0 + TC], in_=hT[:, dtile, :]), reads=[RhT])
                    q_, Rq = sq.next()
                    P.op("act", lambda e, q_=q_, dtile=dtile: e.activation(out=q_[:], in_=hT[:, dtile, :], func=AF.Square), reads=[RhT], writes=[Rq])
                    mm(bss[:, 0:TC], ones_f[:], q_[:], dtile == 0, dtile == KD - 1, [Rq, Rc], [Rbss], True)
            P.op("act", lambda e, bss=bss: e.activation(out=rstd[:], in_=bss[:, 0:TC], func=AF.Ln, bias=epsc[:, 0:1], scale=1.0 / D), reads=[Rbss], writes=[Rrs])
            P.op("act", lambda e, bss=bss: e.activation(out=rstd[:], in_=rstd[:], func=AF.Exp, scale=-0.5), reads=[Rrs], writes=[Rrs])
            for k in range(KD):
                tp, Rtp = tmp.next()
                P.op("dve", lambda e, tp=tp, k=k: e.scalar_tensor_tensor(out=tp[:], in0=hT[:, k, :], scalar=scale2[:, k, s:s + 1], in1=rstd[:], op0=ALU.mult, op1=ALU.mult),
                     reads=[RhT, Rrs, Rmod], writes=[Rtp])
                P.op("act", lambda e, tp=tp, k=k: e.activation(out=n2[:, k, :], in_=tp[:], func=AF.Identity, bias=modT[:, 48 + k, s:s + 1], scale=1.0),
                     reads=[Rtp, Rmod], writes=[Rn2])
            for blk in range(22):
                w, Rw = wbig.next()
                P.dma("sp", lambda e, w=w, blk=blk: e.dma_start(out=w[:, 0:16, :], in_=WUP[:, blk * 512:(blk + 1) * 512].rearrange("(k p) n -> p k n", p=128)), writes=[Rw])
                for j in range(4):
                    ft = blk * 4 + j
                    bk, Rbk = PB.next()
                    for k in range(KD):
                        mm(bk[:, 0:TC], w[:, k, j * 128:(j + 1) * 128], n2[:, k, :], k == 0, k == KD - 1, [Rw, Rn2], [Rbk], k == KD - 1)
                    st, Rst = stg.next()
                    evac(st[:], bk[:, 0:TC], [Rbk], [Rst])
                    dst = AT if ft < NF else GT
                    r0 = (ft % NF) * 128
                    P.dma("pool", lambda e, st=st, dst=dst, r0=r0: e.dma_start(out=dst[r0:r0 + 128, t0:t0 + TC], in_=st[:]), reads=[Rst])
        P.barrier()
        P.pop()

    def phaseD():
        P.push()
        NTC = NT // TC
        fw = P.sbuf("fw", [128, NF, 3], F32)
        fb = P.sbuf("fb", [128, NF], F32)
        Rfw = Res()
        P.dma("sp", lambda e: e.dma_start(out=fw[:], in_=fconv_wT), writes=[Rfw])
        P.dma("sp", lambda e: e.dma_start(out=fb[:], in_=fconv_bT), writes=[Rfw])
        ab = Rot([(P.sbuf("abD", [128, TC + 4], BF16), Res()) for _ in range(4)])
        gb = Rot([(P.sbuf("gbD", [128, TC], BF16), Res()) for _ in range(5)])
        acc = Rot([(P.sbuf("accD", [128, TC], F32), Res()) for _ in range(3)])
        sl_ = Rot([(P.sbuf("slD", [128, TC], BF16), Res()) for _ in range(4)])
        uTr = Rot([(P.sbuf("uT", [128, NF, TC], BF16), Res()) for _ in range(2)])
        wd = Rot([(P.sbuf("wd", [128, NF, 256], BF16), Res()) for _ in range(2)])
        hr = Rot([(P.sbuf("hr", [128, TC], F32), Res()) for _ in range(2)])
        h2 = P.sbuf("h2", [128, KD, TC], F32); Rh2 = Res()
        sq = Rot([(P.sbuf("sqD", [128, TC], F32), Res()) for _ in range(2)])
        rstd = P.sbuf("rstdD", [128, TC], F32); Rrs = Res()
        yo = Rot([(P.sbuf("yo", [128, D], F32), Res()) for _ in range(1)])
        def conv(tt, uT, RuT):
            t0 = tt * TC
            lo = max(t0 - 1, 0)
            hi = min(t0 + TC + 1, NT)
            prev = None
            for f in range(NF + 1):
                if f < NF:
                    cur = conv_s1(t0, lo, hi, f)
                if prev is not None:
                    conv_s2(prev, uT, RuT)
                prev = cur if f < NF else None
                yield

        def conv_s1(t0, lo, hi, f):
            if True:
                a_, Ra = ab.next()
                g_, Rg = gb.next()
                if lo > t0 - 1:
                    P.op("pool", lambda e, a_=a_: e.memset(a_[:, 0:2], 0.0), writes=[Ra])
                if hi < t0 + TC + 1:
                    P.op("pool", lambda e, a_=a_: e.memset(a_[:, TC + 2:TC + 4], 0.0), writes=[Ra])
                P.dma("sp", lambda e, a_=a_, f=f: e.dma_start(out=a_[:, lo - t0 + 2:hi - t0 + 2], in_=AT[f * 128:(f + 1) * 128, lo:hi]), writes=[Ra])
                P.dma("sp", lambda e, g_=g_, f=f: e.dma_start(out=g_[:], in_=GT[f * 128:(f + 1) * 128, t0:t0 + TC]), writes=[Rg])
                if t0 % SEGLEN == 0 and t0 > 0:
                    P.op("act", lambda e, a_=a_: e.activation(out=a_[:, 0:2], in_=a_[:, 0:2], func=AF.Copy, scale=cont[:, 0:1]), reads=[Ra, Rc], writes=[Ra])
                if (t0 + TC) % SEGLEN == 0 and t0 + TC < NT:
                    P.op("act", lambda e, a_=a_: e.activation(out=a_[:, TC + 2:TC + 4], in_=a_[:, TC + 2:TC + 4], func=AF.Copy, scale=cont[:, 0:1]), reads=[Ra, Rc], writes=[Ra])
                c_, Rcc = acc.next()
                P.op("act", lambda e, c_=c_, a_=a_, f=f: e.activation(out=c_[:], in_=a_[:, 1:TC + 1], func=AF.Identity, bias=fb[:, f:f + 1], scale=fw[:, f, 0:1]), reads=[Ra, Rfw], writes=[Rcc])
                for k in range(1, 3):
                    P.op("dve", lambda e, c_=c_, a_=a_, f=f, k=k: e.scalar_tensor_tensor(out=c_[:], in0=a_[:, k + 1:k + 1 + TC], scalar=fw[:, f, k:k + 1], in1=c_[:], op0=ALU.mult, op1=ALU.add),
                         reads=[Ra, Rfw, Rcc], writes=[Rcc])
            return (f, c_, Rcc, g_, Rg)

        def conv_s2(st, uT, RuT):
            f, c_, Rcc, g_, Rg = st
            s_, Rs = sl_.next()
            P.op("act", lambda e, s_=s_, c_=c_: e.activation(out=s_[:], in_=c_[:], func=AF.Silu), reads=[Rcc], writes=[Rs])
            P.op("pool", lambda e, s_=s_, g_=g_, f=f: e.tensor_tensor(out=uT[:, f, :], in0=s_[:], in1=g_[:], op=ALU.mult), reads=[Rs, Rg], writes=[RuT])

        def rest(tt, uT, RuT, gen):
            t0 = tt * TC
            s = t0 // SEGLEN
            bss, Rbss = ACCB
            for blk in range(8):
                if gen is not None:
                    for _ in range(6):
                        next(gen, None)
                w, Rw = wd.next()
                for (f0, f1) in ((0, 22), (22, 44)):
                    P.dma("sp", lambda e, w=w, blk=blk, f0=f0, f1=f1: e.dma_start(
                        out=w[:, f0:f1, :], in_=WDN[f0 * 128:f1 * 128, blk * 256:(blk + 1) * 256].rearrange("(k p) n -> p k n", p=128)), writes=[Rw])
                for j in range(2):
                    dtile = blk * 2 + j
                    h_, Rh = hr.next()
                    P.dma("sp", lambda e, h_=h_, dtile=dtile: e.dma_start(out=h_[:], in_=H[dtile * 128:(dtile + 1) * 128, t0:t0 + TC]), writes=[Rh])
                    bk, Rbk = PB.next()
                    for k in range(NF):
                        mm(bk[:, 0:TC], w[:, k, j * 128:(j + 1) * 128], uT[:, k, :], k == 0, k == NF - 1, [Rw, RuT], [Rbk], k == NF - 1)
                    P.op("dve", lambda e, bk=bk, h_=h_, dtile=dtile: e.scalar_tensor_tensor(
                        out=h2[:, dtile, :], in0=bk[:, 0:TC], scalar=modT[:, 80 + dtile, s:s + 1], in1=h_[:], op0=ALU.mult, op1=ALU.add),
                        reads=[Rbk, Rh, Rmod], writes=[Rh2])
                    q_, Rq = sq.next()
                    P.op("act", lambda e, q_=q_, dtile=dtile: e.activation(out=q_[:], in_=h2[:, dtile, :], func=AF.Square), reads=[Rh2], writes=[Rq])
                    mm(bss[:, 0:TC], ones_f[:], q_[:], dtile == 0, dtile == KD - 1, [Rq, Rc], [Rbss], True)
            P.op("act", lambda e, bss=bss: e.activation(out=rstd[:], in_=bss[:, 0:TC], func=AF.Ln, bias=epsc[:, 0:1], scale=1.0 / D), reads=[Rbss], writes=[Rrs])
            P.op("act", lambda e, bss=bss: e.activation(out=rstd[:], in_=rstd[:], func=AF.Exp, scale=-0.5), reads=[Rrs], writes=[Rrs])
            for k in range(KD):
                P.op("dve", lambda e, k=k: e.scalar_tensor_tensor(out=h2[:, k, :], in0=h2[:, k, :], scalar=nfT[:, k:k + 1], in1=rstd[:], op0=ALU.mult, op1=ALU.mult),
                     reads=[Rh2, Rrs, Rmod], writes=[Rh2])
            for sb in range(TC // 128):
                y_, Ry = yo.next()
                for k4 in range(4):
                    bk, Rbk = PB.next()
                    for kk in range(4):
                        k = k4 * 4 + kk
                        tr(bk[:, kk * 128:(kk + 1) * 128], h2[:, k, sb * 128:(sb + 1) * 128], ident_f[:], [Rh2, Rc], [Rbk], kk == 3)
                    evac(y_[:, k4 * 512:(k4 + 1) * 512], bk[:], [Rbk], [Ry])
                r0 = t0 + sb * 128
                P.dma("pool", lambda e, y_=y_, r0=r0: e.dma_start(out=y_out[r0:r0 + 128, :], in_=y_[:]), reads=[Ry])

        cur_u = uTr.next()
        for _ in conv(0, *cur_u):
            pass
        for tt in range(NTC):
            nxt_u = uTr.next() if tt + 1 < NTC else None
            gen = conv(tt + 1, *nxt_u) if nxt_u is not None else None
            rest(tt, cur_u[0], cur_u[1], gen)
            cur_u = nxt_u
        P.barrier()
        P.pop()

    for i, ph in enumerate((phase0, phaseA, phaseA2, scans, phaseC, phaseD)):
        if i <= upto:
            ph()
    P.finish()
    return nc


NSEG_FULL = 4
SEGLEN_FULL = 2048


def rope_tables(pos):
    half = 128
    inv = 10000.0 ** (-np.arange(half, dtype=np.float32) / half)
    ang = pos.astype(np.float32)[None, :] * inv[:, None].astype(np.float32)
    return np.cos(ang).astype(np.float32), np.sin(ang).astype(np.float32)


def tileT(v, ntile):
    return np.ascontiguousarray(np.asarray(v, np.float32).reshape(ntile, 128).T)


def shared_inputs(w):
    dec_c, xi_c, zeta_c, _, _ = ret_consts()
    sq = lambda a: np.ascontiguousarray(np.asarray(a, np.float32)[0])
    conv_w = sq(w["conv_w"])
    fcw = sq(w["ffn_conv_w"])
    return {
        "dec": dec_c, "xi": xi_c, "zeta": zeta_c,
        "w_ada": sq(w["w_ada"]), "b_adaT": tileT(sq(w["b_ada"]), 96), "norm1T": tileT(sq(w["norm1"]), 16),
        "w_in": sq(w["w_in"]),
        "conv_wT": np.ascontiguousarray(conv_w.reshape(5, 48, 128).transpose(2, 1, 0)),
        "conv_bT": tileT(sq(w["conv_b"]), 48),
        "dt_bias": sq(w["dt_bias"]).reshape(1, 128), "a_log": sq(w["a_log"]).reshape(1, 128),
        "ssd_d": sq(w["ssd_d"]).reshape(1, 64), "ssd_normT": tileT(sq(w["ssd_norm"]), 32),
        "w_pa": sq(w["w_pa"]), "w_pb": sq(w["w_pb"]), "w_out": sq(w["w_out"]), "norm2T": tileT(sq(w["norm2"]), 16),
        "w_up": sq(w["w_up"]),
        "fconv_wT": np.ascontiguousarray(fcw.reshape(3, NF, 128).transpose(2, 1, 0)),
        "fconv_bT": tileT(sq(w["ffn_conv_b"]), NF),
        "w_down": sq(w["w_down"]), "norm_fT": tileT(np.asarray(w["norm_f"], np.float32), 16),
    }


def core_inputs(xseg, cseg, cont, seglen, shared):
    nseg = len(xseg)
    x = np.ascontiguousarray(np.concatenate(xseg, 0), dtype=np.float32)
    c = np.stack(cseg, 0).astype(np.float32)
    cT = np.ascontiguousarray(c.reshape(nseg, KD, 128).transpose(2, 1, 0))
    if cont:
        pos = np.arange(nseg * seglen)
    else:
        pos = np.tile(np.arange(seglen), nseg)
    cosT, sinT = rope_tables(pos)
    m = dict(shared)
    m.update({"x": x, "cT": cT, "cont": np.full((128, 1), float(cont), np.float32), "cosT": cosT, "sinT": sinT})
    return m


_NC_CACHE = {}


def kernel(x_prompt, x_sample, c_prompt, c_sample, **w):
    x_prompt = np.asarray(x_prompt, np.float32)
    x_sample = np.asarray(x_sample, np.float32)
    c_prompt = np.asarray(c_prompt, np.float32)
    c_sample = np.asarray(c_sample, np.float32)
    shared = shared_inputs(w)
    NS, SL = NSEG_FULL, SEGLEN_FULL
    in_maps = []
    for b in range(4):
        xs = [x_sample[b, i * SL:(i + 1) * SL] for i in range(NS)]
        in_maps.append(core_inputs(xs, [c_sample[b]] * NS, 1.0, SL, shared))
    zx = np.zeros((SL, D), np.float32)
    zc = np.zeros((D,), np.float32)
    for i in range(4):
        xs = [x_prompt[2 * i], x_prompt[2 * i + 1], zx, zx]
        cs = [c_prompt[2 * i], c_prompt[2 * i + 1], zc, zc]
        in_maps.append(core_inputs(xs, cs, 0.0, SL, shared))
    key = (NS, SL)
    if key not in _NC_CACHE:
        _NC_CACHE[key] = build(NS, SL)
    nc = _NC_CACHE[key]
    res = run_bass_kernel_spmd(nc, in_maps, core_ids=list(range(8)))
    y_sample = np.stack([np.asarray(res.results[b]["y"], np.float32).reshape(NS * SL, D) for b in range(4)], 0)
    yp = []
    for i in range(4):
        y = np.asarray(res.results[4 + i]["y"], np.float32).reshape(NS * SL, D)
        yp.append(y[0:SL])
        yp.append(y[SL:2 * SL])
    y_prompt = np.stack(yp, 0)
    return (y_prompt, y_sample)
```

```python
import numpy as np
import concourse.bass as bass
import concourse.mybir as mybir
from concourse.bass_utils import run_bass_kernel_spmd
from contextlib import ExitStack

F32 = mybir.dt.float32
BF16 = mybir.dt.bfloat16
AF = mybir.ActivationFunctionType
ALU = mybir.AluOpType
AX = mybir.AxisListType

D = 2048
KD = 16
E_IN = 26752
DFF = 5632
NF = 44
EPS = 1e-6
ENGS = ("pe", "act", "dve", "pool", "sp")
DMAQ = {"sp": 12, "pool": 8, "act": 4}
DEBUG_LINES = []
SCAN_STOP = 99
D_STOP = 99


import types as _types


def freeze(fn):
    if fn.__closure__ is None:
        return fn
    cells = []
    for c in fn.__closure__:
        try:
            cells.append(_types.CellType(c.cell_contents))
        except ValueError:
            cells.append(c)
    return _types.FunctionType(fn.__code__, fn.__globals__, fn.__name__, fn.__defaults__, tuple(cells))


class Res:
    __slots__ = ("name", "w", "r", "x")

    def __init__(self, name="", excl=False):
        self.name = name
        self.w = None
        self.r = {}
        self.x = excl


class Prog:
    def __init__(self, nc):
        self.nc = nc
        self.es = ExitStack()
        self.scopes = []
        self.q = {e: [] for e in ENGS}
        self.sems = {}
        self.cnt = {}
        self.known = {e: {} for e in ENGS}
        for e in ENGS:
            self.sems[e] = self.es.enter_context(nc.semaphore("s_" + e))
            self.cnt[e] = 0
        self.dq = {}
        for qn, n in DMAQ.items():
            ks = []
            for i in range(n):
                k = "d_%s%d" % (qn, i)
                self.sems[k] = self.es.enter_context(nc.semaphore("s_" + k))
                self.cnt[k] = 0
                ks.append(k)
            self.dq[qn] = [ks, 0]
        self.ninst = 0
        self.uid = 0

    def push(self):
        self.scopes.append(ExitStack())

    def pop(self):
        self.scopes.pop().close()

    def _stk(self):
        return self.scopes[-1] if self.scopes else self.es

    def sbuf(self, name, shape, dt):
        self.uid += 1
        return self._stk().enter_context(self.nc.sbuf_tensor("%s_%d" % (name, self.uid), list(shape), dt))

    def psum(self, name, shape, dt):
        self.uid += 1
        return self._stk().enter_context(self.nc.psum_tensor("%s_%d" % (name, self.uid), list(shape), dt))

    def _deps(self, reads, writes, eng=None):
        deps = []
        for r in reads:
            if r.w is not None:
                deps.append(r.w)
            if r.x:
                for k, v in r.r.items():
                    if k != eng:
                        deps.append((k, v))
        for w in writes:
            if w.w is not None:
                deps.append(w.w)
            deps.extend(w.r.items())
        return deps

    def _emit_waits(self, eng, deps, skip_self_pe=False):
        need = {}
        kn = self.known[eng]
        for (k, v) in deps:
            if skip_self_pe and k == "pe" and eng == "pe":
                continue
            if kn.get(k, 0) >= v:
                continue
            if need.get(k, 0) < v:
                need[k] = v
        for k, v in need.items():
            kn[k] = v
            sem = self.sems[k]
            self.q[eng].append(lambda e, sem=sem, v=v: e.wait_ge(sem, v))

    def _mark(self, tok, reads, writes):
        k, v = tok
        for r in reads:
            if r.r.get(k, 0) < v:
                r.r[k] = v
        for w in writes:
            w.w = tok
            w.r = {}

    def op(self, eng, fn, reads=(), writes=(), inc=True):
        fn = freeze(fn)
        if DEBUG_LINES:
            import sys as _s
            fr = _s._getframe(1)
            if fr.f_code.co_name in ("mm", "tr", "evac"):
                fr = fr.f_back
            self.q[eng].append(("L", fr.f_lineno))
        deps = self._deps(reads, writes, eng)
        self._emit_waits(eng, deps, skip_self_pe=True)
        self.ninst += 1
        if inc:
            self.cnt[eng] += 1
            v = self.cnt[eng]
            sem = self.sems[eng]
            self.q[eng].append(lambda e, fn=fn, sem=sem: fn(e).then_inc(sem, 1))
            tok = (eng, v)
        else:
            self.q[eng].append(lambda e, fn=fn: fn(e))
            tok = (eng, self.cnt[eng] + 1)
        self._mark(tok, reads, writes)
        return tok

    def dma(self, qn, fn, reads=(), writes=()):
        fn = freeze(fn)
        ks, rr = self.dq[qn]
        k = ks[rr]
        self.dq[qn][1] = (rr + 1) % len(ks)
        deps = self._deps(reads, writes)
        if self.cnt[k] > 0:
            deps.append((k, self.cnt[k]))
        self._emit_waits(qn, deps)
        self.cnt[k] += 16
        v = self.cnt[k]
        sem = self.sems[k]
        self.ninst += 1
        self.q[qn].append(lambda e, fn=fn, sem=sem: fn(e).then_inc(sem, 16))
        tok = (k, v)
        self._mark(tok, reads, writes)
        return tok

    def barrier(self):
        for eng in ENGS:
            deps = [(k, v) for k, v in self.cnt.items() if v > 0 and k != eng]
            self._emit_waits(eng, deps)

    def finish(self):
        self.barrier()
        nc = self.nc
        q = self.q
        with nc.Block() as block:
            def run(e, lst, nm):
                line = None
                for f in lst:
                    if isinstance(f, tuple):
                        line = f[1]
                        continue
                    if DEBUG_LINES:
                        b = nc.next_id()
                        f(e)
                        a = nc.next_id()
                        for t in DEBUG_LINES:
                            if b <= t < a + 1:
                                print("DEBUG_INST", t, nm, "line", line, "ids", b, a)
                    else:
                        f(e)

            @block.tensor
            def _(e):
                run(e, q["pe"], "pe")

            @block.scalar
            def _(e):
                run(e, q["act"], "act")

            @block.vector
            def _(e):
                run(e, q["dve"], "dve")

            @block.gpsimd
            def _(e):
                run(e, q["pool"], "pool")

            @block.sync
            def _(e):
                run(e, q["sp"], "sp")
        while self.scopes:
            self.pop()
        self.es.close()


class Rot:
    def __init__(self, items):
        self.items = items
        self.i = 0

    def next(self):
        it = self.items[self.i]
        self.i = (self.i + 1) % len(self.items)
        return it


def in_blocks():
    blks = []
    for i in range(8):
        blks.append(("z", i * 512, 512, i))
    for i in range(12):
        blks.append(("xbc", 4096 + i * 512, 512, i))
    blks.append(("dt", 10240, 128, 0))
    for i in range(4):
        blks.append(("q", 10368 + i * 512, 512, i))
    for i in range(4):
        blks.append(("k", 12416 + i * 512, 512, i))
    for i in range(8):
        blks.append(("v", 14464 + i * 512, 512, i))
    for i in range(8):
        blks.append(("gret", 18560 + i * 512, 512, i))
    for i in range(8):
        blks.append(("gates", 22656 + i * 512, 512, i))
    return blks


def ret_consts():
    h = np.arange(8, dtype=np.float64)
    lg_f = np.log1p(-np.exp2(-5.0 - h))
    lg_b = lg_f[::-1].copy()
    idx = np.arange(128, dtype=np.float64)
    dij = idx[None, :] - idx[:, None]
    dec = np.zeros((128, 8, 128), np.float64)
    for hh in range(8):
        f = np.where(dij >= 0, np.exp(np.maximum(dij, 0) * lg_f[hh]), 0.0)
        b = np.where(dij < 0, np.exp(np.maximum(-dij, 0) * lg_b[hh]), 0.0)
        dec[:, hh, :] = f + b
    xif = np.exp((idx[None, :] + 1.0) * lg_f[:, None])
    xib = np.exp((128.0 - idx[None, :]) * lg_b[:, None])
    zf = np.exp((127.0 - idx[:, None]) * lg_f[None, :])
    zb = np.exp((idx[:, None]) * lg_b[None, :])
    gf = np.exp(128.0 * lg_f)
    gb = np.exp(128.0 * lg_b)
    xi = np.stack([xif, xib], 0)
    xi_bc = np.broadcast_to(xi[None], (128, 2, 8, 128)).astype(np.float32).copy()
    zeta = np.stack([zf, zb], 1).astype(np.float32)
    return dec.astype(np.float32), xi_bc, zeta, gf, gb


def build(NSEG, SEGLEN, dump=(), upto=99):
    NT = NSEG * SEGLEN
    NTT = NT // 512
    NCH = NT // 128
    TC = 512
    nc = bass.Bass("TRN2", target_bir_lowering=False)

    def din(name, shape, dt=F32):
        return nc.dram_tensor(name, list(shape), dt, kind="ExternalInput").ap()

    def dscr(name, shape, dt=BF16):
        kind = "ExternalOutput" if name in dump else "Internal"
        return nc.dram_tensor(name, list(shape), dt, kind=kind).ap()

    x_in = din("x", [NT, D])
    cT_in = din("cT", [128, KD, NSEG])
    cont_in = din("cont", [128, 1])
    cos_in = din("cosT", [128, NT])
    sin_in = din("sinT", [128, NT])
    dec_in = din("dec", [128, 8, 128])
    xi_in = din("xi", [128, 2, 8, 128])
    zeta_in = din("zeta", [128, 2, 8])
    w_ada = din("w_ada", [D, 6 * D])
    b_adaT = din("b_adaT", [128, 96])
    norm1T = din("norm1T", [128, KD])
    w_in = din("w_in", [D, E_IN])
    conv_wT = din("conv_wT", [128, 48, 5])
    conv_bT = din("conv_bT", [128, 48])
    dt_bias = din("dt_bias", [1, 128])
    a_log = din("a_log", [1, 128])
    ssd_d = din("ssd_d", [1, 64])
    ssd_normT = din("ssd_normT", [128, 32])
    w_pa = din("w_pa", [4096, D])
    w_pb = din("w_pb", [4096, D])
    w_out = din("w_out", [D, D])
    norm2T = din("norm2T", [128, KD])
    w_up = din("w_up", [D, 2 * DFF])
    fconv_wT = din("fconv_wT", [128, NF, 3])
    fconv_bT = din("fconv_bT", [128, NF])
    w_down = din("w_down", [DFF, D])
    norm_fT = din("norm_fT", [128, KD])
    y_out = nc.dram_tensor("y", [NT, D], F32, kind="ExternalOutput").ap()

    WIN = dscr("WIN", [D, E_IN])
    WPA = dscr("WPA", [4096, D])
    WPB = dscr("WPB", [4096, D])
    WOUT = dscr("WOUT", [D, D])
    WUP = dscr("WUP", [D, 2 * DFF])
    WDN = dscr("WDN", [DFF, D])
    XT = dscr("XT", [D, NT], F32)
    Z = dscr("Z", [NT, 4096])
    XBC = dscr("XBC", [6144, NT])
    DTR = dscr("DTR", [NT, 128], F32)
    QT = dscr("QT", [2048, NT])
    KT = dscr("KT", [2048, NT])
    V = dscr("V", [NT, 4096])
    GRET = dscr("GRET", [NT, 4096])
    GATES = dscr("GATES", [4096, NT])
    XS = dscr("XS", [NT, 4096])
    BTM = dscr("BTM", [NT, 1024])
    BT = dscr("BT", [1024, NT])
    CT = dscr("CT", [1024, NT])
    SBS = dscr("SBS", [NCH, 128, 4096])
    SBR = dscr("SBR", [NCH, 128, 8192])
    YAT = dscr("YAT", [4096, NT])
    OT = dscr("OT", [4096, NT])
    H = dscr("H", [D, NT], F32)
    AT = dscr("AT", [DFF, NT])
    GT = dscr("GT", [DFF, NT])

    P = Prog(nc)
    dec_c, xi_c, zeta_c, gch_f, gch_b = ret_consts()

    ident_f = P.sbuf("ident_f", [128, 128], F32)
    ident_b = P.sbuf("ident_b", [128, 128], BF16)
    ones_f = P.sbuf("ones_f", [128, 128], F32)
    Rc = Res("consts")
    P.op("pool", lambda e: e.memset(ident_f[:], 1.0), writes=[Rc])
    P.op("pool", lambda e: e.affine_select(out=ident_f[:], in_=ident_f[:], pattern=[[-1, 128]],
                                           compare_op=ALU.is_equal, fill=0.0, base=0, channel_multiplier=1),
         reads=[Rc], writes=[Rc])
    P.op("pool", lambda e: e.tensor_copy(out=ident_b[:], in_=ident_f[:]), reads=[Rc], writes=[Rc])
    P.op("pool", lambda e: e.memset(ones_f[:], 1.0), writes=[Rc])
    epsc = P.sbuf("epsc", [128, 1], F32)
    P.op("pool", lambda e: e.memset(epsc[:], EPS), writes=[Rc])
    cont = P.sbuf("cont", [128, 1], F32)
    P.dma("sp", lambda e: e.dma_start(out=cont[:], in_=cont_in), writes=[Rc])
    modT = P.sbuf("modT", [128, 96, NSEG], F32)
    scale1 = P.sbuf("scale1", [128, KD, NSEG], F32)
    scale2 = P.sbuf("scale2", [128, KD, NSEG], F32)
    n1T = P.sbuf("n1T_c", [128, KD], F32)
    n2T = P.sbuf("n2T_c", [128, KD], F32)
    nfT = P.sbuf("nfT_c", [128, KD], F32)
    Rmod = Res("mod")
    P.dma("sp", lambda e: e.dma_start(out=n1T[:], in_=norm1T), writes=[Rmod])
    P.dma("sp", lambda e: e.dma_start(out=n2T[:], in_=norm2T), writes=[Rmod])
    P.dma("sp", lambda e: e.dma_start(out=nfT[:], in_=norm_fT), writes=[Rmod])
    P.barrier()

    banks = []
    for i in range(8):
        t = P.psum("bank%d" % i, [128, 512], F32)
        banks.append((t, Res("bank%d" % i, excl=True)))
    PB = Rot(banks[:7])
    ACCB = banks[7]

    evac_flip = [0]

    def evac(out, in_, reads, writes):
        evac_flip[0] ^= 1
        if evac_flip[0]:
            P.op("act", lambda e: e.activation(out=out, in_=in_, func=AF.Copy), reads, writes)
        else:
            P.op("dve", lambda e: e.tensor_copy(out=out, in_=in_), reads, writes)

    def mm(out, lhsT, rhs, start, stop, reads, writes, inc):
        P.op("pe", lambda e: e.matmul(out, lhsT=lhsT, rhs=rhs, start=start, stop=stop), reads, writes, inc)

    def tr(out, in_, idn, reads, writes, inc=True):
        P.op("pe", lambda e: e.transpose(out, in_, idn), reads, writes, inc)

    def phase0():
        P.push()
        def cast(dst, src, rows, cols, cstep):
            for r in range(0, rows, 128):
                for c0 in range(0, cols, cstep):
                    c1 = min(cols, c0 + cstep)
                    P.dma("pool", lambda e, r=r, c0=c0, c1=c1: e.dma_start(out=dst[r:r + 128, c0:c1], in_=src[r:r + 128, c0:c1]))
        cast(WIN, w_in, D, E_IN, 6688)
        cast(WPB, w_pb, 4096, D, 2048)
        cast(WOUT, w_out, D, D, 2048)
        cast(WUP, w_up, D, 2 * DFF, 5632)
        cast(WDN, w_down, DFF, D, 2048)
        snT = P.sbuf("snT", [128, 32], F32)
        Rsn = Res()
        P.dma("sp", lambda e: e.dma_start(out=snT[:], in_=ssd_normT), writes=[Rsn])
        stg = [(P.sbuf("wpa_f", [128, D], F32), Res()) for _ in range(2)]
        stgb = [(P.sbuf("wpa_b", [128, D], BF16), Res()) for _ in range(2)]
        for t in range(32):
            (a, Ra), (b, Rb) = stg[t % 2], stgb[t % 2]
            P.dma("sp", lambda e, a=a, t=t: e.dma_start(out=a[:], in_=w_pa[t * 128:(t + 1) * 128, :]), writes=[Ra])
            P.op("act", lambda e, a=a, b=b, t=t: e.activation(out=b[:], in_=a[:], func=AF.Copy, scale=snT[:, t:t + 1]),
                 reads=[Ra, Rsn], writes=[Rb])
            P.dma("sp", lambda e, b=b, t=t: e.dma_start(out=WPA[t * 128:(t + 1) * 128, :], in_=b[:]), reads=[Rb])
        cT = P.sbuf("cT", [128, KD, NSEG], F32)
        sc = P.sbuf("sc", [128, KD, NSEG], F32)
        baT = P.sbuf("baT", [128, 96], F32)
        Rct, Rsc, Rba = Res(), Res(), Res()
        P.dma("sp", lambda e: e.dma_start(out=cT[:], in_=cT_in), writes=[Rct])
        P.dma("sp", lambda e: e.dma_start(out=baT[:], in_=b_adaT), writes=[Rba])
        P.op("act", lambda e: e.activation(out=sc[:], in_=cT[:], func=AF.Silu), reads=[Rct], writes=[Rsc])
        wa = [(P.sbuf("wada", [128, KD, 512], F32), Res()) for _ in range(2)]
        for blk in range(24):
            w, Rw = wa[blk % 2]
            P.dma("sp", lambda e, w=w, blk=blk: e.dma_start(
                out=w[:], in_=w_ada[:, blk * 512:(blk + 1) * 512].rearrange("(k p) n -> p k n", p=128)), writes=[Rw])
            bk, Rbk = PB.next()
            for j in range(4):
                for k in range(KD):
                    mm(bk[:, j * NSEG:(j + 1) * NSEG], w[:, k, j * 128:(j + 1) * 128], sc[:, k, :],
                       k == 0, k == KD - 1, [Rw, Rsc], [Rbk], k == KD - 1)
            for j in range(4):
                jj = blk * 4 + j
                P.op("dve", lambda e, bk=bk, j=j, jj=jj: e.tensor_scalar(
                    out=modT[:, jj, :], in0=bk[:, j * NSEG:(j + 1) * NSEG], scalar1=baT[:, jj:jj + 1], scalar2=None,
                    op0=ALU.add), reads=[Rbk, Rba], writes=[Rmod])
        for (dst, nT, off) in ((scale1, n1T, 16), (scale2, n2T, 64)):
            for s in range(NSEG):
                P.op("dve", lambda e, dst=dst, nT=nT, off=off, s=s: e.scalar_tensor_tensor(
                    out=dst[:, :, s], in0=modT[:, off:off + 16, s], scalar=1.0, in1=nT[:, :],
                    op0=ALU.add, op1=ALU.mult), reads=[Rmod], writes=[Rmod])
        P.barrier()
        P.pop()

    def phaseA():
        P.push()
        xin = [(P.sbuf("xin", [128, D], F32), Res()) for _ in range(2)]
        xT = P.sbuf("xT", [128, KD, 512], F32)
        RxT = Res()
        nT = P.sbuf("nT", [128, KD, 512], BF16)
        RnT = Res()
        sq = [(P.sbuf("sq", [128, 512], F32), Res()) for _ in range(2)]
        rstd = P.sbuf("rstd", [128, 512], F32)
        Rrstd = Res()
        tmp = [(P.sbuf("tmpA", [128, 512], F32), Res()) for _ in range(2)]
        cs = P.sbuf("cosA", [128, 512], F32)
        sn = P.sbuf("sinA", [128, 512], F32)
        Rcs = Res()
        wb = [(P.sbuf("wblk", [128, KD, 512], BF16), Res()) for _ in range(3)]
        WB = Rot(wb)
        stg = Rot([(P.sbuf("stgA", [128, 512], BF16), Res()) for _ in range(6)])
        stgf = Rot([(P.sbuf("stgAf", [128, 128], F32), Res()) for _ in range(2)])
        rt = [(P.sbuf("ropeT", [128, 512], F32), Res()) for _ in range(4)]
        blks = in_blocks()
        for tt in range(NTT):
            t0 = tt * 512
            s = t0 // SEGLEN
            P.dma("sp", lambda e, t0=t0: e.dma_start(out=cs[:], in_=cos_in[:, t0:t0 + 512]), writes=[Rcs])
            P.dma("sp", lambda e, t0=t0: e.dma_start(out=sn[:], in_=sin_in[:, t0:t0 + 512]), writes=[Rcs])
            for sb in range(4):
                xi_, Rxi = xin[sb % 2]
                P.dma("sp", lambda e, xi_=xi_, r0=t0 + sb * 128: e.dma_start(out=xi_[:], in_=x_in[r0:r0 + 128, :]), writes=[Rxi])
                for k4 in range(4):
                    bk, Rbk = PB.next()
                    for kk in range(4):
                        k = k4 * 4 + kk
                        tr(bk[:, kk * 128:(kk + 1) * 128], xi_[:, k * 128:(k + 1) * 128], ident_f[:], [Rxi, Rc], [Rbk], kk == 3)
                    evac(xT[:, k4 * 4:(k4 + 1) * 4, sb * 128:(sb + 1) * 128],
                         bk[:].rearrange("p (k n) -> p k n", k=4), [Rbk], [RxT])
            for k in range(KD):
                P.dma("pool", lambda e, k=k, t0=t0: e.dma_start(out=XT[k * 128:(k + 1) * 128, t0:t0 + 512], in_=xT[:, k, :]), reads=[RxT])
            bss, Rbss = ACCB
            for k in range(KD):
                q_, Rq = sq[k % 2]
                P.op("act", lambda e, q_=q_, k=k: e.activation(out=q_[:], in_=xT[:, k, :], func=AF.Square), reads=[RxT], writes=[Rq])
                mm(bss[:], ones_f[:], q_[:], k == 0, k == KD - 1, [Rq, Rc], [Rbss], True)
            P.op("act", lambda e, bss=bss: e.activation(out=rstd[:], in_=bss[:], func=AF.Ln, bias=epsc[:, 0:1], scale=1.0 / D), reads=[Rbss], writes=[Rrstd])
            P.op("act", lambda e, bss=bss: e.activation(out=rstd[:], in_=rstd[:], func=AF.Exp, scale=-0.5), reads=[Rrstd], writes=[Rrstd])
            for k in range(KD):
                tp, Rtp = tmp[k % 2]
                P.op("dve", lambda e, tp=tp, k=k, s=s: e.scalar_tensor_tensor(
                    out=tp[:], in0=xT[:, k, :], scalar=scale1[:, k, s:s + 1], in1=rstd[:], op0=ALU.mult, op1=ALU.mult),
                    reads=[RxT, Rrstd, Rmod], writes=[Rtp])
                P.op("act", lambda e, tp=tp, k=k, s=s: e.activation(
                    out=nT[:, k, :], in_=tp[:], func=AF.Identity, bias=modT[:, k, s:s + 1], scale=1.0),
                    reads=[Rtp, Rmod], writes=[RnT])
            for (kind, e0, ncol, bi) in blks:
                w, Rw = WB.next()
                P.dma("sp", lambda e, w=w, e0=e0, ncol=ncol: e.dma_start(
                    out=w[:, :, 0:ncol], in_=WIN[:, e0:e0 + ncol].rearrange("(k p) n -> p k n", p=128)), writes=[Rw])
                if kind in ("z", "v", "gret", "dt"):
                    dst = {"z": Z, "v": V, "gret": GRET, "dt": DTR}[kind]
                    for sb in range(4):
                        bk, Rbk = PB.next()
                        for k in range(KD):
                            mm(bk[:, 0:ncol], nT[:, k, sb * 128:(sb + 1) * 128], w[:, k, 0:ncol], k == 0, k == KD - 1,
                               [RnT, Rw], [Rbk], k == KD - 1)
                        r0 = t0 + sb * 128
                        if kind == "dt":
                            st, Rst = stgf.next()
                            evac(st[:], bk[:, 0:128], [Rbk], [Rst])
                            P.dma("pool", lambda e, st=st, r0=r0: e.dma_start(out=DTR[r0:r0 + 128, :], in_=st[:]), reads=[Rst])
                        else:
                            st, Rst = stg.next()
                            evac(st[:], bk[:], [Rbk], [Rst])
                            P.dma("pool", lambda e, st=st, r0=r0, dst=dst, c0=bi * 512: e.dma_start(
                                out=dst[r0:r0 + 128, c0:c0 + 512], in_=st[:]), reads=[Rst])
                elif kind in ("xbc", "gates"):
                    dst = XBC if kind == "xbc" else GATES
                    for j in range(4):
                        bk, Rbk = PB.next()
                        for k in range(KD):
                            mm(bk[:], w[:, k, j * 128:(j + 1) * 128], nT[:, k, :], k == 0, k == KD - 1,
                               [RnT, Rw], [Rbk], k == KD - 1)
                        st, Rst = stg.next()
                        evac(st[:], bk[:], [Rbk], [Rst])
                        r0 = bi * 512 + j * 128
                        P.dma("pool", lambda e, st=st, r0=r0, dst=dst, t0=t0: e.dma_start(
                            out=dst[r0:r0 + 128, t0:t0 + 512], in_=st[:]), reads=[Rst])
                else:
                    dst = QT if kind == "q" else KT
                    sc_ = 1.0 if kind == "q" else 0.0625
                    for hh in range(2):
                        pr = []
                        for j in range(2):
                            bk, Rbk = PB.next()
                            jj = hh * 2 + j
                            for k in range(KD):
                                mm(bk[:], w[:, k, jj * 128:(jj + 1) * 128], nT[:, k, :], k == 0, k == KD - 1,
                                   [RnT, Rw], [Rbk], k == KD - 1)
                            pr.append((bk, Rbk))
                        (b1, R1), (b2, R2) = pr
                        (ta, Ra), (tb, Rb), (tc_, Rcc), (td, Rd) = rt
                        P.op("dve", lambda e, b1=b1, ta=ta: e.scalar_tensor_tensor(out=ta[:], in0=b1[:], scalar=sc_, in1=cs[:], op0=ALU.mult, op1=ALU.mult), reads=[R1, Rcs], writes=[Ra])
                        P.op("dve", lambda e, b2=b2, tb=tb: e.scalar_tensor_tensor(out=tb[:], in0=b2[:], scalar=sc_, in1=sn[:], op0=ALU.mult, op1=ALU.mult), reads=[R2, Rcs], writes=[Rb])
                        P.op("dve", lambda e, b1=b1, tc_=tc_: e.scalar_tensor_tensor(out=tc_[:], in0=b1[:], scalar=sc_, in1=sn[:], op0=ALU.mult, op1=ALU.mult), reads=[R1, Rcs], writes=[Rcc])
                        P.op("dve", lambda e, b2=b2, td=td: e.scalar_tensor_tensor(out=td[:], in0=b2[:], scalar=sc_, in1=cs[:], op0=ALU.mult, op1=ALU.mult), reads=[R2, Rcs], writes=[Rd])
                        s1, Rs1 = stg.next()
                        P.op("pool", lambda e, s1=s1, ta=ta, tb=tb: e.tensor_tensor(out=s1[:], in0=ta[:], in1=tb[:], op=ALU.subtract), reads=[Ra, Rb], writes=[Rs1])
                        s2, Rs2 = stg.next()
                        P.op("pool", lambda e, s2=s2, tc_=tc_, td=td: e.tensor_tensor(out=s2[:], in0=tc_[:], in1=td[:], op=ALU.add), reads=[Rcc, Rd], writes=[Rs2])
                        r0 = bi * 512 + hh * 256
                        P.dma("pool", lambda e, s1=s1, r0=r0, dst=dst, t0=t0: e.dma_start(out=dst[r0:r0 + 128, t0:t0 + 512], in_=s1[:]), reads=[Rs1])
                        P.dma("pool", lambda e, s2=s2, r0=r0, dst=dst, t0=t0: e.dma_start(out=dst[r0 + 128:r0 + 256, t0:t0 + 512], in_=s2[:]), reads=[Rs2])
        P.barrier()
        P.pop()

    def phaseA2():
        P.push()
        cw = P.sbuf("cw", [128, 48, 5], F32)
        cb = P.sbuf("cb", [128, 48], F32)
        Rcw = Res()
        P.dma("sp", lambda e: e.dma_start(out=cw[:], in_=conv_wT), writes=[Rcw])
        P.dma("sp", lambda e: e.dma_start(out=cb[:], in_=conv_bT), writes=[Rcw])
        xb = Rot([(P.sbuf("xbA2", [128, 516], BF16), Res()) for _ in range(6)])
        acc = Rot([(P.sbuf("accA2", [128, 512], F32), Res()) for _ in range(4)])
        grp = Rot([(P.sbuf("grpA2", [128, 8, 512], BF16), Res()) for _ in range(2)])
        stg = Rot([(P.sbuf("stgA2", [128, 1024], BF16), Res()) for _ in range(3)])
        def a2_s1(g, f8, t0, lo, hi):
            f = g * 8 + f8
            x_, Rx = xb.next()
            if lo > t0 - 2:
                P.op("pool", lambda e, x_=x_: e.memset(x_[:, 0:2], 0.0), writes=[Rx])
            if hi < t0 + 514:
                P.op("pool", lambda e, x_=x_: e.memset(x_[:, 514:516], 0.0), writes=[Rx])
            P.dma("sp", lambda e, x_=x_, f=f, lo=lo, hi=hi, t0=t0: e.dma_start(
                out=x_[:, lo - (t0 - 2):hi - (t0 - 2)], in_=XBC[f * 128:(f + 1) * 128, lo:hi]), writes=[Rx])
            if t0 % SEGLEN == 0 and t0 > 0:
                P.op("act", lambda e, x_=x_: e.activation(out=x_[:, 0:2], in_=x_[:, 0:2], func=AF.Copy, scale=cont[:, 0:1]), reads=[Rx, Rc], writes=[Rx])
            if (t0 + 512) % SEGLEN == 0 and t0 + 512 < NT:
                P.op("act", lambda e, x_=x_: e.activation(out=x_[:, 514:516], in_=x_[:, 514:516], func=AF.Copy, scale=cont[:, 0:1]), reads=[Rx, Rc], writes=[Rx])
            a_, Ra = acc.next()
            P.op("act", lambda e, a_=a_, x_=x_, f=f: e.activation(out=a_[:], in_=x_[:, 0:512], func=AF.Identity,
                                                               bias=cb[:, f:f + 1], scale=cw[:, f, 0:1]), reads=[Rx, Rcw], writes=[Ra])
            for k in range(1, 5):
                P.op("dve", lambda e, a_=a_, x_=x_, f=f, k=k: e.scalar_tensor_tensor(
                    out=a_[:], in0=x_[:, k:k + 512], scalar=cw[:, f, k:k + 1], in1=a_[:], op0=ALU.mult, op1=ALU.add),
                    reads=[Rx, Rcw, Ra], writes=[Ra])
            return (f8, a_, Ra)

        for tt in range(NTT):
            t0 = tt * 512
            lo = max(t0 - 2, 0)
            hi = min(t0 + 514, NT)
            for g in range(6):
                gt, Rgt = grp.next()
                prev = None
                for f8 in range(9):
                    if f8 < 8:
                        cur = a2_s1(g, f8, t0, lo, hi)
                    if prev is not None:
                        pf8, pa_, pRa = prev
                        P.op("act", lambda e, pa_=pa_, gt=gt, pf8=pf8: e.activation(out=gt[:, pf8, :], in_=pa_[:], func=AF.Silu), reads=[pRa], writes=[Rgt])
                    prev = cur if f8 < 8 else None
                if g >= 4:
                    dstT = BT if g == 4 else CT
                    for f8 in range(8):
                        P.dma("pool", lambda e, gt=gt, f8=f8, dstT=dstT, t0=t0: e.dma_start(
                            out=dstT[f8 * 128:(f8 + 1) * 128, t0:t0 + 512], in_=gt[:, f8, :]), reads=[Rgt])
                if g <= 4:
                    for sb in range(4):
                        bk, Rbk = PB.next()
                        bkb = bk[:].bitcast(BF16)
                        for f8 in range(8):
                            tr(bkb[:, f8 * 128:(f8 + 1) * 128], gt[:, f8, sb * 128:(sb + 1) * 128], ident_b[:], [Rgt, Rc], [Rbk], f8 == 7)
                        st, Rst = stg.next()
                        evac(st[:], bkb, [Rbk], [Rst])
                        r0 = t0 + sb * 128
                        if g < 4:
                            P.dma("pool", lambda e, st=st, r0=r0, g=g: e.dma_start(out=XS[r0:r0 + 128, g * 1024:(g + 1) * 1024], in_=st[:]), reads=[Rst])
                        else:
                            P.dma("pool", lambda e, st=st, r0=r0: e.dma_start(out=BTM[r0:r0 + 128, :], in_=st[:]), reads=[Rst])
        P.barrier()
        P.pop()

    def scans():
        P.push()
        Rk = Res("scanconst")
        dtb = P.sbuf("dtb", [128, 128], F32)
        Abc = P.sbuf("Abc", [128, 128], F32)
        Dbc = P.sbuf("Dbc", [128, 64], F32)
        P.dma("sp", lambda e: e.dma_start(out=dtb[:], in_=dt_bias.partition_broadcast(128)), writes=[Rk])
        P.dma("sp", lambda e: e.dma_start(out=Abc[:], in_=a_log.partition_broadcast(128)), writes=[Rk])
        P.dma("sp", lambda e: e.dma_start(out=Dbc[:], in_=ssd_d.partition_broadcast(128)), writes=[Rk])
        P.op("act", lambda e: e.activation(out=Abc[:], in_=Abc[:], func=AF.Exp), reads=[Rk], writes=[Rk])
        P.op("dve", lambda e: e.tensor_scalar(out=Abc[:], in0=Abc[:], scalar1=-1.0, scalar2=None, op0=ALU.mult), reads=[Rk], writes=[Rk])
        dec = P.sbuf("dec", [128, 8, 128], F32)
        xi = P.sbuf("xi", [128, 2, 8, 128], F32)
        zeta = P.sbuf("zeta", [128, 2, 8], F32)
        P.dma("sp", lambda e: e.dma_start(out=dec[:], in_=dec_in), writes=[Rk])
        P.dma("sp", lambda e: e.dma_start(out=xi[:], in_=xi_in), writes=[Rk])
        P.dma("sp", lambda e: e.dma_start(out=zeta[:], in_=zeta_in), writes=[Rk])
        UT = P.sbuf("UT", [128, 128], F32)
        LT = P.sbuf("LT", [128, 128], F32)
        NMf = P.sbuf("NMf", [128, 128], BF16)
        NMb = P.sbuf("NMb", [128, 128], BF16)
        P.op("pool", lambda e: e.memset(UT[:], 1.0), writes=[Rk])
        P.op("pool", lambda e: e.affine_select(out=UT[:], in_=UT[:], pattern=[[1, 128]], compare_op=ALU.is_ge, fill=0.0, base=0, channel_multiplier=-1), reads=[Rk], writes=[Rk])
        P.op("pool", lambda e: e.memset(LT[:], 1.0), writes=[Rk])
        P.op("pool", lambda e: e.affine_select(out=LT[:], in_=LT[:], pattern=[[-1, 128]], compare_op=ALU.is_ge, fill=0.0, base=0, channel_multiplier=1), reads=[Rk], writes=[Rk])
        P.op("pool", lambda e: e.memset(NMf[:], 0.0), writes=[Rk])
        P.op("pool", lambda e: e.affine_select(out=NMf[:], in_=NMf[:], pattern=[[1, 128]], compare_op=ALU.is_ge, fill=-30000.0, base=0, channel_multiplier=-1), reads=[Rk], writes=[Rk])
        P.op("pool", lambda e: e.memset(NMb[:], 0.0), writes=[Rk])
        P.op("pool", lambda e: e.affine_select(out=NMb[:], in_=NMb[:], pattern=[[-1, 128]], compare_op=ALU.is_ge, fill=-30000.0, base=0, channel_multiplier=1), reads=[Rk], writes=[Rk])

        S32 = P.sbuf("S32", [128, 4096], F32)
        R32 = P.sbuf("R32", [128, 8, 2, 512], F32)
        RS, RR = Res("S"), Res("R")

        def dbl(name, shape, dt, n=2):
            return Rot([(P.sbuf(name, shape, dt), Res()) for _ in range(n)])
        sbf = dbl("sbf", [128, 512], BF16, 3)
        rbf = dbl("rbf", [128, 2, 512], BF16, 3)
        xsT = dbl("xs", [128, 4096], BF16)
        bTM = dbl("btm", [128, 1024], BF16)
        dtr = dbl("dtr", [128, 128], F32)
        vT = dbl("v", [128, 4096], BF16, 1)
        kTt = dbl("kT", [128, 16, 128], BF16)
        dtt = P.sbuf("dtt", [128, 128], F32); Rdt = Res()
        lndt = P.sbuf("lndt", [128, 128], F32)
        dA = P.sbuf("dA", [128, 128], F32); RdA = Res()
        cum = P.sbuf("cum", [128, 128], F32); Rcum = Res()
        biasE = P.sbuf("biasE", [128, 128], F32); RbE = Res()
        etotP = [(P.sbuf("etot", [128, 128], F32), Res()) for _ in range(2)]
        wgtP = [(P.sbuf("wgt", [128, 128], F32), Res()) for _ in range(2)]
        ecum = P.sbuf("ecum", [128, 128], F32); Rec = Res()
        tsm = P.sbuf("tsm", [128, 128], F32); Rts = Res()
        xw = dbl("xw", [128, 512], BF16)
        kz = P.sbuf("kz", [128, 8, 256], BF16); Rkz = Res()

        def small_dt(c, dr, Rdr):
            etot, Ret = etotP[c % 2]
            wgt, Rwg = wgtP[c % 2]
            P.op("dve", lambda e: e.tensor_tensor(out=tsm[:], in0=dr[:], in1=dtb[:], op=ALU.add), reads=[Rdr, Rk], writes=[Rts])
            P.op("act", lambda e: e.activation(out=tsm[:], in_=tsm[:], func=AF.Exp), reads=[Rts], writes=[Rts])
            P.op("act", lambda e: e.activation(out=dtt[:], in_=tsm[:], func=AF.Ln, bias=1.0, scale=1.0), reads=[Rts], writes=[Rdt])
            P.op("act", lambda e: e.activation(out=lndt[:], in_=dtt[:], func=AF.Ln), reads=[Rdt], writes=[Rdt])
            P.op("dve", lambda e: e.tensor_tensor(out=dA[:], in0=dtt[:], in1=Abc[:], op=ALU.mult), reads=[Rdt, Rk], writes=[RdA])
            if SCAN_STOP == 31:
                return
            bk, Rbk = PB.next()
            mm(bk[:, 0:64], UT[:], dA[:, 0:64], True, True, [RdA, Rk], [Rbk], False)
            mm(bk[:, 64:128], LT[:], dA[:, 64:128], True, True, [RdA, Rk], [Rbk], False)
            mm(bk[:, 128:256], ones_f[:], dA[:], True, True, [RdA, Rk, Rc], [Rbk], True)
            if SCAN_STOP == 32:
                return
            P.op("dve", lambda e: e.tensor_copy(out=cum[:], in_=bk[:, 0:128]), reads=[Rbk], writes=[Rcum])
            P.op("dve", lambda e: e.tensor_tensor(out=biasE[:], in0=lndt[:], in1=cum[:], op=ALU.subtract), reads=[Rdt, Rcum], writes=[RbE])
            if SCAN_STOP == 33:
                return
            P.op("act", lambda e: e.activation(out=etot[:], in_=bk[:, 128:256], func=AF.Exp), reads=[Rbk], writes=[Ret])
            P.op("dve", lambda e: e.tensor_tensor(out=tsm[:], in0=bk[:, 128:256], in1=biasE[:], op=ALU.add), reads=[Rbk, RbE, Rts], writes=[Rts])
            P.op("act", lambda e: e.activation(out=wgt[:], in_=tsm[:], func=AF.Exp), reads=[Rts], writes=[Rwg])

        def bc(ap2d, n_outer, n_inner):
            return ap2d.unsqueeze(2).to_broadcast([128, n_outer, n_inner])

        def load_early(c):
            t0 = c * 128
            xs_, Rxs = xsT.next()
            b_, Rb = bTM.next()
            dr, Rdr = dtr.next()
            P.dma("sp", lambda e: e.dma_start(out=xs_[:], in_=XS[t0:t0 + 128, :]), writes=[Rxs])
            P.dma("sp", lambda e: e.dma_start(out=b_[:], in_=BTM[t0:t0 + 128, :]), writes=[Rb])
            P.dma("sp", lambda e: e.dma_start(out=dr[:], in_=DTR[t0:t0 + 128, :]), writes=[Rdr])
            return (xs_, Rxs), (b_, Rb), (dr, Rdr)

        def load_v(c):
            t0 = c * 128
            v_, Rv = vT.next()
            P.dma("sp", lambda e: e.dma_start(out=v_[:], in_=V[t0:t0 + 128, :]), writes=[Rv])
            return (v_, Rv)

        def load_k(c):
            t0 = c * 128
            k_, Rkt = kTt.next()
            P.dma("sp", lambda e: e.dma_start(out=k_[:], in_=KT[:, t0:t0 + 128].rearrange("(t p) n -> p t n", p=128)), writes=[Rkt])
            return (k_, Rkt)

        def state_update(c, d, xs_, Rxs, b_, Rb, v_, Rv, k_, Rkt):
            etot, Ret = etotP[c % 2]
            wgt, Rwg = wgtP[c % 2]
            ho = d * 64
            for g in range(8):
                sl = slice(g * 512, (g + 1) * 512)
                xw_, Rxw = xw.next()
                P.op("dve", lambda e, xw_=xw_, sl=sl, g=g: e.tensor_tensor(
                    out=xw_[:].rearrange("p (h q) -> p h q", q=64), in0=xs_[:, sl].rearrange("p (h q) -> p h q", q=64),
                    in1=bc(wgt[:, ho + g * 8:ho + g * 8 + 8], 8, 64), op=ALU.mult), reads=[Rxs, Rwg], writes=[Rxw])
                bk, Rbk = PB.next()
                mm(bk[:], b_[:, g * 128:(g + 1) * 128], xw_[:], True, True, [Rb, Rxw], [Rbk], True)
                P.op("dve", lambda e, sl=sl, g=g: e.tensor_tensor(
                    out=S32[:, sl].rearrange("p (h q) -> p h q", q=64), in0=S32[:, sl].rearrange("p (h q) -> p h q", q=64),
                    in1=bc(etot[:, ho + g * 8:ho + g * 8 + 8], 8, 64), op=ALU.mult), reads=[Ret, RS], writes=[RS])
                P.op("dve", lambda e, sl=sl, bk=bk: e.tensor_tensor(out=S32[:, sl], in0=S32[:, sl], in1=bk[:], op=ALU.add), reads=[Rbk, RS], writes=[RS])
            for half in range(2):
                bk, Rbk = PB.next()
                bkb = bk[:].bitcast(BF16)
                for t in range(8):
                    tt_ = half * 8 + t
                    tr(bkb[:, t * 128:(t + 1) * 128], k_[:, tt_, :], ident_b[:], [Rkt, Rc], [Rbk], t == 7)
                P.op("dve", lambda e, bkb=bkb, half=half: e.tensor_tensor(
                    out=kz[:, half * 4:(half + 1) * 4, :], in0=bkb.rearrange("p (h q) -> p h q", q=256),
                    in1=bc(zeta[:, d, half * 4:(half + 1) * 4], 4, 256), op=ALU.mult), reads=[Rbk, Rk], writes=[Rkz])
            gch = gch_f if d == 0 else gch_b
            for h in range(8):
                for dt_ in range(2):
                    bk, Rbk = PB.next()
                    mm(bk[:], kz[:, h, dt_ * 128:(dt_ + 1) * 128], v_[:, h * 512:(h + 1) * 512], True, True, [Rkz, Rv], [Rbk], True)
                    P.op("dve", lambda e, h=h, dt_=dt_, bk=bk: e.scalar_tensor_tensor(
                        out=R32[:, h, dt_, :], in0=R32[:, h, dt_, :], scalar=float(gch[h]), in1=bk[:], op0=ALU.mult, op1=ALU.add),
                        reads=[Rbk, RR], writes=[RR])

        def reset_states():
            P.op("dve", lambda e: e.memset(S32[:], 0.0), writes=[RS])
            P.op("dve", lambda e: e.memset(R32[:].rearrange("p a b c -> p (a b c)"), 0.0), writes=[RR])

        def apply_cont():
            P.op("pool", lambda e: e.tensor_scalar(out=S32[:], in0=S32[:], scalar1=cont[:, 0:1], scalar2=None, op0=ALU.mult), reads=[RS, Rc], writes=[RS])
            v2 = R32[:].rearrange("p a b c -> p (a b c)")
            P.op("pool", lambda e: e.tensor_scalar(out=v2, in0=v2, scalar1=cont[:, 0:1], scalar2=None, op0=ALU.mult), reads=[RR, Rc], writes=[RR])

        shadow_eng = ["act"]

        def s_shadow(g):
            t, Rt = sbf.next()
            if shadow_eng[0] == "act":
                P.op("act", lambda e: e.activation(out=t[:], in_=S32[:, g * 512:(g + 1) * 512], func=AF.Copy), reads=[RS], writes=[Rt])
            else:
                P.op("pool", lambda e: e.tensor_copy(out=t[:], in_=S32[:, g * 512:(g + 1) * 512]), reads=[RS], writes=[Rt])
            return t, Rt

        def r_shadow(h):
            t, Rt = rbf.next()
            if shadow_eng[0] == "act":
                P.op("act", lambda e: e.activation(out=t[:], in_=R32[:, h], func=AF.Copy), reads=[RR], writes=[Rt])
            else:
                P.op("pool", lambda e: e.tensor_copy(out=t[:], in_=R32[:, h]), reads=[RR], writes=[Rt])
            return t, Rt

        if SCAN_STOP == 1:
            P.barrier(); P.pop(); return
        reset_states()
        def r0_load(c):
            ops = dict(c=c)
            ops['xs'], ops['b'], ops['dr'] = load_early(c)
            ops['k'] = load_k(c)
            small_dt(c, *ops['dr'])
            return ops

        nxt = r0_load(NCH - 1)
        for c in range(NCH - 1, -1, -1):
            cur = nxt
            nxt = r0_load(c - 1) if c > 0 else None
            if (c + 1) * 128 % SEGLEN == 0 and c != NCH - 1:
                apply_cont()
            for g in range(8):
                t, Rt = s_shadow(g)
                P.dma("pool", lambda e, t=t, g=g: e.dma_start(out=SBS[c, :, g * 512:(g + 1) * 512], in_=t[:]), reads=[Rt])
            for h in range(8):
                t, Rt = r_shadow(h)
                P.dma("pool", lambda e, t=t, h=h: e.dma_start(out=SBR[c, :, h * 1024:(h + 1) * 1024], in_=t[:].rearrange("p a b -> p (a b)")), reads=[Rt])
            if c > 0:
                v_, Rv = load_v(c)
                (xs_, Rxs), (b_, Rb), (k_, Rkt) = cur['xs'], cur['b'], cur['k']
                state_update(c, 1, xs_, Rxs, b_, Rb, v_, Rv, k_, Rkt)
        P.barrier()
        if SCAN_STOP <= 4:
            P.pop(); return

        reset_states()
        shadow_eng[0] = "pool"
        btT = dbl("bt", [128, 8, 128], BF16)
        ctT = dbl("ct", [128, 8, 128], BF16)
        qTt = dbl("qT", [128, 16, 128], BF16, 1)
        zT = dbl("z", [128, 4096], BF16, 1)
        grT = dbl("gr", [128, 4096], BF16, 1)
        sbin = dbl("sbin", [128, 512], BF16, 3)
        rbin = dbl("rbin", [128, 2, 512], BF16, 3)
        cbT = P.sbuf("cbT", [128, 8, 128], BF16); Rcb = Res()
        hi = P.sbuf("hi", [128, 128], BF16)
        lo = P.sbuf("lo", [128, 128], BF16); Rhl = Res()
        Eb = dbl("E", [128, 2, 4, 128], BF16)
        Es = dbl("Es", [128, 4, 128], BF16)
        MT = dbl("MT", [128, 4, 128], BF16)
        xsD = dbl("xsD", [128, 512], BF16)
        ysb = dbl("ysb", [128, 512], F32)
        t512 = dbl("t512", [128, 512], F32)
        szr = dbl("sz", [128, 512], BF16)
        junk = P.sbuf("junk", [128, 512], F32); Rjk = Res()
        ss = P.sbuf("ss", [128, 16], F32); Rss = Res()
        yab = P.sbuf("yab", [128, 4096], BF16); Ryab = Res()
        MTr = P.sbuf("MTr", [128, 8, 128], BF16); RMr = Res()
        qxr = dbl("qx", [128, 2, 2, 128], BF16)
        stg = dbl("stgF", [128, 8, 128], BF16, 3)
        def chunk_gen(c):
            t0 = c * 128
            (xs_, Rxs), (b_, Rb), (dr, Rdr) = load_early(c)
            bt_, Rbt = btT.next(); ct_, Rct = ctT.next(); z_, Rz = zT.next()
            P.dma("sp", lambda e: e.dma_start(out=bt_[:], in_=BT[:, t0:t0 + 128].rearrange("(t p) n -> p t n", p=128)), writes=[Rbt])
            P.dma("sp", lambda e: e.dma_start(out=ct_[:], in_=CT[:, t0:t0 + 128].rearrange("(t p) n -> p t n", p=128)), writes=[Rct])
            P.dma("sp", lambda e: e.dma_start(out=z_[:], in_=Z[t0:t0 + 128, :]), writes=[Rz])
            yield "E"
            small_dt(c, dr, Rdr)
            P.op("act", lambda e: e.activation(out=ecum[:], in_=cum[:], func=AF.Exp), reads=[Rcum], writes=[Rec])
            bk, Rbk = PB.next()
            mm(bk[0:64, 0:128], dA[:, 0:64], UT[:], True, True, [RdA, Rk], [Rbk], False)
            mm(bk[64:128, 0:128], dA[:, 64:128], LT[:], True, True, [RdA, Rk], [Rbk], True)
            P.op("act", lambda e, bk=bk: e.activation(out=hi[:], in_=bk[:, 0:128], func=AF.Copy), reads=[Rbk], writes=[Rhl])
            P.op("dve", lambda e, bk=bk: e.tensor_tensor(out=lo[:], in0=bk[:, 0:128], in1=hi[:], op=ALU.subtract), reads=[Rbk, Rhl], writes=[Rhl])
            for half in range(2):
                bk, Rbk = PB.next()
                for g4 in range(4):
                    g = half * 4 + g4
                    mm(bk[:, g4 * 128:(g4 + 1) * 128], bt_[:, g, :], ct_[:, g, :], True, True, [Rbt, Rct], [Rbk], g4 == 3)
                evac(cbT[:, half * 4:(half + 1) * 4, :], bk[:].rearrange("p (g n) -> p g n", g=4), [Rbk], [Rcb])
            yield "P"
            (v_, Rv) = load_v(c)
            (k_, Rkt) = load_k(c)
            q_, Rq = qTt.next(); gr_, Rgr = grT.next()
            P.dma("sp", lambda e: e.dma_start(out=q_[:], in_=QT[:, t0:t0 + 128].rearrange("(t p) n -> p t n", p=128)), writes=[Rq])
            P.dma("sp", lambda e: e.dma_start(out=gr_[:], in_=GRET[t0:t0 + 128, :]), writes=[Rgr])
            yield "L"
            if t0 % SEGLEN == 0 and c > 0:
                apply_cont()
            YB = Rot(banks[0:2]); GB = Rot(banks[2:6]); ST = Rot(banks[6:8])

            def ssd_pro(g):
                sl = slice(g * 512, (g + 1) * 512)
                xd, Rxd = xsD.next()
                P.op("pool", lambda e, xd=xd, sl=sl, g=g: e.tensor_tensor(
                    out=xd[:].rearrange("p (h q) -> p h q", q=64), in0=xs_[:, sl].rearrange("p (h q) -> p h q", q=64),
                    in1=bc(Dbc[:, g * 8:g * 8 + 8], 8, 64), op=ALU.mult), reads=[Rxs, Rk], writes=[Rxd])
                sz_, Rsz = szr.next()
                P.op("act", lambda e, sz_=sz_, sl=sl: e.activation(out=sz_[:], in_=z_[:, sl], func=AF.Silu), reads=[Rz], writes=[Rsz])
                sfb, Rsfb = s_shadow(g)
                sbi, Rsbi = sbin.next()
                P.dma("sp", lambda e, sbi=sbi, sl=sl: e.dma_start(out=sbi[:], in_=SBS[c, :, sl]), writes=[Rsbi])
                yb, Ryb = YB.next()
                mm(yb[:], ident_b[:], xd[:], True, False, [Rxd, Rc], [Ryb], False)
                return dict(sl=sl, sz_=sz_, Rsz=Rsz, sfb=sfb, Rsfb=Rsfb, sbi=sbi, Rsbi=Rsbi, yb=yb, Ryb=Ryb, mt={})

            def ssd_A(cx, g, h4):
                if True:
                    E_, RE = Eb.next()
                    for d in range(2):
                        gb_, Rgb = GB.next()
                        NM = NMf if d == 0 else NMb
                        for r in range(4):
                            hidx = d * 64 + g * 8 + h4 * 4 + r
                            sel = bass.AP(ident_b, hidx, [[128, 128], [0, 128]])
                            o_ = gb_[:, r * 128:(r + 1) * 128]
                            mm(o_, sel, hi[:], True, False, [Rhl, Rc], [Rgb], False)
                            mm(o_, sel, lo[:], False, False, [Rhl, Rc], [Rgb], False)
                            mm(o_, ident_b[:], NM[:], False, True, [Rk, Rc], [Rgb], r == 3)
                        for r in range(4):
                            hidx = d * 64 + g * 8 + h4 * 4 + r
                            P.op("act", lambda e, E_=E_, d=d, r=r, gb_=gb_, hidx=hidx: e.activation(
                                out=E_[:, d, r, :], in_=gb_[:, r * 128:(r + 1) * 128], func=AF.Exp, bias=biasE[:, hidx:hidx + 1], scale=1.0),
                                reads=[Rgb, RbE], writes=[RE])
                    es_, Res_ = Es.next()
                    mt_, Rmt = MT.next()
                    P.op("dve", lambda e, es_=es_, E_=E_: e.tensor_tensor(out=es_[:], in0=E_[:, 0], in1=E_[:, 1], op=ALU.add), reads=[RE], writes=[Res_])
                    P.op("dve", lambda e, es_=es_, mt_=mt_, g=g: e.tensor_tensor(out=mt_[:], in0=es_[:], in1=cbT[:, g:g + 1, :].to_broadcast([128, 4, 128]), op=ALU.mult),
                         reads=[Res_, Rcb], writes=[Rmt])
                    cx['mt'][h4] = (mt_, Rmt)

            def ssd_B(cx, g, h4):
                if True:
                    mt_, Rmt = cx['mt'][h4]
                    yb, Ryb = cx['yb'], cx['Ryb']
                    for r in range(4):
                        hl = g * 8 + h4 * 4 + r
                        last = (h4 == 1 and r == 3)
                        mm(yb[:, (h4 * 4 + r) * 64:(h4 * 4 + r + 1) * 64], mt_[:, r, :], xs_[:, hl * 64:(hl + 1) * 64],
                           False, last, [Rmt, Rxs], [Ryb], last)

            def ssd_tail(g, cx):
                sl = cx['sl']; sz_ = cx['sz_']; Rsz = cx['Rsz']; sfb = cx['sfb']; Rsfb = cx['Rsfb']; sbi = cx['sbi']; Rsbi = cx['Rsbi']; yb = cx['yb']; Ryb = cx['Ryb']
                sf, Rsf = ST.next()
                mm(sf[:], ct_[:, g, :], sfb[:], True, True, [Rct, Rsfb], [Rsf], True)
                sbk, Rsbk = ST.next()
                mm(sbk[:], ct_[:, g, :], sbi[:], True, True, [Rct, Rsbi], [Rsbk], True)
                ta, Rta = t512.next()
                P.op("dve", lambda e, ta=ta, sf=sf, g=g: e.tensor_tensor(out=ta[:].rearrange("p (h q) -> p h q", q=64), in0=sf[:].rearrange("p (h q) -> p h q", q=64),
                                                                         in1=bc(ecum[:, g * 8:g * 8 + 8], 8, 64), op=ALU.mult), reads=[Rsf, Rec], writes=[Rta])
                tb, Rtb = t512.next()
                P.op("dve", lambda e, tb=tb, sbk=sbk, g=g: e.tensor_tensor(out=tb[:].rearrange("p (h q) -> p h q", q=64), in0=sbk[:].rearrange("p (h q) -> p h q", q=64),
                                                                           in1=bc(ecum[:, 64 + g * 8:64 + g * 8 + 8], 8, 64), op=ALU.mult), reads=[Rsbk, Rec], writes=[Rtb])
                P.op("pool", lambda e, ta=ta, tb=tb: e.tensor_tensor(out=ta[:], in0=ta[:], in1=tb[:], op=ALU.add), reads=[Rta, Rtb], writes=[Rta])
                ys_, Rys = ysb.next()
                P.op("dve", lambda e, ta=ta, yb=yb, ys_=ys_: e.tensor_tensor(out=ys_[:], in0=yb[:], in1=ta[:], op=ALU.add), reads=[Ryb, Rta], writes=[Rys])
                P.op("pool", lambda e, ys_=ys_, sz_=sz_: e.tensor_tensor(out=ys_[:], in0=ys_[:], in1=sz_[:], op=ALU.mult), reads=[Rys, Rsz], writes=[Rys])
                P.op("act", lambda e, ys_=ys_, g=g: e.activation(out=junk[:], in_=ys_[:], func=AF.Square, accum_out=ss[:, g:g + 1]), reads=[Rys], writes=[Rjk, Rss])
                P.op("act", lambda e, g=g: e.activation(out=ss[:, g:g + 1], in_=ss[:, g:g + 1], func=AF.Ln, bias=epsc[:, 0:1], scale=1.0 / 512), reads=[Rss], writes=[Rss])
                P.op("act", lambda e, g=g: e.activation(out=ss[:, g:g + 1], in_=ss[:, g:g + 1], func=AF.Exp, scale=-0.5), reads=[Rss], writes=[Rss])
                P.op("dve", lambda e, ys_=ys_, sl=sl, g=g: e.tensor_scalar(out=yab[:, sl], in0=ys_[:], scalar1=ss[:, g:g + 1], scalar2=None, op0=ALU.mult), reads=[Rys, Rss], writes=[Ryab])

            cxs = {}
            for n in range(18):
                if n < 16:
                    g, h4 = divmod(n, 2)
                    if h4 == 0:
                        cxs[g] = ssd_pro(g)
                    ssd_A(cxs[g], g, h4)
                if 1 <= n <= 16:
                    g, h4 = divmod(n - 1, 2)
                    ssd_B(cxs[g], g, h4)
                if n >= 3 and (n - 3) % 2 == 0:
                    g = (n - 3) // 2
                    ssd_tail(g, cxs.pop(g))
            for t8 in range(4):
                bk, Rbk = PB.next()
                bkb = bk[:].bitcast(BF16)
                for t in range(8):
                    f = t8 * 8 + t
                    tr(bkb[:, t * 128:(t + 1) * 128], yab[:, f * 128:(f + 1) * 128], ident_b[:], [Ryab, Rc], [Rbk], t == 7)
                st, Rst = stg.next()
                evac(st[:], bkb.rearrange("p (t n) -> p t n", t=8), [Rbk], [Rst])
                P.dma("pool", lambda e, st=st, t8=t8: e.dma_start(
                    out=YAT[t8 * 1024:(t8 + 1) * 1024, t0:t0 + 128].rearrange("(t p) n -> p t n", p=128), in_=st[:]), reads=[Rst])
            yield "S"
            for half in range(2):
                bk, Rbk = PB.next()
                for h4 in range(4):
                    h = half * 4 + h4
                    for dt_ in range(2):
                        mm(bk[:, h4 * 128:(h4 + 1) * 128], k_[:, h * 2 + dt_, :], q_[:, h * 2 + dt_, :], dt_ == 0, dt_ == 1, [Rkt, Rq], [Rbk], (h4 == 3 and dt_ == 1))
                P.op("dve", lambda e, bk=bk, half=half: e.tensor_tensor(out=MTr[:, half * 4:(half + 1) * 4, :], in0=bk[:].rearrange("p (h n) -> p h n", h=4),
                                                                        in1=dec[:, half * 4:(half + 1) * 4, :], op=ALU.mult), reads=[Rbk, Rk], writes=[RMr])
            OBK = Rot(banks[0:2])

            def ret_head(h):
                hs = slice(h * 512, (h + 1) * 512)
                qx, Rqx = qxr.next()
                P.op("pool", lambda e, qx=qx, h=h: e.tensor_tensor(out=qx[:], in0=q_[:, h * 2:h * 2 + 2, :].unsqueeze(1).to_broadcast([128, 2, 2, 128]),
                                                                   in1=xi[:, :, h, :].unsqueeze(2).to_broadcast([128, 2, 2, 128]), op=ALU.mult), reads=[Rq, Rk], writes=[Rqx])
                sz_, Rsz = szr.next()
                P.op("act", lambda e, sz_=sz_, hs=hs: e.activation(out=sz_[:], in_=gr_[:, hs], func=AF.Silu), reads=[Rgr], writes=[Rsz])
                rfb, Rrfb = r_shadow(h)
                rbi, Rrbi = rbin.next()
                P.dma("sp", lambda e, rbi=rbi, h=h: e.dma_start(out=rbi[:].rearrange("p a b -> p (a b)"), in_=SBR[c, :, h * 1024:(h + 1) * 1024]), writes=[Rrbi])
                obk, Robk = OBK.next()
                mm(obk[:], MTr[:, h, :], v_[:, hs], True, False, [RMr, Rv], [Robk], False)
                for dt_ in range(2):
                    mm(obk[:], qx[:, 0, dt_, :], rfb[:, dt_, :], False, False, [Rqx, Rrfb], [Robk], False)
                for dt_ in range(2):
                    mm(obk[:], qx[:, 1, dt_, :], rbi[:, dt_, :], False, dt_ == 1, [Rqx, Rrbi], [Robk], dt_ == 1)
                return dict(hs=hs, sz_=sz_, Rsz=Rsz, obk=obk, Robk=Robk)

            def ret_tail(h, cx):
                hs = cx['hs']; sz_ = cx['sz_']; Rsz = cx['Rsz']; obk = cx['obk']; Robk = cx['Robk']
                P.op("act", lambda e, obk=obk, h=h: e.activation(out=junk[:], in_=obk[:], func=AF.Square, accum_out=ss[:, 8 + h:9 + h]), reads=[Robk], writes=[Rjk, Rss])
                P.op("act", lambda e, h=h: e.activation(out=ss[:, 8 + h:9 + h], in_=ss[:, 8 + h:9 + h], func=AF.Ln, bias=epsc[:, 0:1], scale=1.0 / 512), reads=[Rss], writes=[Rss])
                P.op("act", lambda e, h=h: e.activation(out=ss[:, 8 + h:9 + h], in_=ss[:, 8 + h:9 + h], func=AF.Exp, scale=-0.5), reads=[Rss], writes=[Rss])
                P.op("dve", lambda e, obk=obk, h=h, hs=hs, sz_=sz_: e.scalar_tensor_tensor(out=yab[:, hs], in0=obk[:], scalar=ss[:, 8 + h:9 + h], in1=sz_[:],
                                                                                  op0=ALU.mult, op1=ALU.mult), reads=[Robk, Rss, Rsz], writes=[Ryab])

            cxs = {}
            for h in range(9):
                if h < 8:
                    cxs[h] = ret_head(h)
                if h >= 1:
                    ret_tail(h - 1, cxs.pop(h - 1))
            for t8 in range(4):
                bk, Rbk = PB.next()
                bkb = bk[:].bitcast(BF16)
                for t in range(8):
                    f = t8 * 8 + t
                    tr(bkb[:, t * 128:(t + 1) * 128], yab[:, f * 128:(f + 1) * 128], ident_b[:], [Ryab, Rc], [Rbk], t == 7)
                st, Rst = stg.next()
                evac(st[:], bkb.rearrange("p (t n) -> p t n", t=8), [Rbk], [Rst])
                P.dma("pool", lambda e, st=st, t8=t8: e.dma_start(
                    out=OT[t8 * 1024:(t8 + 1) * 1024, t0:t0 + 128].rearrange("(t p) n -> p t n", p=128), in_=st[:]), reads=[Rst])
            yield "R"
            if c < NCH - 1:
                state_update(c, 0, xs_, Rxs, b_, Rb, v_, Rv, k_, Rkt)
            yield "U"

        gens = [chunk_gen(c) for c in range(NCH)]

        def adv(c, n):
            for _ in range(n):
                next(gens[c])

        adv(0, 4)
        for c in range(NCH):
            if c + 1 < NCH:
                adv(c + 1, 2)
            adv(c, 2)
            if c + 1 < NCH:
                adv(c + 1, 2)
        P.barrier()
        P.pop()

    def phaseC():
        P.push()
        NTC = NT // TC
        ain = Rot([(P.sbuf("ain", [128, 32, TC], BF16), Res()) for _ in range(1)])
        wbig = Rot([(P.sbuf("wbig", [128, 32, 512], BF16), Res()) for _ in range(2)])
        gte = Rot([(P.sbuf("gte", [128, TC], BF16), Res()) for _ in range(3)])
        sg = Rot([(P.sbuf("sgC", [128, TC], F32), Res()) for _ in range(2)])
        m32 = P.sbuf("m32", [128, KD, TC], F32); Rm32 = Res()
        mT = P.sbuf("mT", [128, KD, TC], BF16); RmT = Res()
        hT = P.sbuf("hT", [128, KD, TC], F32); RhT = Res()
        n2, Rn2 = mT, RmT
        xr = Rot([(P.sbuf("xr", [128, TC], F32), Res()) for _ in range(2)])
        sq = Rot([(P.sbuf("sqC", [128, TC], F32), Res()) for _ in range(2)])
        rstd = P.sbuf("rstdC", [128, TC], F32); Rrs = Res()
        tmp = Rot([(P.sbuf("tmpC", [128, TC], F32), Res()) for _ in range(2)])
        stg = Rot([(P.sbuf("stgC", [128, TC], BF16), Res()) for _ in range(4)])
        for tt in range(NTC):
            t0 = tt * TC
            s = t0 // SEGLEN
            for br in range(2):
                src = YAT if br == 0 else OT
                W = WPA if br == 0 else WPB
                a_, Ra = ain.next()
                for hlf in range(2):
                    P.dma("sp", lambda e, a_=a_, src=src, hlf=hlf: e.dma_start(
                        out=a_[:, hlf * 16:(hlf + 1) * 16, :], in_=src[hlf * 2048:(hlf + 1) * 2048, t0:t0 + TC].rearrange("(k p) n -> p k n", p=128)), writes=[Ra])
                for blk in range(4):
                    w, Rw = wbig.next()
                    for hlf in range(2):
                        P.dma("sp", lambda e, w=w, W=W, blk=blk, hlf=hlf: e.dma_start(
                            out=w[:, hlf * 16:(hlf + 1) * 16, :], in_=W[hlf * 2048:(hlf + 1) * 2048, blk * 512:(blk + 1) * 512].rearrange("(k p) n -> p k n", p=128)), writes=[Rw])
                    for j in range(4):
                        dtile = blk * 4 + j
                        g_, Rg = gte.next()
                        P.dma("sp", lambda e, g_=g_, r0=br * 2048 + dtile * 128: e.dma_start(out=g_[:], in_=GATES[r0:r0 + 128, t0:t0 + TC]), writes=[Rg])
                        s_, Rs = sg.next()
                        P.op("act", lambda e, s_=s_, g_=g_: e.activation(out=s_[:], in_=g_[:], func=AF.Sigmoid), reads=[Rg], writes=[Rs])
                        bk, Rbk = PB.next()
                        for k in range(32):
                            mm(bk[:, 0:TC], w[:, k, j * 128:(j + 1) * 128], a_[:, k, :], k == 0, k == 31, [Rw, Ra], [Rbk], k == 31)
                        if br == 0:
                            P.op("dve", lambda e, bk=bk, s_=s_, dtile=dtile: e.tensor_tensor(out=m32[:, dtile, :], in0=bk[:, 0:TC], in1=s_[:], op=ALU.mult), reads=[Rbk, Rs], writes=[Rm32])
                        else:
                            tp, Rtp = tmp.next()
                            P.op("dve", lambda e, bk=bk, s_=s_, tp=tp: e.tensor_tensor(out=tp[:], in0=bk[:, 0:TC], in1=s_[:], op=ALU.mult), reads=[Rbk, Rs], writes=[Rtp])
                            P.op("pool", lambda e, tp=tp, dtile=dtile: e.tensor_tensor(out=mT[:, dtile, :], in0=tp[:], in1=m32[:, dtile, :], op=ALU.add), reads=[Rtp, Rm32], writes=[RmT])
            bss, Rbss = ACCB
            pend_c = None
            for blk in range(4):
                w, Rw = wbig.next()
                P.dma("sp", lambda e, w=w, blk=blk: e.dma_start(out=w[:, 0:16, :], in_=WOUT[:, blk * 512:(blk + 1) * 512].rearrange("(k p) n -> p k n", p=128)), writes=[Rw])
                for j in range(4):
                    dtile = blk * 4 + j
                    x_, Rx = xr.next()
                    P.dma("sp", lambda e, x_=x_, dtile=dtile: e.dma_start(out=x_[:], in_=XT[dtile * 128:(dtile + 1) * 128, t0:t0 + TC]), writes=[Rx])
                    bk, Rbk = PB.next()
                    for k in range(KD):
                        mm(bk[:, 0:TC], w[:, k, j * 128:(j + 1) * 128], mT[:, k, :], k == 0, k == KD - 1, [Rw, RmT], [Rbk], k == KD - 1)
                    P.op("dve", lambda e, bk=bk, x_=x_, dtile=dtile: e.scalar_tensor_tensor(
                        out=hT[:, dtile, :], in0=bk[:, 0:TC], scalar=modT[:, 32 + dtile, s:s + 1], in1=x_[:], op0=ALU.mult, op1=ALU.add),
                        reads=[Rbk, Rx, Rmod], writes=[RhT])
                    P.dma("pool", lambda e, dtile=dtile: e.dma_start(out=H[dtile * 128:(dtile + 1) * 128, t0:t0 + TC], in_=hT[:, dtile, :]), reads=[RhT])
                    q_, Rq = sq.next()
                    P.op("act", lambda e, q_=q_, dtile=dtile: e.activation(out=q_[:], in_=hT[:, dtile, :], func=AF.Square), reads=[RhT], writes=[Rq])
                    if pend_c is not None:
                        pq, pRq, pd = pend_c
                        mm(bss[:, 0:TC], ones_f[:], pq[:], pd == 0, pd == KD - 1, [pRq, Rc], [Rbss], True)
                    pend_c = (q_, Rq, dtile)
            pq, pRq, pd = pend_c
            mm(bss[:, 0:TC], ones_f[:], pq[:], pd == 0, pd == KD - 1, [pRq, Rc], [Rbss], True)
            P.op("act", lambda e, bss=bss: e.activation(out=rstd[:], in_=bss[:, 0:TC], func=AF.Ln, bias=epsc[:, 0:1], scale=1.0 / D), reads=[Rbss], writes=[Rrs])
            P.op("act", lambda e, bss=bss: e.activation(out=rstd[:], in_=rstd[:], func=AF.Exp, scale=-0.5), reads=[Rrs], writes=[Rrs])
            for k in range(KD):
                tp, Rtp = tmp.next()
                P.op("dve", lambda e, tp=tp, k=k: e.scalar_tensor_tensor(out=tp[:], in0=hT[:, k, :], scalar=scale2[:, k, s:s + 1], in1=rstd[:], op0=ALU.mult, op1=ALU.mult),
                     reads=[RhT, Rrs, Rmod], writes=[Rtp])
                P.op("act", lambda e, tp=tp, k=k: e.activation(out=n2[:, k, :], in_=tp[:], func=AF.Identity, bias=modT[:, 48 + k, s:s + 1], scale=1.0),
                     reads=[Rtp, Rmod], writes=[Rn2])
            for blk in range(22):
                w, Rw = wbig.next()
                P.dma("sp", lambda e, w=w, blk=blk: e.dma_start(out=w[:, 0:16, :], in_=WUP[:, blk * 512:(blk + 1) * 512].rearrange("(k p) n -> p k n", p=128)), writes=[Rw])
                for j in range(4):
                    ft = blk * 4 + j
                    bk, Rbk = PB.next()
                    for k in range(KD):
                        mm(bk[:, 0:TC], w[:, k, j * 128:(j + 1) * 128], n2[:, k, :], k == 0, k == KD - 1, [Rw, Rn2], [Rbk], k == KD - 1)
                    st, Rst = stg.next()
                    evac(st[:], bk[:, 0:TC], [Rbk], [Rst])
                    dst = AT if ft < NF else GT
                    r0 = (ft % NF) * 128
                    P.dma("pool", lambda e, st=st, dst=dst, r0=r0: e.dma_start(out=dst[r0:r0 + 128, t0:t0 + TC], in_=st[:]), reads=[Rst])
        P.barrier()
        P.pop()

    def phaseD():
        P.push()
        NTC = NT // TC
        fw = P.sbuf("fw", [128, NF, 3], F32)
        fb = P.sbuf("fb", [128, NF], F32)
        Rfw = Res()
        P.dma("sp", lambda e: e.dma_start(out=fw[:], in_=fconv_wT), writes=[Rfw])
        P.dma("sp", lambda e: e.dma_start(out=fb[:], in_=fconv_bT), writes=[Rfw])
        ab = Rot([(P.sbuf("abD", [128, TC + 4], BF16), Res()) for _ in range(4)])
        gb = Rot([(P.sbuf("gbD", [128, TC], BF16), Res()) for _ in range(5)])
        acc = Rot([(P.sbuf("accD", [128, TC], F32), Res()) for _ in range(3)])
        sl_ = Rot([(P.sbuf("slD", [128, TC], BF16), Res()) for _ in range(4)])
        uTr = Rot([(P.sbuf("uT", [128, NF, TC], BF16), Res()) for _ in range(2)])
        wd = Rot([(P.sbuf("wd", [128, NF, 256], BF16), Res()) for _ in range(2)])
        hr = Rot([(P.sbuf("hr", [128, TC], F32), Res()) for _ in range(2)])
        h2 = P.sbuf("h2", [128, KD, TC], F32); Rh2 = Res()
        sq = Rot([(P.sbuf("sqD", [128, TC], F32), Res()) for _ in range(2)])
        rstd = P.sbuf("rstdD", [128, TC], F32); Rrs = Res()
        yo = Rot([(P.sbuf("yo", [128, D], F32), Res()) for _ in range(1)])
        def conv(tt, uT, RuT):
            t0 = tt * TC
            lo = max(t0 - 1, 0)
            hi = min(t0 + TC + 1, NT)
            prev = None
            for f in range(NF + 1):
                if f < NF:
                    cur = conv_s1(t0, lo, hi, f)
                if prev is not None:
                    conv_s2(prev, uT, RuT)
                prev = cur if f < NF else None
                yield

        def conv_s1(t0, lo, hi, f):
            if True:
                a_, Ra = ab.next()
                g_, Rg = gb.next()
                if lo > t0 - 1:
                    P.op("pool", lambda e, a_=a_: e.memset(a_[:, 0:2], 0.0), writes=[Ra])
                if hi < t0 + TC + 1:
                    P.op("pool", lambda e, a_=a_: e.memset(a_[:, TC + 2:TC + 4], 0.0), writes=[Ra])
                P.dma("sp", lambda e, a_=a_, f=f: e.dma_start(out=a_[:, lo - t0 + 2:hi - t0 + 2], in_=AT[f * 128:(f + 1) * 128, lo:hi]), writes=[Ra])
                P.dma("sp", lambda e, g_=g_, f=f: e.dma_start(out=g_[:], in_=GT[f * 128:(f + 1) * 128, t0:t0 + TC]), writes=[Rg])
                if t0 % SEGLEN == 0 and t0 > 0:
                    P.op("act", lambda e, a_=a_: e.activation(out=a_[:, 0:2], in_=a_[:, 0:2], func=AF.Copy, scale=cont[:, 0:1]), reads=[Ra, Rc], writes=[Ra])
                if (t0 + TC) % SEGLEN == 0 and t0 + TC < NT:
                    P.op("act", lambda e, a_=a_: e.activation(out=a_[:, TC + 2:TC + 4], in_=a_[:, TC + 2:TC + 4], func=AF.Copy, scale=cont[:, 0:1]), reads=[Ra, Rc], writes=[Ra])
                c_, Rcc = acc.next()
                P.op("act", lambda e, c_=c_, a_=a_, f=f: e.activation(out=c_[:], in_=a_[:, 1:TC + 1], func=AF.Identity, bias=fb[:, f:f + 1], scale=fw[:, f, 0:1]), reads=[Ra, Rfw], writes=[Rcc])
                for k in range(1, 3):
                    P.op("dve", lambda e, c_=c_, a_=a_, f=f, k=k: e.scalar_tensor_tensor(out=c_[:], in0=a_[:, k + 1:k + 1 + TC], scalar=fw[:, f, k:k + 1], in1=c_[:], op0=ALU.mult, op1=ALU.add),
                         reads=[Ra, Rfw, Rcc], writes=[Rcc])
            return (f, c_, Rcc, g_, Rg)

        def conv_s2(st, uT, RuT):
            f, c_, Rcc, g_, Rg = st
            s_, Rs = sl_.next()
            P.op("act", lambda e, s_=s_, c_=c_: e.activation(out=s_[:], in_=c_[:], func=AF.Silu), reads=[Rcc], writes=[Rs])
            P.op("pool", lambda e, s_=s_, g_=g_, f=f: e.tensor_tensor(out=uT[:, f, :], in0=s_[:], in1=g_[:], op=ALU.mult), reads=[Rs, Rg], writes=[RuT])

        def rest(tt, uT, RuT, gen):
            t0 = tt * TC
            s = t0 // SEGLEN
            bss, Rbss = ACCB
            def load_w(blk):
                w, Rw = wd.next()
                for (f0, f1) in ((0, 22), (22, 44)):
                    P.dma("sp", lambda e, w=w, blk=blk, f0=f0, f1=f1: e.dma_start(
                        out=w[:, f0:f1, :], in_=WDN[f0 * 128:f1 * 128, blk * 256:(blk + 1) * 256].rearrange("(k p) n -> p k n", p=128)), writes=[Rw])
                return w, Rw

            wnext = load_w(0)
            pend_ss = None
            for blk in range(8):
                w, Rw = wnext
                if blk + 1 < 8:
                    wnext = load_w(blk + 1)
                if gen is not None:
                    for _ in range(6):
                        next(gen, None)
                for j in range(2):
                    dtile = blk * 2 + j
                    h_, Rh = hr.next()
                    P.dma("sp", lambda e, h_=h_, dtile=dtile: e.dma_start(out=h_[:], in_=H[dtile * 128:(dtile + 1) * 128, t0:t0 + TC]), writes=[Rh])
                    bk, Rbk = PB.next()
                    for k in range(NF):
                        mm(bk[:, 0:TC], w[:, k, j * 128:(j + 1) * 128], uT[:, k, :], k == 0, k == NF - 1, [Rw, RuT], [Rbk], k == NF - 1)
                    P.op("dve", lambda e, bk=bk, h_=h_, dtile=dtile: e.scalar_tensor_tensor(
                        out=h2[:, dtile, :], in0=bk[:, 0:TC], scalar=modT[:, 80 + dtile, s:s + 1], in1=h_[:], op0=ALU.mult, op1=ALU.add),
                        reads=[Rbk, Rh, Rmod], writes=[Rh2])
                    q_, Rq = sq.next()
                    P.op("act", lambda e, q_=q_, dtile=dtile: e.activation(out=q_[:], in_=h2[:, dtile, :], func=AF.Square), reads=[Rh2], writes=[Rq])
                    if pend_ss is not None:
                        pq, pRq, pd = pend_ss
                        mm(bss[:, 0:TC], ones_f[:], pq[:], pd == 0, pd == KD - 1, [pRq, Rc], [Rbss], True)
                    pend_ss = (q_, Rq, dtile)
            pq, pRq, pd = pend_ss
            mm(bss[:, 0:TC], ones_f[:], pq[:], pd == 0, pd == KD - 1, [pRq, Rc], [Rbss], True)
            P.op("act", lambda e, bss=bss: e.activation(out=rstd[:], in_=bss[:, 0:TC], func=AF.Ln, bias=epsc[:, 0:1], scale=1.0 / D), reads=[Rbss], writes=[Rrs])
            P.op("act", lambda e, bss=bss: e.activation(out=rstd[:], in_=rstd[:], func=AF.Exp, scale=-0.5), reads=[Rrs], writes=[Rrs])
            for k in range(KD):
                P.op("dve", lambda e, k=k: e.scalar_tensor_tensor(out=h2[:, k, :], in0=h2[:, k, :], scalar=nfT[:, k:k + 1], in1=rstd[:], op0=ALU.mult, op1=ALU.mult),
                     reads=[Rh2, Rrs, Rmod], writes=[Rh2])
            for sb in range(TC // 128):
                y_, Ry = yo.next()
                for k4 in range(4):
                    bk, Rbk = PB.next()
                    for kk in range(4):
                        k = k4 * 4 + kk
                        tr(bk[:, kk * 128:(kk + 1) * 128], h2[:, k, sb * 128:(sb + 1) * 128], ident_f[:], [Rh2, Rc], [Rbk], kk == 3)
                    evac(y_[:, k4 * 512:(k4 + 1) * 512], bk[:], [Rbk], [Ry])
                r0 = t0 + sb * 128
                P.dma("pool", lambda e, y_=y_, r0=r0: e.dma_start(out=y_out[r0:r0 + 128, :], in_=y_[:]), reads=[Ry])

        cur_u = uTr.next()
        for _ in conv(0, *cur_u):
            pass
        for tt in range(NTC):
            nxt_u = uTr.next() if tt + 1 < NTC else None
            gen = conv(tt + 1, *nxt_u) if nxt_u is not None else None
            rest(tt, cur_u[0], cur_u[1], gen)
            cur_u = nxt_u
        P.barrier()
        P.pop()

    for i, ph in enumerate((phase0, phaseA, phaseA2, scans, phaseC, phaseD)):
        if i <= upto:
            ph()
    P.finish()
    return nc


NSEG_FULL = 4
SEGLEN_FULL = 2048


def rope_tables(pos):
    half = 128
    inv = 10000.0 ** (-np.arange(half, dtype=np.float32) / half)
    ang = pos.astype(np.float32)[None, :] * inv[:, None].astype(np.float32)
    return np.cos(ang).astype(np.float32), np.sin(ang).astype(np.float32)


def tileT(v, ntile):
    return np.ascontiguousarray(np.asarray(v, np.float32).reshape(ntile, 128).T)


def shared_inputs(w):
    dec_c, xi_c, zeta_c, _, _ = ret_consts()
    sq = lambda a: np.ascontiguousarray(np.asarray(a, np.float32)[0])
    conv_w = sq(w["conv_w"])
    fcw = sq(w["ffn_conv_w"])
    return {
        "dec": dec_c, "xi": xi_c, "zeta": zeta_c,
        "w_ada": sq(w["w_ada"]), "b_adaT": tileT(sq(w["b_ada"]), 96), "norm1T": tileT(sq(w["norm1"]), 16),
        "w_in": sq(w["w_in"]),
        "conv_wT": np.ascontiguousarray(conv_w.reshape(5, 48, 128).transpose(2, 1, 0)),
        "conv_bT": tileT(sq(w["conv_b"]), 48),
        "dt_bias": sq(w["dt_bias"]).reshape(1, 128), "a_log": sq(w["a_log"]).reshape(1, 128),
        "ssd_d": sq(w["ssd_d"]).reshape(1, 64), "ssd_normT": tileT(sq(w["ssd_norm"]), 32),
        "w_pa": sq(w["w_pa"]), "w_pb": sq(w["w_pb"]), "w_out": sq(w["w_out"]), "norm2T": tileT(sq(w["norm2"]), 16),
        "w_up": sq(w["w_up"]),
        "fconv_wT": np.ascontiguousarray(fcw.reshape(3, NF, 128).transpose(2, 1, 0)),
        "fconv_bT": tileT(sq(w["ffn_conv_b"]), NF),
        "w_down": sq(w["w_down"]), "norm_fT": tileT(np.asarray(w["norm_f"], np.float32), 16),
    }


def core_inputs(xseg, cseg, cont, seglen, shared):
    nseg = len(xseg)
    x = np.ascontiguousarray(np.concatenate(xseg, 0), dtype=np.float32)
    c = np.stack(cseg, 0).astype(np.float32)
    cT = np.ascontiguousarray(c.reshape(nseg, KD, 128).transpose(2, 1, 0))
    if cont:
        pos = np.arange(nseg * seglen)
    else:
        pos = np.tile(np.arange(seglen), nseg)
    cosT, sinT = rope_tables(pos)
    m = dict(shared)
    m.update({"x": x, "cT": cT, "cont": np.full((128, 1), float(cont), np.float32), "cosT": cosT, "sinT": sinT})
    return m


_NC_CACHE = {}


def kernel(x_prompt, x_sample, c_prompt, c_sample, **w):
    x_prompt = np.asarray(x_prompt, np.float32)
    x_sample = np.asarray(x_sample, np.float32)
    c_prompt = np.asarray(c_prompt, np.float32)
    c_sample = np.asarray(c_sample, np.float32)
    shared = shared_inputs(w)
    NS, SL = NSEG_FULL, SEGLEN_FULL
    in_maps = []
    for b in range(4):
        xs = [x_sample[b, i * SL:(i + 1) * SL] for i in range(NS)]
        in_maps.append(core_inputs(xs, [c_sample[b]] * NS, 1.0, SL, shared))
    zx = np.zeros((SL, D), np.float32)
    zc = np.zeros((D,), np.float32)
    for i in range(4):
        xs = [x_prompt[2 * i], x_prompt[2 * i + 1], zx, zx]
        cs = [c_prompt[2 * i], c_prompt[2 * i + 1], zc, zc]
        in_maps.append(core_inputs(xs, cs, 0.0, SL, shared))
    key = (NS, SL)
    if key not in _NC_CACHE:
        _NC_CACHE[key] = build(NS, SL)
    nc = _NC_CACHE[key]
    res = run_bass_kernel_spmd(nc, in_maps, core_ids=list(range(8)))
    y_sample = np.stack([np.asarray(res.results[b]["y"], np.float32).reshape(NS * SL, D) for b in range(4)], 0)
    yp = []
    for i in range(4):
        y = np.asarray(res.results[4 + i]["y"], np.float32).reshape(NS * SL, D)
        yp.append(y[0:SL])
        yp.append(y[SL:2 * SL])
    y_prompt = np.stack(yp, 0)
    return (y_prompt, y_sample)
```
